# Optimizing a Trainium2 kernel written in Bass

```python
import jax, jax.numpy as jnp
from jax import lax
import numpy as np

D_MODEL = 2048
BATCH = 4
SEQ = 4096
DEPTH = 1

PLE_DIM = 256
R_HEADS = 16
R_HEAD = 64
R_WIDTH = R_HEADS * R_HEAD
DECAY_LORA = 64
AAA_LORA = 64
GATE_LORA = 160
R_GN_EPS = 64e-5
RWKV_COLS = 3 * R_WIDTH + DECAY_LORA + AAA_LORA + GATE_LORA
M_HEADS = 8
M_QK = 64
M_V = 128
M_WIDTH = M_HEADS * M_V
CONV_K = 4
CHUNK = 128
M_NORM_EPS = 1e-6
MLSTM_COLS = 2 * M_HEADS * M_QK + 2 * M_WIDTH + 2 * M_HEADS
GATE_COLS = 2 * D_MODEL
IN_COLS = RWKV_COLS + MLSTM_COLS + GATE_COLS
N_GROUPS = 4
EXPERTS_PER_GROUP = 8
N_EXPERTS = N_GROUPS * EXPERTS_PER_GROUP
TOP_K = 2
D_EXPERT = 512
MOE_BLOCK = 128
LN_EPS = 1e-5
ALPHA = (2 * DEPTH) ** 0.25
BETA = (8 * DEPTH) ** -0.25

kernel_name = 'hybrid_rwkv7_mlstm_hmoe_block'

F32 = jnp.float32


def _split(t, sizes):
    return jnp.split(t, np.cumsum(sizes)[:-1].tolist(), axis=-1)


def _layer_norm(x, w, b):
    xf = x.astype(F32)
    mu = xf.mean(-1, keepdims=True)
    var = jnp.square(xf - mu).mean(-1, keepdims=True)
    return ((xf - mu) * lax.rsqrt(var + LN_EPS) * w + b).astype(x.dtype)


def _head_norm(y, eps):
    mu = y.mean(-1, keepdims=True)
    var = jnp.square(y - mu).mean(-1, keepdims=True)
    return (y - mu) * lax.rsqrt(var + eps)


def _causal_conv(t, w, b):
    seq = t.shape[1]
    tp = jnp.pad(t, ((0, 0), (CONV_K - 1, 0), (0, 0)))
    return b + sum(w[j] * tp[:, j:j + seq] for j in range(CONV_K))


def _rwkv7_scan(r, decay, k, v, kk, a):
    bsz, _, h, n = r.shape

    def step(state, inp):
        r_t, w_t, k_t, v_t, kk_t, a_t = inp
        s_kk = jnp.einsum('bhvk,bhk->bhv', state, kk_t)
        state = (state * w_t[:, :, None, :]
                 - s_kk[..., None] * (kk_t * a_t)[:, :, None, :]
                 + v_t[..., None] * k_t[:, :, None, :])
        return state, jnp.einsum('bhvk,bhk->bhv', state, r_t)

    xs = tuple(jnp.moveaxis(t, 1, 0) for t in (r, decay, k, v, kk, a))
    _, ys = lax.scan(step, jnp.zeros((bsz, h, n, n), F32), xs)
    return jnp.moveaxis(ys, 0, 1)


def _rwkv7_mixer(z, mu, w0, w_w2, a0, w_a2, w_g2, k_k, k_a, r_k, lnx_w, lnx_b):
    bsz, seq, _ = z.shape
    z_prev = jnp.pad(z, ((0, 0), (1, 0), (0, 0)))[:, :-1]
    z = z + mu * (z_prev - z)
    r, k, v, zw, za, zg = _split(z, [R_WIDTH, R_WIDTH, R_WIDTH, DECAY_LORA, AAA_LORA, GATE_LORA])
    w = -jax.nn.softplus(-(w0 + jnp.tanh(zw) @ w_w2).astype(F32)) - 0.5
    decay = jnp.exp(-jnp.exp(w))
    a = jax.nn.sigmoid((a0 + za @ w_a2).astype(F32))
    g = jax.nn.sigmoid(zg) @ w_g2
    heads = lambda t: t.astype(F32).reshape(bsz, seq, R_HEADS, R_HEAD)
    r, k, v, decay, a = heads(r), heads(k), heads(v), heads(decay), heads(a)
    kk = k * k_k.astype(F32).reshape(R_HEADS, R_HEAD)
    kk = kk / jnp.maximum(jnp.sqrt(jnp.sum(kk * kk, -1, keepdims=True)), 1e-12)
    k = k * (1.0 + (a - 1.0) * k_a.astype(F32).reshape(R_HEADS, R_HEAD))
    y = _rwkv7_scan(r, decay, k, v, kk, a)
    y = _head_norm(y, R_GN_EPS).reshape(bsz, seq, R_WIDTH) * lnx_w + lnx_b
    bonus = (jnp.sum(r * k * r_k.astype(F32), -1, keepdims=True) * v).reshape(bsz, seq, R_WIDTH)
    return ((y + bonus) * g).astype(z.dtype)


def _mlstm_chunkwise(q, k, v, i_pre, log_f):
    bsz, h, seq, dqk = q.shape
    dv = v.shape[-1]
    nc = seq // CHUNK

    def chunks(t):
        return jnp.moveaxis(t.reshape(t.shape[:2] + (nc, CHUNK) + t.shape[3:]), 2, 0)

    causal = jnp.tril(jnp.ones((CHUNK, CHUNK), bool))

    def step(carry, inp):
        c_st, n_st, m_st = carry
        q_c, k_c, v_c, i_c, f_c = inp
        a = jnp.cumsum(f_c, -1)
        a_tot = a[..., -1]
        d = jnp.where(causal, a[..., :, None] - a[..., None, :] + i_c[..., None, :], -jnp.inf)
        inter = a + m_st[..., None]
        m_t = jnp.maximum(inter, d.max(-1))
        s = jnp.einsum('bhtd,bhsd->bhts', q_c, k_c) * jnp.exp(d - m_t[..., None])
        ie = jnp.exp(inter - m_t)
        num = (ie[..., None] * jnp.einsum('bhtd,bhde->bhte', q_c, c_st)
               + jnp.einsum('bhts,bhse->bhte', s, v_c))
        den = ie * jnp.einsum('bhtd,bhd->bht', q_c, n_st) + s.sum(-1)
        h_c = num / jnp.maximum(jnp.abs(den), jnp.exp(-m_t))[..., None]
        gl = a_tot[..., None] - a + i_c
        m_new = jnp.maximum(a_tot + m_st, gl.max(-1))
        sc = jnp.exp(a_tot + m_st - m_new)
        ge = jnp.exp(gl - m_new[..., None])
        c_st = sc[..., None, None] * c_st + jnp.einsum('bhs,bhsd,bhse->bhde', ge, k_c, v_c)
        n_st = sc[..., None] * n_st + jnp.einsum('bhs,bhsd->bhd', ge, k_c)
        return (c_st, n_st, m_new), h_c

    init = (jnp.zeros((bsz, h, dqk, dv), F32), jnp.zeros((bsz, h, dqk), F32),
            jnp.full((bsz, h), -jnp.inf, F32))
    _, hs = lax.scan(step, init, tuple(chunks(t) for t in (q, k, v, i_pre, log_f)))
    return jnp.moveaxis(hs, 0, 2).reshape(bsz, h, seq, dv)


def _mlstm_mixer(z, conv_w, conv_b, i_bias, f_bias, mh_w):
    bsz, seq, _ = z.shape
    qk, v, ig, fg, o = _split(z, [2 * M_HEADS * M_QK, M_WIDTH, M_HEADS, M_HEADS, M_WIDTH])
    qk = jax.nn.silu(_causal_conv(qk, conv_w, conv_b))
    q, k = _split(qk, [M_HEADS * M_QK, M_HEADS * M_QK])
    heads = lambda t, d: t.astype(F32).reshape(bsz, seq, M_HEADS, d).transpose(0, 2, 1, 3)
    q, k, v = heads(q, M_QK), heads(k, M_QK) * M_QK ** -0.5, heads(v, M_V)
    i_pre = (ig + i_bias).astype(F32).transpose(0, 2, 1)
    log_f = jax.nn.log_sigmoid((fg + f_bias).astype(F32)).transpose(0, 2, 1)
    h = _mlstm_chunkwise(q, k, v, i_pre, log_f)
    h = _head_norm(h, M_NORM_EPS).transpose(0, 2, 1, 3).reshape(bsz, seq, M_WIDTH) * mh_w
    return (jax.nn.sigmoid(o) * h).astype(z.dtype)


def _hier_moe(xf, w_rg, b_rg, w_re, b_re, w_gate, w_up, w_down):
    n, d = xf.shape
    lg = (xf @ w_rg).astype(F32) + b_rg
    g_sel = jnp.argmax(lg, -1)
    g_w = jnp.take_along_axis(jax.nn.softmax(lg, -1), g_sel[:, None], -1)
    le = ((xf @ w_re).astype(F32) + b_re).reshape(n, N_GROUPS, EXPERTS_PER_GROUP)
    le = jnp.take_along_axis(le, g_sel[:, None, None], 1)[:, 0]
    top_l, top_i = lax.top_k(le, TOP_K)
    wts = jax.nn.softmax(top_l, -1) * g_w
    eid = (g_sel[:, None] * EXPERTS_PER_GROUP + top_i).reshape(-1)
    tok = jnp.repeat(jnp.arange(n, dtype=jnp.int32), TOP_K)
    wt = wts.reshape(-1)
    order = jnp.argsort(eid)
    eid_s, tok_s, wt_s = eid[order], tok[order], wt[order]
    counts = jnp.bincount(eid, length=N_EXPERTS)
    start = jnp.cumsum(counts) - counts
    padded = ((counts + MOE_BLOCK - 1) // MOE_BLOCK) * MOE_BLOCK
    pend = jnp.cumsum(padded)
    pstart = pend - padded
    n_rows = TOP_K * n
    dest = pstart[eid_s] + (jnp.arange(n_rows) - start[eid_s])
    nb = -(-n_rows // MOE_BLOCK) + N_EXPERTS
    rows = nb * MOE_BLOCK
    tok_rows = jnp.zeros((rows,), jnp.int32).at[dest].set(tok_s)
    wt_rows = jnp.zeros((rows,), F32).at[dest].set(wt_s)
    blk_e = jnp.minimum(jnp.searchsorted(pend, jnp.arange(nb) * MOE_BLOCK, side='right'),
                        N_EXPERTS - 1)

    def one_block(args):
        t_idx, e = args
        xb = xf[t_idx]
        hb = jax.nn.silu(xb @ w_gate[e]) * (xb @ w_up[e])
        return hb @ w_down[e]

    out = lax.map(one_block, (tok_rows.reshape(nb, MOE_BLOCK), blk_e)).reshape(rows, d)
    out = out * wt_rows[:, None].astype(out.dtype)
    return jnp.zeros_like(xf).at[tok_rows].add(out)


def setup_inputs(seed: int = 0) -> dict:
    key = jax.random.key(seed)
    ks = iter(jax.random.split(key, 40))
    nrm = lambda shape, s: s * jax.random.normal(next(ks), shape, F32)
    uni = lambda shape, lo, hi: jax.random.uniform(next(ks), shape, F32, lo, hi)
    L, D = DEPTH, D_MODEL
    col_scale = jnp.concatenate([
        jnp.ones((2 * R_WIDTH,), F32), jnp.full((R_WIDTH,), BETA, F32),
        jnp.ones((RWKV_COLS - 3 * R_WIDTH + 2 * M_HEADS * M_QK,), F32),
        jnp.full((M_WIDTH,), BETA, F32),
        jnp.ones((2 * M_HEADS + M_WIDTH + GATE_COLS,), F32)])
    return {
        'x': nrm((BATCH, SEQ, D), 1.0),
        'p': nrm((L, BATCH, SEQ, PLE_DIM), 1.0),
        'w_in': nrm((L, D, IN_COLS), D ** -0.5) * col_scale,
        'mu_shift': uni((L, RWKV_COLS), 0.0, 1.0),
        'w0': uni((L, R_WIDTH), -6.0, -1.0),
        'w_w2': nrm((L, DECAY_LORA, R_WIDTH), 0.5 * DECAY_LORA ** -0.5),
        'a0': nrm((L, R_WIDTH), 0.1),
        'w_a2': nrm((L, AAA_LORA, R_WIDTH), 0.5 * AAA_LORA ** -0.5),
        'w_g2': nrm((L, GATE_LORA, R_WIDTH), GATE_LORA ** -0.5),
        'k_k': 0.85 + nrm((L, R_WIDTH), 0.05),
        'k_a': 1.0 + nrm((L, R_WIDTH), 0.05),
        'r_k': nrm((L, R_HEADS, R_HEAD), 0.1),
        'lnx_w': 1.0 + nrm((L, R_WIDTH), 0.05),
        'lnx_b': nrm((L, R_WIDTH), 0.01),
        'conv_w': nrm((L, CONV_K, 2 * M_HEADS * M_QK), CONV_K ** -0.5),
        'conv_b': nrm((L, 2 * M_HEADS * M_QK), 0.01),
        'i_bias': nrm((L, M_HEADS), 0.1),
        'f_bias': uni((L, M_HEADS), 3.0, 6.0),
        'mh_w': 1.0 + nrm((L, M_WIDTH), 0.05),
        'b_gate': nrm((L, GATE_COLS), 0.01),
        'w_br': nrm((L, R_WIDTH, D), BETA * R_WIDTH ** -0.5),
        'w_bm': nrm((L, M_WIDTH, D), BETA * M_WIDTH ** -0.5),
        'w_out': nrm((L, D, D), BETA * D ** -0.5),
        'ln1_w': 1.0 + nrm((L, D), 0.05),
        'ln1_b': nrm((L, D), 0.01),
        'w_rg': nrm((L, D, N_GROUPS), D ** -0.5),
        'b_rg': nrm((L, N_GROUPS), 0.01),
        'w_re': nrm((L, D, N_EXPERTS), D ** -0.5),
        'b_re': nrm((L, N_EXPERTS), 0.01),
        'w_gate': nrm((L, N_EXPERTS, D, D_EXPERT), BETA * D ** -0.5),
        'w_up': nrm((L, N_EXPERTS, D, D_EXPERT), BETA * D ** -0.5),
        'w_down': nrm((L, N_EXPERTS, D_EXPERT, D), BETA * D_EXPERT ** -0.5),
        'w_pg': nrm((L, D, D), D ** -0.5),
        'w_ple': nrm((L, PLE_DIM, D), BETA * PLE_DIM ** -0.5),
        'ln2_w': 1.0 + nrm((L, D), 0.05),
        'ln2_b': nrm((L, D), 0.01),
    }


def reference(x, p, w_in, mu_shift, w0, w_w2, a0, w_a2, w_g2, k_k, k_a, r_k, lnx_w, lnx_b,
              conv_w, conv_b, i_bias, f_bias, mh_w, b_gate, w_br, w_bm, w_out, ln1_w, ln1_b,
              w_rg, b_rg, w_re, b_re, w_gate, w_up, w_down, w_pg, w_ple, ln2_w, ln2_b):
    bsz, seq, d = x.shape
    for i in range(DEPTH):
        u = x @ w_in[i]
        u_r, u_m, u_g = _split(u, [RWKV_COLS, MLSTM_COLS, GATE_COLS])
        y_r = _rwkv7_mixer(u_r, mu_shift[i], w0[i], w_w2[i], a0[i], w_a2[i], w_g2[i],
                           k_k[i], k_a[i], r_k[i], lnx_w[i], lnx_b[i])
        y_m = _mlstm_mixer(u_m, conv_w[i], conv_b[i], i_bias[i], f_bias[i], mh_w[i])
        g_r, g_m = _split(u_g + b_gate[i], [d, d])
        mix = (jax.nn.sigmoid(g_r) * (y_r @ w_br[i])
               + jax.nn.sigmoid(g_m) * (y_m @ w_bm[i])) @ w_out[i]
        x = _layer_norm(ALPHA * x + mix, ln1_w[i], ln1_b[i])
        moe = _hier_moe(x.reshape(bsz * seq, d), w_rg[i], b_rg[i], w_re[i], b_re[i],
                        w_gate[i], w_up[i], w_down[i]).reshape(bsz, seq, d)
        ple = jax.nn.sigmoid(x @ w_pg[i]) * (p[i] @ w_ple[i])
        x = _layer_norm(ALPHA * x + moe + ple, ln2_w[i], ln2_b[i])
    return x
```

```python
import numpy as np
import concourse.bass as bass
import concourse.mybir as mybir
from concourse.bass_utils import run_bass_kernel_spmd
from contextlib import ExitStack

F32 = mybir.dt.float32
BF16 = mybir.dt.bfloat16
F32R = mybir.dt.float32r
ALU = mybir.AluOpType
AF = mybir.ActivationFunctionType
AX = mybir.AxisListType

D = 2048
KC = 16
ALPHA = 2.0 ** 0.25
LN_EPS = 1e-5
R_GN_EPS = 64e-5
M_NORM_EPS = 1e-6
TM = 512
USE_R = False


class Res:
    __slots__ = ("lw", "rd", "excl")

    def __init__(self, excl=False):
        self.lw = None
        self.rd = {}
        self.excl = excl


class Tile:
    def __init__(self, t):
        self.t = t
        self.r = Res()

    def __getitem__(self, k):
        return self.t[k]


def _base_part(ap):
    bp = ap.base_partition
    return bp() if callable(bp) else bp


def _res(x):
    return x.r if isinstance(x, Tile) else x


class Prog:
    ENG = ("pe", "act", "dve", "pool", "sp")
    SAME_WIN = 3

    def __init__(self, nc, tag):
        self.nc = nc
        self.tag = tag
        self.ops = {e: [] for e in self.ENG}
        self.cnt = {}
        self.clock = {e: {} for e in self.ENG}
        self.snap = {}
        self.sems = {}
        self._pe_free = False
        for e in self.ENG:
            self._mksem(e)

    def _mksem(self, key):
        self.sems[key] = self.nc.alloc_semaphore("s%s_%s" % (self.tag, str(key).replace(" ", "")))
        self.cnt[key] = 0

    def _need(self, eng, ev, waits):
        if ev is None:
            return
        key, val = ev
        if key == eng:
            if eng == "pe" and self._pe_free:
                return
            if self.cnt[eng] + 1 - val <= self.SAME_WIN:
                waits[key] = max(waits.get(key, 0), val)
            return
        if self.clock[eng].get(key, 0) >= val:
            return
        waits[key] = max(waits.get(key, 0), val)

    def _absorb(self, eng, waits):
        ck = self.clock[eng]
        for key, val in waits.items():
            if key == eng:
                continue
            if ck.get(key, 0) < val:
                ck[key] = val
            sn = self.snap.get((key, val))
            if sn:
                for k2, v2 in sn.items():
                    if k2 != eng and ck.get(k2, 0) < v2:
                        ck[k2] = v2

    def _deps(self, eng, reads, writes):
        waits = {}
        for r in reads:
            self._need(eng, _res(r).lw, waits)
        for w in writes:
            w = _res(w)
            self._need(eng, w.lw, waits)
            for k, v in w.rd.items():
                self._need(eng, (k, v), waits)
        return waits

    def op(self, eng, fn, reads=(), writes=()):
        ex = [r for r in reads if _res(r).excl]
        if ex:
            reads = [r for r in reads if not _res(r).excl]
            writes = list(writes) + ex
        waits = self._deps(eng, reads, writes)
        self._absorb(eng, waits)
        self.cnt[eng] += 1
        val = self.cnt[eng]
        self.ops[eng].append((tuple(waits.items()), fn, eng, 1))
        self.snap[(eng, val)] = dict(self.clock[eng])
        for r in reads:
            _res(r).rd[eng] = val
        for w in writes:
            w = _res(w)
            w.lw = (eng, val)
            w.rd = {}

    def dma(self, q, fn, semkey, reads=(), writes=()):
        if semkey not in self.sems:
            self._mksem(semkey)
        waits = self._deps(q, reads, writes)
        if self.cnt[semkey] > 0:
            self._need(q, (semkey, self.cnt[semkey]), waits)
        self._absorb(q, waits)
        self.cnt[semkey] += 16
        val = self.cnt[semkey]
        self.ops[q].append((tuple(waits.items()), fn, semkey, 16))
        self.snap[(semkey, val)] = dict(self.clock[q])
        for r in reads:
            _res(r).rd[semkey] = val
        for w in writes:
            w = _res(w)
            w.lw = (semkey, val)
            w.rd = {}

    def finish(self):
        for e in self.ENG:
            waits = {k: v for k, v in self.cnt.items() if v > 0 and k != e}
            self.ops[e].append((tuple(waits.items()), None, None, 0))

    def emit(self):
        nc = self.nc
        waited = {e: set() for e in self.ENG}
        for e in self.ENG:
            for waits, fn, semkey, inc in self.ops[e]:
                for k, v in waits:
                    if k in waited:
                        waited[k].add(v)
        rank = {e: {v: i + 1 for i, v in enumerate(sorted(waited[e]))} for e in self.ENG}
        with nc.Block() as block:
            def run(e, engobj):
                ci = 0
                for waits, fn, semkey, inc in self.ops[e]:
                    for k, v in waits:
                        engobj.wait_ge(self.sems[k], rank[k][v] if k in rank else v)
                    if fn is None:
                        continue
                    if semkey == e:
                        ci += 1
                        if ci in rank[e]:
                            fn(engobj).then_inc(self.sems[e], 1)
                        else:
                            fn(engobj)
                    else:
                        fn(engobj).then_inc(self.sems[semkey], inc)

            @block.tensor
            def _(eng):
                run("pe", eng)

            @block.scalar
            def _(eng):
                run("act", eng)

            @block.vector
            def _(eng):
                run("dve", eng)

            @block.gpsimd
            def _(eng):
                run("pool", eng)

            @block.sync
            def _(eng):
                run("sp", eng)

    def mm(self, out, lhsT, rhs, start, stop, reads, writes, r=False, free=True):
        self._pe_free = free
        try:
            self._mm(out, lhsT, rhs, start, stop, reads, writes, r)
        finally:
            self._pe_free = False

    def _mm(self, out, lhsT, rhs, start, stop, reads, writes, r=False):
        if r and USE_R and _base_part(out) == 0:
            lhsT = lhsT.bitcast(F32R)
            rhs = rhs.bitcast(F32R)
        elif r:
            lhsT = lhsT.bitcast(F32)
            rhs = rhs.bitcast(F32)
        self.op("pe", lambda e: e.matmul(out, lhsT=lhsT, rhs=rhs, start=start, stop=stop), reads, writes)

    def act(self, out, in_, func, reads, writes, bias=None, scale=None):
        kw = {}
        if bias is not None:
            kw["bias"] = bias
        if scale is not None:
            kw["scale"] = scale
        self.op("act", lambda e: e.activation(out=out, in_=in_, func=func, **kw), reads, writes)

    def tt(self, out, in0, in1, op, reads, writes, eng="dve"):
        self.op(eng, lambda e: e.tensor_tensor(out=out, in0=in0, in1=in1, op=op), reads, writes)

    def ts(self, out, in0, s1, op0, reads, writes, s2=None, op1=None, eng="dve"):
        if op1 is None:
            self.op(eng, lambda e: e.tensor_scalar(out=out, in0=in0, scalar1=s1, scalar2=None, op0=op0), reads, writes)
        else:
            self.op(eng, lambda e: e.tensor_scalar(out=out, in0=in0, scalar1=s1, scalar2=s2, op0=op0, op1=op1), reads, writes)

    def stt(self, out, in0, scalar, in1, op0, op1, reads, writes, eng="dve"):
        self.op(eng, lambda e: e.scalar_tensor_tensor(out=out, in0=in0, scalar=scalar, in1=in1, op0=op0, op1=op1), reads, writes)

    def copy(self, out, in_, reads, writes, eng="dve"):
        self.op(eng, lambda e: e.tensor_copy(out=out, in_=in_), reads, writes)

    def recip(self, out, in_, reads, writes):
        self.op("dve", lambda e: e.reciprocal(out=out, in_=in_), reads, writes)

    def memset(self, out, val, writes, eng="dve"):
        self.op(eng, lambda e: e.memset(out, val), (), writes)

    def rsum(self, out, in_, reads, writes):
        self.op("dve", lambda e: e.reduce_sum(out=out, in_=in_, axis=AX.X), reads, writes)

    def rmax(self, out, in_, reads, writes):
        self.op("dve", lambda e: e.reduce_max(out=out, in_=in_, axis=AX.X), reads, writes)


def _chunks():
    ch = []
    for hp in range(8):
        ch.append(("r%d" % hp, 0 + hp * 128, 128))
        ch.append(("k%d" % hp, 1024 + hp * 128, 128))
        ch.append(("v%d" % hp, 2048 + hp * 128, 128))
    ch.append(("L0", 3072, 128))
    ch.append(("L1", 3200, 128))
    ch.append(("L2", 3328, 32))
    for hp in range(4):
        ch.append(("mq%d" % hp, 3360 + hp * 128, 128))
        ch.append(("mk%d" % hp, 3872 + hp * 128, 128))
    for h in range(8):
        ch.append(("mv%d" % h, 4384 + h * 128, 128))
        ch.append(("mo%d" % h, 5424 + h * 128, 128))
    ch.append(("mg", 5408, 16))
    for j in range(16):
        ch.append(("gr%d" % j, 6448 + j * 128, 128))
        ch.append(("gm%d" % j, 8496 + j * 128, 128))
    return ch


CHUNKS = _chunks()
CIDX = {c[0]: i for i, c in enumerate(CHUNKS)}
NCH = len(CHUNKS)


def build(T, NPH2, debug=False, parts=("lora", "rwkv", "mlstm", "ph2"), lvl=9):
    NMT = T // TM
    PH2_0 = T - NPH2
    nc = bass.Bass("TRN2", target_bir_lowering=False)

    def din(name, shape):
        return nc.dram_tensor(name, list(shape), F32, kind="ExternalInput").ap()

    xT_d = din("xT", [128, KC, T])
    pT_d = din("pT", [128, 2, NPH2])
    valid_d = din("valid", [128, T // 128])
    Wp_d = din("Wp", [NCH, 128, 2048])
    W2A_d = din("W2A", [128, 1024])
    G2a_d = din("G2a", [128, 1024])
    G2b_d = din("G2b", [32, 1024])
    vecR_d = din("vecR", [128, 80])
    vecL_d = din("vecL", [128, 3])
    vecM_d = din("vecM", [128, 40])
    mhw_d = din("mhw", [128, 8])
    gb_d = din("gb", [128, 16])
    bgate_d = din("bgate", [128, 32])
    WBR_d = din("WBR", [16, 128, 1024])
    WBM_d = din("WBM", [16, 128, 1024])
    WOUT_d = din("WOUT", [16, 128, 2048])
    WPG_d = din("WPG", [16, 128, 2048])
    WPLE_d = din("WPLE", [16, 128, 256])
    WG_d = din("WG", [32, 4, 128, 2048])
    WU_d = din("WU", [32, 4, 128, 2048])
    WD_d = din("WD", [32, 4, 128, 2048])
    WR_d = din("WR", [128, KC * 36])
    rb_d = din("rb", [128, 36])
    lnp_d = din("lnp", [128, 64])
    cst_d = din("cst", [128, 128 * 5])
    mskA_d = din("mskA", [128, 192])
    mskB_d = din("mskB", [128, 192])
    mskC_d = din("mskC", [128, 128])
    ifull_d = din("ifull", [128, 512])
    sele_d = din("sele", [32, 32 * 128])
    outT_d = nc.dram_tensor("outT", [128, KC, NPH2], F32, kind="ExternalOutput").ap()
    if debug:
        dbg_d = nc.dram_tensor("dbg", [128, 2 * 8 * NPH2], F32, kind="ExternalOutput").ap()

    def sb(name, shape, dt=F32):
        return Tile(nc.alloc_sbuf_tensor("sb_" + name, list(shape), dt))

    PS = nc.alloc_psum_tensor("PS", [128, 4096], F32)
    PSR = [Res(excl=True) for _ in range(8)]
    yrT = sb("yrT", [128, 8, NPH2], BF16)
    ymT = sb("ymT", [128, 8, NPH2], BF16)
    cst = sb("cst", [128, 640])
    ident = cst[:, 0:128]
    bones = cst[:, 128:256]
    tri = cst[:, 256:384]
    ones = cst[:, 384:512]
    NW = 4
    wslot = [sb("wslot%d" % i, [128, 2048], BF16) for i in range(NW)]
    xb = sb("xb", [128, KC, TM], BF16)

    state = {"ps": 0, "w": 0}

    def psum(nb=1):
        i = state["ps"]
        if i + nb > 8:
            i = 0
        state["ps"] = (i + nb) % 8
        return i, PS[:, i * 512:(i + nb) * 512], PSR[i:i + nb]

    def wload(P, src, ncols=2048):
        i = state["w"]
        state["w"] = (i + 1) % NW
        t = wslot[i]
        P.dma("pool", lambda e: e.dma_start(out=t[:, 0:ncols], in_=src), ("w", i), writes=[t])
        return t

    def cload(P, tile, src, q="sp", key="c"):
        P.dma(q, lambda e: e.dma_start(out=tile[:], in_=src), key, writes=[tile])

    with ExitStack() as es:
        def sbt(name, shape, dt=F32):
            return Tile(es.enter_context(nc.sbuf_tensor("sb_" + name, list(shape), dt)))

        P = Prog(nc, "a")
        cload(P, cst, cst_d)
        W2A = sbt("W2A", [128, 1024], BF16)
        G2a = sbt("G2a", [128, 1024], BF16)
        G2b = sbt("G2b", [32, 1024], BF16)
        cload(P, W2A, W2A_d, "pool", "c2")
        cload(P, G2a, G2a_d, "pool", "c2")
        cload(P, G2b, G2b_d, "pool", "c2")
        vecR = sbt("vecR", [128, 80]); cload(P, vecR, vecR_d)
        vecL = sbt("vecL", [128, 3]); cload(P, vecL, vecL_d)
        vecM = sbt("vecM", [128, 40]); cload(P, vecM, vecM_d)
        mhw = sbt("mhw", [128, 8]); cload(P, mhw, mhw_d)
        gb = sbt("gb", [128, 16]); cload(P, gb, gb_d)
        valid = sbt("valid", [128, T // 128]); cload(P, valid, valid_d)
        mskA = sbt("mskA", [128, 192]); cload(P, mskA, mskA_d)
        mskB = sbt("mskB", [128, 192]); cload(P, mskB, mskB_d)
        mskC = sbt("mskC", [128, 128]); cload(P, mskC, mskC_d)
        ifull = sbt("ifull", [128, 512]); cload(P, ifull, ifull_d)
        cstR = sbt("cstR", [128, 512], F32R)
        P.copy(cstR[:], cst[:, 0:512], [cst], [cstR])
        identR = cstR[:, 0:128]
        bonesR = cstR[:, 128:256]
        triR = cstR[:, 256:384]
        onesR = cstR[:, 384:512]
        scanm = sbt("scanm", [128, 512])
        P.memset(scanm[:], 1.0, [scanm])
        P.memset(scanm[:].rearrange("p (c l) -> p c l", l=64)[:, :, 0:1], 0.0, [scanm])

        ST = sbt("ST", [128, 8, 64], F32R)
        P.memset(ST[:].bitcast(F32), 0.0, [ST])
        CS = sbt("CS", [128, 4, 129], F32R)
        P.memset(CS[:].bitcast(F32), 0.0, [CS])
        carry = sbt("carry", [128, 32])
        P.memset(carry[:], 0.0, [carry])
        ccarry = sbt("ccarry", [128, 8, 3])
        P.memset(ccarry[:], 0.0, [ccarry])

        NSC = 15
        SC = [sbt("sc%d" % i, [128, 516]) for i in range(NSC)]
        epsR = sbt("epsR", [128, 1]); P.memset(epsR[:], R_GN_EPS, [epsR])
        epsM = sbt("epsM", [128, 1]); P.memset(epsM[:], M_NORM_EPS, [epsM])
        oneC = sbt("oneC", [128, 1]); P.memset(oneC[:], 1.0, [oneC])
        TL = sbt("TL", [128, TM], BF16)
        SG0 = sbt("SG0", [128, TM], BF16)
        SG1 = sbt("SG1", [32, TM], BF16)
        R3 = sbt("R3", [128, 8, 192], F32R)
        K2 = sbt("K2", [128, 8, 128], F32R)
        EA = sbt("EA", [128, 4, 2, 192], F32R)
        EB = sbt("EB", [128, 4, 2, 192], F32R)
        EC = sbt("EC", [128, 4, 2, 128], F32R)
        XA = sbt("XA", [128, 4, 2, 64], F32R); XTA = sbt("XTA", [128, 4, 2, 64], F32R)
        XB = sbt("XB", [128, 4, 2, 64], F32R); XTB = sbt("XTB", [128, 4, 2, 64], F32R)
        PM = sbt("PM", [128, 4, 2, 64], F32R)
        VmT = sbt("VmT", [128, 4, 2, 64], F32R)
        RT = sbt("RT", [128, 2, 64], F32R); UT = sbt("UT", [128, 2, 64], F32R)
        for c in range(8):
            P.copy(R3[0:64, c, 128:192], ident[0:64, 0:64], [cst], [R3])
            P.copy(R3[64:128, c, 128:192], ident[64:128, 64:128], [cst], [R3])
        GI = sbt("GI", [128, 16]); EL = sbt("EL", [128, 8]); LL = sbt("LL", [128, 8], F32R)
        EAc = sbt("EAc", [128, 4, 8]); EKc = sbt("EKc", [128, 4, 8]); EALc = sbt("EALc", [128, 4, 8])
        tm8 = sbt("tm8", [128, 8])
        VP = sbt("VP", [128, 4, 129], F32R)
        Gm = sbt("Gm", [128, 128], F32R); kTk = sbt("kTk", [128, 64], F32R); hh = sbt("hh", [128, 128]); hn = sbt("hn", [128, 128], F32R)
        hsq = sbt("hsq", [128, 128])
        sm = sbt("sm", [128, 16])

        def inproj(cname, ncols=128, n0=0, nn=TM):
            w = wload(P, Wp_d[CIDX[cname]])
            wv = w[:].rearrange("p (k m) -> p k m", m=128)
            pi, pap, pr = psum()
            for kc in range(KC):
                P.mm(pap[0:ncols, 0:nn], wv[:, kc, 0:ncols], xb[:, kc, n0:n0 + nn], kc == 0, kc == KC - 1, [w, xb], pr)
            return pap, pr

        def shifted(cname, ci, mu_ap, zt, out_t, ncols=128, rnd=False):
            pap, pr = inproj(cname, ncols)
            P.copy(zt[0:ncols, 0:1], carry[0:ncols, ci:ci + 1], [carry], [zt])
            P.act(zt[0:ncols, 1:TM + 1], pap[0:ncols, 0:TM], AF.Copy, pr, [zt])
            P.copy(carry[0:ncols, ci:ci + 1], zt[0:ncols, TM:TM + 1], [zt], [carry])
            P.tt(out_t[0:ncols, 0:TM], zt[0:ncols, 0:TM], zt[0:ncols, 1:TM + 1], ALU.subtract, [zt], [out_t])
            oo = out_t[0:ncols, 0:TM].bitcast(F32R) if rnd else out_t[0:ncols, 0:TM]
            P.stt(oo, out_t[0:ncols, 0:TM], mu_ap, zt[0:ncols, 1:TM + 1], ALU.mult, ALU.add, [out_t, zt, vecR, vecL], [out_t])

        def bmm(in_ap, in_t):
            pi, pap, pr = psum()
            P.mm(pap[:, 0:TM], bones, in_ap, True, True, [cst, in_t], pr)
            return pap, pr

        for mt in range(NMT):
            t0 = mt * TM
            P.dma("pool", lambda e, t0=t0: e.dma_start(out=xb[:], in_=xT_d[:, :, t0:t0 + TM]), "xb", writes=[xb])
            inph2 = t0 >= PH2_0
            q0 = t0 - PH2_0
            z, o = SC[0], SC[1]
            shifted("L0", 24, vecL[:, 0:1], z, o)
            P.act(TL[0:64, :], o[0:64, 0:TM], AF.Tanh, [o], [TL])
            P.copy(TL[64:128, :], o[64:128, 0:TM], [o], [TL])
            shifted("L1", 25, vecL[:, 1:2], z, o)
            P.act(SG0[:, :], o[:, 0:TM], AF.Sigmoid, [o], [SG0])
            shifted("L2", 26, vecL[0:32, 2:3], z, o, ncols=32)
            P.act(SG1[:, :], o[0:32, 0:TM], AF.Sigmoid, [o], [SG1])
            for hp in (range(8) if "rwkv" in parts else []):
                cs = slice(hp * 128, (hp + 1) * 128)
                vr = lambda i: vecR[:, hp * 10 + i:hp * 10 + i + 1]
                rs, ks, vs = SC[2], SC[3], SC[4]
                shifted("r%d" % hp, hp * 3 + 0, vr(0), SC[0], rs)
                shifted("k%d" % hp, hp * 3 + 1, vr(1), SC[0], ks)
                shifted("v%d" % hp, hp * 3 + 2, vr(2), SC[0], vs)
                _, pw, pwr = psum()
                P.mm(pw[:, 0:TM], W2A[0:64, cs], TL[0:64, :], True, True, [W2A, TL], pwr)
                _, pa, par_ = psum()
                P.mm(pa[:, 0:TM], W2A[64:128, cs], TL[64:128, :], True, True, [W2A, TL], par_)
                _, pg, pgr = psum()
                P.mm(pg[:, 0:TM], G2a[:, cs], SG0[:, :], True, False, [G2a, SG0], pgr)
                P.mm(pg[:, 0:TM], G2b[:, cs], SG1[:, :], False, True, [G2b, SG1], pgr)
                lw, aa, gg = SC[5], SC[6], SC[7]
                P.act(lw[:, 0:TM], pw[:, 0:TM], AF.Sigmoid, pwr + [vecR], [lw], bias=vr(3))
                P.ts(lw[:, 0:TM], lw[:, 0:TM], -float(np.exp(-0.5)), ALU.mult, [lw], [lw])
                P.act(aa[:, 0:TM], pa[:, 0:TM], AF.Sigmoid, par_ + [vecR], [aa], bias=vr(4))
                P.act(gg[:, 0:TM], pg[:, 0:TM], AF.Copy, pgr, [gg])
                kk, sq, kap = SC[8], SC[9], SC[10]
                P.ts(kk[:, 0:TM], ks[:, 0:TM], vr(5), ALU.mult, [ks, vecR], [kk])
                P.tt(sq[:, 0:TM], kk[:, 0:TM], kk[:, 0:TM], ALU.mult, [kk], [sq])
                pss, pssr = bmm(sq[:, 0:TM], sq)
                P.act(sq[:, 0:TM], pss[:, 0:TM], AF.Sqrt, pssr, [sq])
                P.ts(sq[:, 0:TM], sq[:, 0:TM], 1e-12, ALU.max, [sq], [sq])
                P.recip(sq[:, 0:TM], sq[:, 0:TM], [sq], [sq])
                P.tt(kap[:, 0:TM], kk[:, 0:TM], sq[:, 0:TM], ALU.mult, [kk, sq], [kap])
                km, beta = SC[11], SC[12]
                P.ts(km[:, 0:TM], aa[:, 0:TM], -1.0, ALU.add, [aa, vecR], [km], s2=vr(6), op1=ALU.mult)
                P.stt(km[:, 0:TM], km[:, 0:TM], 1.0, ks[:, 0:TM], ALU.add, ALU.mult, [km, ks], [km])
                P.tt(beta[:, 0:TM], aa[:, 0:TM], kap[:, 0:TM], ALU.mult, [aa, kap], [beta])
                bon = SC[13]
                P.stt(bon[:, 0:TM], rs[:, 0:TM], vr(7), km[:, 0:TM], ALU.mult, ALU.mult, [rs, km, vecR], [bon])
                pb, pbr = bmm(bon[:, 0:TM], bon)
                P.tt(bon[:, 0:TM], pb[:, 0:TM], vs[:, 0:TM], ALU.mult, pbr + [vs], [bon])
                cc, ep, en, epv = SC[8], SC[14], SC[6], SC[9]
                P.op("dve", lambda e, cc=cc, lw=lw: e.tensor_tensor_scan(out=cc[:, 0:TM], data0=scanm[:, 0:TM], data1=lw[:, 0:TM],
                                                                         initial=0.0, op0=ALU.mult, op1=ALU.add), [scanm, lw], [cc])
                P.act(ep[:, 0:TM], cc[:, 0:TM], AF.Exp, [cc], [ep])
                P.act(en[:, 0:TM], cc[:, 0:TM], AF.Exp, [cc], [en], scale=-1.0)
                P.tt(epv[:, 0:TM], cc[:, 0:TM], lw[:, 0:TM], ALU.subtract, [cc, lw], [epv])
                P.act(epv[:, 0:TM], epv[:, 0:TM], AF.Exp, [epv], [epv])
                c3 = lambda t_: t_[:, 0:TM].rearrange("p (c l) -> p c l", l=64)
                P.tt(R3[:, :, 0:64], c3(kap), c3(epv), ALU.mult, [kap, epv], [R3])
                P.tt(R3[:, :, 64:128], c3(rs), c3(ep), ALU.mult, [rs, ep], [R3])
                P.tt(K2[:, :, 0:64], c3(km), c3(en), ALU.mult, [km, en], [K2])
                P.tt(K2[:, :, 64:128], c3(beta), c3(en), ALU.mult, [beta, en], [K2])
                for dc in range(4):
                    _, pt, ptr = psum()
                    P.mm(pt[:, 0:128], vs[:, dc * 128:(dc + 1) * 128], ident, True, True, [vs, cst], ptr, free=False)
                    P.copy(VmT[:, dc, :, :], pt[:, 0:128].rearrange("p (h v) -> p h v", v=64), ptr, [VmT])
                hr = lambda h: slice(h * 64, (h + 1) * 64)
                pr_ = lambda c: slice((c % 2) * 64, (c % 2) * 64 + 64)
                _, pA, pAr = psum(4)
                pAv = pA.rearrange("p (d h w) -> p d h w", d=4, h=2)
                for c in range(8):
                    for h in range(2):
                        P.mm(pAv[pr_(c), c // 2, h, 0:192], K2[hr(h), c, 0:64], R3[hr(h), c, 0:192], True, True, [K2, R3], pAr, r=True, free=False)
                for d_ in range(4):
                    for h in range(2):
                        P.tt(EA[:, d_, h, :], pAv[:, d_, h, 0:192], mskA[:], ALU.mult, pAr + [mskA], [EA])
                _, pB, pBr = psum(4)
                pBv = pB.rearrange("p (d h w) -> p d h w", d=4, h=2)
                for c in range(8):
                    for h in range(2):
                        P.mm(pBv[pr_(c), c // 2, h, 0:192], K2[hr(h), c, 64:128], R3[hr(h), c, 0:192], True, True, [K2, R3], pBr, r=True, free=False)
                for d_ in range(4):
                    for h in range(2):
                        P.tt(EB[:, d_, h, :], pBv[:, d_, h, 0:192], mskB[:], ALU.mult, pBr + [mskB], [EB])
                _, pC, pCr = psum(2)
                pCv = pC.rearrange("p (d h w) -> p d h w", d=4, h=2)
                for c in range(8):
                    for h in range(2):
                        P.mm(pCv[pr_(c), c // 2, h, 0:128], R3[hr(h), c, 0:64], K2[hr(h), c, 0:128], True, True, [K2, R3], pCr, r=True, free=False)
                for d_ in range(4):
                    for h in range(2):
                        P.tt(EC[:, d_, h, :], pCv[:, d_, h, :], mskC[:], ALU.mult, pCr + [mskC], [EC])
                X = (EB, lambda c, h: EB[pr_(c), c // 2, h, 0:64])
                XT = (EC, lambda c, h: EC[pr_(c), c // 2, h, 64:128])
                P.tt(PM[:], EB[:, :, :, 0:64], ifull[:].rearrange("p (d h w) -> p d h w", d=4, h=2), ALU.add, [EB, ifull], [PM])
                bufs = [(XA, XTA), (XB, XTB)]
                for it in range(5):
                    nX, nXT = bufs[it % 2]
                    last = it == 4
                    _, p2, p2r = psum()
                    p2v = p2.rearrange("p (d h w) -> p d h w", d=4, h=2)
                    for c in range(8):
                        for h in range(2):
                            P.mm(p2v[pr_(c), c // 2, h, :], X[1](c, h), XT[1](c, h), True, True, [X[0], XT[0]], p2r, r=True, free=False)
                    P.act(nXT[:].rearrange("p d h w -> p (d h w)"), p2, AF.Copy, p2r, [nXT])
                    if not last:
                        _, p1, p1r = psum()
                        p1v = p1.rearrange("p (d h w) -> p d h w", d=4, h=2)
                        for c in range(8):
                            for h in range(2):
                                P.mm(p1v[pr_(c), c // 2, h, :], XT[1](c, h), X[1](c, h), True, True, [X[0], XT[0]], p1r, r=True, free=False)
                        P.act(nX[:].rearrange("p d h w -> p (d h w)"), p1, AF.Copy, p1r, [nX])
                    _, p3, p3r = psum()
                    p3v = p3.rearrange("p (d h w) -> p d h w", d=4, h=2)
                    for c in range(8):
                        for h in range(2):
                            P.mm(p3v[pr_(c), c // 2, h, :], nXT[pr_(c), c // 2, h, :], PM[pr_(c), c // 2, h, :], True, True, [nXT, PM], p3r, r=True, free=False)
                    P.tt(PM[:].rearrange("p d h w -> p (d h w)"), PM[:].rearrange("p d h w -> p (d h w)"), p3, ALU.add, p3r + [PM], [PM])
                    X = (nX, lambda c, h, nX=nX: nX[pr_(c), c // 2, h, :])
                    XT = (nXT, lambda c, h, nXT=nXT: nXT[pr_(c), c // 2, h, :])
                yb = SC[3]
                for c in range(8):
                    rows = pr_(c)
                    dc = c // 2
                    _, p1, p1r = psum()
                    for h in range(2):
                        P.mm(p1[rows, h * 64:(h + 1) * 64], R3[hr(h), c, 0:64], ST[hr(h), hp, :], True, False, [R3, ST], p1r, r=True, free=False)
                        P.mm(p1[rows, h * 64:(h + 1) * 64], EA[rows, dc, h, 0:64], VmT[rows, dc, h, :], False, True, [EA, VmT], p1r, r=True, free=False)
                    P.act(RT[rows, :, :], p1[rows, 0:128].rearrange("p (h v) -> p h v", v=64), AF.Copy, p1r, [RT])
                    _, p2, p2r = psum()
                    for h in range(2):
                        P.mm(p2[rows, h * 64:(h + 1) * 64], PM[rows, dc, h, :], RT[rows, h, :], True, True, [PM, RT], p2r, r=True, free=False)
                    P.copy(UT[rows, :, :], p2[rows, 0:128].rearrange("p (h v) -> p h v", v=64), p2r, [UT])
                    _, pY, pYr = psum()
                    _, pS, pSr = psum()
                    for h in range(2):
                        P.mm(pY[hr(h), 0:64], ST[hr(h), hp, :], R3[hr(h), c, 64:128], True, False, [ST, R3], pYr, r=True, free=False)
                        P.mm(pY[hr(h), 0:64], VmT[rows, dc, h, :], EA[rows, dc, h, 64:128], False, False, [VmT, EA], pYr, r=True, free=False)
                        P.mm(pY[hr(h), 0:64], UT[rows, h, :], EB[rows, dc, h, 64:128], False, True, [UT, EB], pYr, r=True, free=False)
                        idh = ident[hr(h), hr(h)]
                        P.mm(pS[hr(h), 0:64], idh, ST[hr(h), hp, :].bitcast(F32), True, False, [cst, ST], pSr, free=False)
                        P.mm(pS[hr(h), 0:64], EA[rows, dc, h, 128:192], VmT[rows, dc, h, :], False, False, [EA, VmT], pSr, r=True, free=False)
                        P.mm(pS[hr(h), 0:64], EB[rows, dc, h, 128:192], UT[rows, h, :], False, True, [EB, UT], pSr, r=True, free=False)
                    P.act(yb[:, c * 64:(c + 1) * 64], pY[:, 0:64], AF.Copy, pYr, [yb])
                    P.ts(ST[:, hp, :], pS[:, 0:64], ep[:, c * 64 + 63:c * 64 + 64], ALU.mult, pSr + [ep], [ST])
                if inph2:
                    pm, pmr = bmm(yb[:, 0:TM], yb)
                    mean, dd, var = SC[10], SC[11], SC[12]
                    P.act(mean[:, 0:TM], pm[:, 0:TM], AF.Copy, pmr, [mean], scale=1.0 / 64)
                    P.tt(dd[:, 0:TM], yb[:, 0:TM], mean[:, 0:TM], ALU.subtract, [yb, mean], [dd])
                    P.tt(var[:, 0:TM], dd[:, 0:TM], dd[:, 0:TM], ALU.mult, [dd], [var])
                    pq, pqr = bmm(var[:, 0:TM], var)
                    P.act(var[:, 0:TM], pq[:, 0:TM], AF.Sqrt, pqr + [epsR], [var], scale=1.0 / 64, bias=epsR[:, 0:1])
                    P.recip(var[:, 0:TM], var[:, 0:TM], [var], [var])
                    P.tt(dd[:, 0:TM], dd[:, 0:TM], var[:, 0:TM], ALU.mult, [dd, var], [dd])
                    P.act(dd[:, 0:TM], dd[:, 0:TM], AF.Identity, [dd, vecR], [dd], scale=vr(8), bias=vr(9))
                    P.tt(dd[:, 0:TM], dd[:, 0:TM], bon[:, 0:TM], ALU.add, [dd, bon], [dd])
                    P.tt(yrT[:, hp, q0:q0 + TM], dd[:, 0:TM], gg[:, 0:TM], ALU.mult, [dd, gg], [yrT])
            wmg = wload(P, Wp_d[CIDX["mg"]])
            wmgv = wmg[:].rearrange("p (k m) -> p k m", m=128)
            for ck in (range(4) if "mlstm" in parts else []):
                _, pgt, pgtr = psum()
                for kc in range(KC):
                    P.mm(pgt[:, 0:16], xb[:, kc, ck * 128:(ck + 1) * 128], wmgv[:, kc, 0:16], kc == 0, kc == KC - 1, [xb, wmg], pgtr)
                P.tt(GI[:], pgt[:, 0:16], gb[:], ALU.add, pgtr + [gb], [GI])
                P.act(EL[:], GI[:, 8:16], AF.Exp, [GI], [EL], scale=-1.0)
                P.act(LL[:], EL[:], AF.Ln, [EL, oneC], [LL], bias=oneC[:, 0:1])
                _, pc, pcr = psum()
                P.mm(pc[:, 0:8], triR, LL[:], True, True, [cstR, LL], pcr, r=True)
                P.mm(pc[:, 8:16], onesR, LL[:], True, True, [cstR, LL], pcr, r=True)
                P.act(EAc[:, ck, :], pc[:, 0:8], AF.Exp, pcr, [EAc], scale=-1.0)
                P.tt(tm8[:], GI[:, 0:8], pc[:, 0:8], ALU.add, pcr + [GI], [tm8])
                P.act(EKc[:, ck, :], tm8[:], AF.Exp, [tm8], [EKc])
                P.act(tm8[:], pc[:, 8:16], AF.Exp, pcr, [tm8], scale=-1.0)
                gck = mt * 4 + ck
                P.ts(EALc[:, ck, :], tm8[:], valid[:, gck:gck + 1], ALU.mult, [tm8, valid], [EALc])
            for hp2 in (range(4) if "mlstm" in parts else []):
                qf, kf = SC[2], SC[3]
                for which, dst in (("mq", qf), ("mk", kf)):
                    ci = hp2 * 2 + (0 if which == "mq" else 1)
                    vm = lambda i: vecM[:, ci * 5 + i:ci * 5 + i + 1]
                    pap, pr = inproj("%s%d" % (which, hp2))
                    zc = SC[0]
                    P.copy(zc[:, 0:3], ccarry[:, ci, :], [ccarry], [zc])
                    P.act(zc[:, 3:TM + 3], pap[:, 0:TM], AF.Copy, pr, [zc])
                    P.copy(ccarry[:, ci, :], zc[:, TM:TM + 3], [zc], [ccarry])
                    acc = SC[1]
                    P.ts(acc[:, 0:TM], zc[:, 0:TM], vm(0), ALU.mult, [zc, vecM], [acc], s2=vm(4), op1=ALU.add)
                    for j in range(1, 4):
                        P.stt(acc[:, 0:TM], zc[:, j:j + TM], vm(j), acc[:, 0:TM], ALU.mult, ALU.add, [zc, acc, vecM], [acc])
                    if which == "mk":
                        P.act(dst[:, 0:TM], acc[:, 0:TM], AF.Silu, [acc], [dst])
                        P.ts(dst[:, 0:TM], dst[:, 0:TM], 0.125, ALU.mult, [dst], [dst])
                    else:
                        P.act(dst[:, 0:TM], acc[:, 0:TM], AF.Silu, [acc], [dst])
                for hh_ in range(2):
                    h = hp2 * 2 + hh_
                    hrows = slice(hh_ * 64, hh_ * 64 + 64)
                    so = SC[4]
                    pap, pr = inproj("mo%d" % h)
                    P.act(so[:, 0:TM], pap[:, 0:TM], AF.Sigmoid, pr, [so])
                    wv_ = wload(P, Wp_d[CIDX["mv%d" % h]])
                    wvv = wv_[:].rearrange("p (k m) -> p k m", m=128)
                    for ck in range(4):
                        _, pv, pvr = psum()
                        for kc in range(KC):
                            P.mm(pv[:, 0:128], xb[:, kc, ck * 128:(ck + 1) * 128], wvv[:, kc, :], kc == 0, kc == KC - 1, [xb, wv_], pvr)
                        P.act(VP[:, ck, 0:128], pv[:, 0:128], AF.Copy, pvr, [VP])
                    P.memset(VP[:, :, 128:129].bitcast(F32), 1.0, [VP])
                    for ck in range(4):
                        tk = slice(ck * 128, (ck + 1) * 128)
                        _, pG, pGr = psum()
                        P.mm(pG[:, 0:128], kf[hrows, tk], qf[hrows, tk], True, True, [kf, qf], pGr)
                        P.stt(Gm[:], pG[:, 0:128], EKc[:, ck, h:h + 1], tri, ALU.mult, ALU.mult, pGr + [EKc, cst], [Gm])
                        _, pK, pKr = psum()
                        P.mm(pK[:, 0:64], kf[hrows, tk], ident[hrows, hrows], True, True, [kf, cst], pKr)
                        P.ts(kTk[:], pK[:, 0:64], EKc[:, ck, h:h + 1], ALU.mult, pKr + [EKc], [kTk])
                        _, pN, pNr = psum()
                        P.mm(pN[:, 0:129], Gm[:], VP[:, ck, :], True, False, [Gm, VP], pNr, r=True)
                        P.mm(pN[:, 0:129], qf[hrows, tk], CS[hrows, hp2, :].bitcast(F32), False, True, [qf, CS], pNr)
                        _, pS, pSr = psum()
                        P.mm(pS[hrows, 0:129], ident[hrows, hrows], CS[hrows, hp2, :].bitcast(F32), True, False, [cst, CS], pSr)
                        P.mm(pS[hrows, 0:129], kTk[:], VP[:, ck, :], False, True, [kTk, VP], pSr, r=True)
                        if inph2:
                            P.tt(sm[:, 0:1], pN[:, 128:129], EAc[:, ck, h:h + 1], ALU.mult, pNr + [EAc], [sm])
                            P.act(sm[:, 1:2], sm[:, 0:1], AF.Abs, [sm], [sm])
                            P.ts(sm[:, 1:2], sm[:, 1:2], 1.0, ALU.max, [sm], [sm])
                            P.recip(sm[:, 2:3], sm[:, 1:2], [sm], [sm])
                            P.tt(sm[:, 3:4], sm[:, 2:3], EAc[:, ck, h:h + 1], ALU.mult, [sm, EAc], [sm])
                            P.ts(hh[:], pN[:, 0:128], sm[:, 3:4], ALU.mult, pNr + [sm], [hh])
                            P.rsum(sm[:, 4:5], hh[:], [hh], [sm])
                            P.ts(sm[:, 5:6], sm[:, 4:5], 1.0 / 128, ALU.mult, [sm], [sm])
                            P.ts(hn[:], hh[:], sm[:, 5:6], ALU.subtract, [hh, sm], [hn])
                            P.tt(hsq[:], hn[:], hn[:], ALU.mult, [hn], [hsq])
                            P.rsum(sm[:, 6:7], hsq[:], [hsq], [sm])
                            P.act(sm[:, 7:8], sm[:, 6:7], AF.Sqrt, [sm, epsM], [sm], scale=1.0 / 128, bias=epsM[:, 0:1])
                            P.recip(sm[:, 8:9], sm[:, 7:8], [sm], [sm])
                            P.ts(hn[:], hn[:], sm[:, 8:9], ALU.mult, [hn, sm], [hn])
                            _, pT_, pTr = psum()
                            P.mm(pT_[:, 0:128], hn[:], identR, True, True, [hn, cstR], pTr, r=True)
                            P.stt(ymT[:, h, q0 + ck * 128:q0 + (ck + 1) * 128], pT_[:, 0:128], mhw[:, h:h + 1], so[:, tk],
                                  ALU.mult, ALU.mult, pTr + [mhw, so], [ymT])
                        P.ts(CS[hrows, hp2, :], pS[hrows, 0:129], EALc[hrows, ck, h:h + 1], ALU.mult, pSr + [EALc], [CS])
        if debug:
            dtile = sbt("dtile", [128, NPH2])
            for i, src in enumerate([yrT, ymT]):
                for j in range(8):
                    P.copy(dtile[:], src[:, j, :], [src], [dtile])
                    off = (i * 8 + j) * NPH2
                    P.dma("sp", lambda e, off=off: e.dma_start(out=dbg_d[:, off:off + NPH2], in_=dtile[:]), "dbg", reads=[dtile])
        P.finish()
        P.emit()

    for r_ in PSR + [t_.r for t_ in [yrT, ymT, cst, xb] + wslot]:
        r_.lw = None
        r_.rd = {}

    with ExitStack() as es:
        def sbt(name, shape, dt=F32):
            return Tile(es.enter_context(nc.sbuf_tensor("sb_" + name, list(shape), dt)))

        P = Prog(nc, "b")
        epsL = sbt("epsL", [128, 1]); P.memset(epsL[:], LN_EPS, [epsL])
        bgate = sbt("bgate", [128, 32]); cload(P, bgate, bgate_d)
        lnp = sbt("lnp", [128, 64]); cload(P, lnp, lnp_d)
        WR = sbt("WR", [128, KC, 36]); cload(P, WR, WR_d.rearrange("p (k m) -> p k m", m=36))
        rb = sbt("rb", [128, 36]); cload(P, rb, rb_d)
        sele = sbt("sele", [32, 32, 128]); cload(P, sele, sele_d.rearrange("p (e m) -> p e m", m=128))
        ZB = sbt("ZB", [128, KC, TM])
        mrg = sbt("mrg", [128, KC, TM], BF16)
        x1T = sbt("x1T", [128, KC, TM], BF16)
        pTb = sbt("pTb", [128, 2, TM], BF16)
        hT = sbt("hT", [128, 4, TM], BF16)
        TS = [sbt("ts%d" % i, [128, TM]) for i in range(6)]
        COEFT = sbt("COEFT", [32, TM])
        cbt = sbt("cbt", [128, TM])
        LG = sbt("LG", [128, 36]); R8 = sbt("R8", [128, 40]); LE2 = sbt("LE2", [128, 32]); OH1 = sbt("OH1", [128, 32])
        OH2 = sbt("OH2", [128, 32]); COEF = sbt("COEF", [128, 32]); rs_ = sbt("rs_", [128, 16])

        def inproj2(cname):
            w = wload(P, Wp_d[CIDX[cname]])
            wv = w[:].rearrange("p (k m) -> p k m", m=128)
            _, pap, pr = psum()
            for kc in range(KC):
                P.mm(pap[:, 0:TM], wv[:, kc, :], xb[:, kc, :], kc == 0, kc == KC - 1, [w, xb], pr)
            return pap, pr

        def proj(src_d, nk, rhs_t, rhs_fn):
            w = wload(P, src_d) if nk == 16 else None
            return w

        def layernorm(wcol, bcol):
            _, psu, psur = psum()
            _, psq, psqr = psum()
            for j in range(KC):
                sq = TS[j % 2]
                P.act(sq[:], ZB[:, j, :], AF.Square, [ZB], [sq])
                P.mm(psu[:, 0:TM], ones, ZB[:, j, :], j == 0, j == KC - 1, [cst, ZB], psur)
                P.mm(psq[:, 0:TM], ones, sq[:], j == 0, j == KC - 1, [cst, sq], psqr)
            mean, rstd, msq = TS[2], TS[3], TS[4]
            P.act(mean[:], psu[:, 0:TM], AF.Copy, psur, [mean], scale=1.0 / D)
            P.tt(msq[:], mean[:], mean[:], ALU.mult, [mean], [msq])
            P.stt(rstd[:], psq[:, 0:TM], 1.0 / D, msq[:], ALU.mult, ALU.subtract, psqr + [msq], [rstd])
            P.act(rstd[:], rstd[:], AF.Sqrt, [rstd, epsL], [rstd], bias=epsL[:, 0:1])
            P.recip(rstd[:], rstd[:], [rstd], [rstd])
            for j in range(KC):
                d_ = TS[j % 2]
                P.tt(d_[:], ZB[:, j, :], mean[:], ALU.subtract, [ZB, mean], [d_])
                P.tt(d_[:], d_[:], rstd[:], ALU.mult, [d_, rstd], [d_])
                P.act(ZB[:, j, :], d_[:], AF.Identity, [d_, lnp], [ZB], scale=lnp[:, wcol + j:wcol + j + 1], bias=lnp[:, bcol + j:bcol + j + 1])

        for tt_ in (range(NPH2 // TM) if "ph2" in parts else []):
            q0 = tt_ * TM
            g0 = PH2_0 + q0
            P.dma("pool", lambda e, g0=g0: e.dma_start(out=xb[:], in_=xT_d[:, :, g0:g0 + TM]), "xb", writes=[xb])
            P.dma("sp", lambda e, g0=g0: e.dma_start(out=ZB[:], in_=xT_d[:, :, g0:g0 + TM]), "zb", writes=[ZB])
            P.dma("pool", lambda e, q0=q0: e.dma_start(out=pTb[:], in_=pT_d[:, :, q0:q0 + TM]), "ptb", writes=[pTb])
            for j in (range(KC) if lvl >= 1 else []):
                pgr, pgrr = inproj2("gr%d" % j)
                sgr = TS[0]
                P.act(sgr[:], pgr[:, 0:TM], AF.Sigmoid, pgrr + [bgate], [sgr], bias=bgate[:, j:j + 1])
                pgm, pgmr = inproj2("gm%d" % j)
                sgm = TS[1]
                P.act(sgm[:], pgm[:, 0:TM], AF.Sigmoid, pgmr + [bgate], [sgm], bias=bgate[:, 16 + j:17 + j])
                w = wload(P, WBR_d[j], 1024)
                wv = w[:, 0:1024].rearrange("p (k m) -> p k m", m=128)
                _, ppr, pprr = psum()
                for kc in range(8):
                    P.mm(ppr[:, 0:TM], wv[:, kc, :], yrT[:, kc, q0:q0 + TM], kc == 0, kc == 7, [w, yrT], pprr)
                P.tt(sgr[:], sgr[:], ppr[:, 0:TM], ALU.mult, pprr + [sgr], [sgr])
                w = wload(P, WBM_d[j], 1024)
                wv = w[:, 0:1024].rearrange("p (k m) -> p k m", m=128)
                _, ppm, ppmr = psum()
                for kc in range(8):
                    P.mm(ppm[:, 0:TM], wv[:, kc, :], ymT[:, kc, q0:q0 + TM], kc == 0, kc == 7, [w, ymT], ppmr)
                P.tt(sgm[:], sgm[:], ppm[:, 0:TM], ALU.mult, ppmr + [sgm], [sgm])
                P.tt(mrg[:, j, :], sgr[:], sgm[:], ALU.add, [sgr, sgm], [mrg])
            for j in (range(KC) if lvl >= 2 else []):
                w = wload(P, WOUT_d[j])
                wv = w[:].rearrange("p (k m) -> p k m", m=128)
                _, pm, pmr = psum()
                for kc in range(KC):
                    P.mm(pm[:, 0:TM], wv[:, kc, :], mrg[:, kc, :], kc == 0, kc == KC - 1, [w, mrg], pmr)
                P.stt(ZB[:, j, :], ZB[:, j, :], ALPHA, pm[:, 0:TM], ALU.mult, ALU.add, pmr + [ZB], [ZB])
            if lvl >= 3:
                layernorm(0, 16)
            for j in range(KC):
                P.copy(x1T[:, j, :], ZB[:, j, :], [ZB], [x1T])
            for ts_ in (range(TM // 128) if lvl >= 4 else []):
                tk = slice(ts_ * 128, (ts_ + 1) * 128)
                _, pl, plr = psum()
                for j in range(KC):
                    P.mm(pl[:, 0:36], ZB[:, j, tk], WR[:, j, :], j == 0, j == KC - 1, [ZB, WR], plr)
                P.tt(LG[:], pl[:, 0:36], rb[:], ALU.add, plr + [rb], [LG])
                P.rmax(R8[:, 0:1], LG[:, 0:4], [LG], [R8])
                P.ts(R8[:, 4:8], LG[:, 0:4], R8[:, 0:1], ALU.is_equal, [LG, R8], [R8])
                P.ts(R8[:, 1:2], R8[:, 0:1], -1.0, ALU.mult, [R8], [R8])
                P.act(R8[:, 8:12], LG[:, 0:4], AF.Exp, [LG, R8], [R8], bias=R8[:, 1:2])
                P.rsum(R8[:, 2:3], R8[:, 8:12], [R8], [R8])
                P.recip(R8[:, 3:4], R8[:, 2:3], [R8], [R8])
                P.ts(R8[:, 12:16], R8[:, 4:8], -1.0, ALU.add, [R8], [R8], s2=1e30, op1=ALU.mult)
                for g in range(4):
                    P.ts(LE2[:, g * 8:(g + 1) * 8], LG[:, 4 + g * 8:12 + g * 8], R8[:, 12 + g:13 + g], ALU.add, [LG, R8], [LE2])
                P.rmax(R8[:, 16:17], LE2[:], [LE2], [R8])
                P.ts(OH1[:], LE2[:], R8[:, 16:17], ALU.is_equal, [LE2, R8], [OH1])
                P.stt(LE2[:], OH1[:], -1e30, LE2[:], ALU.mult, ALU.add, [OH1, LE2], [LE2])
                P.rmax(R8[:, 17:18], LE2[:], [LE2], [R8])
                P.ts(OH2[:], LE2[:], R8[:, 17:18], ALU.is_equal, [LE2, R8], [OH2])
                P.tt(R8[:, 18:19], R8[:, 17:18], R8[:, 16:17], ALU.subtract, [R8], [R8])
                P.act(R8[:, 19:20], R8[:, 18:19], AF.Exp, [R8], [R8])
                P.ts(R8[:, 20:21], R8[:, 19:20], 1.0, ALU.add, [R8], [R8])
                P.recip(R8[:, 21:22], R8[:, 20:21], [R8], [R8])
                P.tt(R8[:, 22:23], R8[:, 21:22], R8[:, 3:4], ALU.mult, [R8], [R8])
                P.tt(R8[:, 23:24], R8[:, 22:23], R8[:, 19:20], ALU.mult, [R8], [R8])
                P.ts(COEF[:], OH1[:], R8[:, 22:23], ALU.mult, [OH1, R8], [COEF])
                P.stt(COEF[:], OH2[:], R8[:, 23:24], COEF[:], ALU.mult, ALU.add, [OH2, R8, COEF], [COEF])
                _, pct, pctr = psum()
                P.mm(pct[0:32, 0:128], COEF[:], ident, True, True, [COEF, cst], pctr)
                P.copy(COEFT[:, tk], pct[0:32, 0:128], pctr, [COEFT])
            for j in (range(KC) if lvl >= 5 else []):
                w = wload(P, WPG_d[j])
                wv = w[:].rearrange("p (k m) -> p k m", m=128)
                _, pp, ppr_ = psum()
                for kc in range(KC):
                    P.mm(pp[:, 0:TM], wv[:, kc, :], x1T[:, kc, :], kc == 0, kc == KC - 1, [w, x1T], ppr_)
                sg = TS[0]
                P.act(sg[:], pp[:, 0:TM], AF.Sigmoid, ppr_, [sg])
                w = wload(P, WPLE_d[j], 256)
                wv = w[:, 0:256].rearrange("p (k m) -> p k m", m=128)
                _, pq, pqr = psum()
                for kc in range(2):
                    P.mm(pq[:, 0:TM], wv[:, kc, :], pTb[:, kc, :], kc == 0, kc == 1, [w, pTb], pqr)
                P.tt(sg[:], sg[:], pq[:, 0:TM], ALU.mult, pqr + [sg], [sg])
                P.stt(ZB[:, j, :], ZB[:, j, :], ALPHA, sg[:], ALU.mult, ALU.add, [ZB, sg], [ZB])
            for e_ in (range(32) if lvl >= 6 else []):
                _, pcb, pcbr = psum()
                P.mm(pcb[:, 0:TM], sele[:, e_, :], COEFT[:], True, True, [sele, COEFT], pcbr)
                P.act(cbt[:], pcb[:, 0:TM], AF.Copy, pcbr, [cbt])
                for f in range(4):
                    w = wload(P, WG_d[e_, f])
                    wv = w[:].rearrange("p (k m) -> p k m", m=128)
                    _, pg, pgr_ = psum()
                    for kc in range(KC):
                        P.mm(pg[:, 0:TM], wv[:, kc, :], x1T[:, kc, :], kc == 0, kc == KC - 1, [w, x1T], pgr_)
                    w2 = wload(P, WU_d[e_, f])
                    wv2 = w2[:].rearrange("p (k m) -> p k m", m=128)
                    _, pu, pur = psum()
                    for kc in range(KC):
                        P.mm(pu[:, 0:TM], wv2[:, kc, :], x1T[:, kc, :], kc == 0, kc == KC - 1, [w2, x1T], pur)
                    sg = TS[f % 2]
                    P.act(sg[:], pg[:, 0:TM], AF.Silu, pgr_, [sg])
                    P.tt(sg[:], sg[:], pu[:, 0:TM], ALU.mult, pur + [sg], [sg])
                    P.tt(hT[:, f, :], sg[:], cbt[:], ALU.mult, [sg, cbt], [hT])
                for dg in range(4):
                    w = wload(P, WD_d[e_, dg])
                    wv = w[:].rearrange("p (c k m) -> p c k m", c=4, k=4)
                    for dcc in range(4):
                        j = dg * 4 + dcc
                        _, pd, pdr = psum()
                        for kc in range(4):
                            P.mm(pd[:, 0:TM], wv[:, dcc, kc, :], hT[:, kc, :], kc == 0, kc == 3, [w, hT], pdr)
                        P.tt(ZB[:, j, :], ZB[:, j, :], pd[:, 0:TM], ALU.add, pdr + [ZB], [ZB])
            if lvl >= 7:
                layernorm(32, 48)
            P.dma("sp", lambda e, q0=q0: e.dma_start(out=outT_d[:, :, q0:q0 + TM], in_=ZB[:]), "out", reads=[ZB])
        P.finish()
        P.emit()
    return nc


def _pack_lhsT(W):
    K, N = W.shape
    return np.ascontiguousarray(W.reshape(K // 128, 128, N // 128, 128).transpose(2, 1, 0, 3)).reshape(N // 128, 128, K)


def _consts():
    p = np.arange(128)
    ident = np.eye(128, dtype=np.float32)
    bones = (p[:, None] // 64 == p[None, :] // 64).astype(np.float32)
    tri = (p[:, None] <= p[None, :]).astype(np.float32)
    ones = np.ones((128, 128), np.float32)
    cst = np.concatenate([ident, bones, tri, ones, np.zeros((128, 128), np.float32)], axis=1)
    j = (p % 64)[:, None]
    t = np.arange(64)[None, :]
    lt = (j < t).astype(np.float32)
    le = (j <= t).astype(np.float32)
    one = np.ones((128, 64), np.float32)
    zero = np.zeros((128, 64), np.float32)
    mskA = np.concatenate([lt, le, one], axis=1)
    mskB = -mskA
    gt = (j > t).astype(np.float32)
    mskC = np.concatenate([gt, -gt], axis=1)
    i64 = (j == t).astype(np.float32)
    ifull = np.tile(i64, (1, 8))
    sele = np.zeros((32, 32, 128), np.float32)
    for e in range(32):
        sele[e, e, :] = 1.0
    return cst, mskA, mskB, mskC, ifull, sele.reshape(32, 32 * 128)


def _prep_shared(inp):
    g = lambda k: np.asarray(inp[k], dtype=np.float32)[0]
    w_in = g("w_in")
    Wp = np.zeros((NCH, 128, 2048), np.float32)
    for i, (_, c0, n) in enumerate(CHUNKS):
        blk = np.zeros((2048, 128), np.float32)
        blk[:, :n] = w_in[:, c0:c0 + n]
        Wp[i] = blk.reshape(16, 128, 128).transpose(1, 0, 2).reshape(128, 2048)
    sh = {"Wp": Wp}
    sh["W2A"] = np.ascontiguousarray(np.concatenate([g("w_w2"), g("w_a2")], axis=0))
    wg2 = g("w_g2")
    sh["G2a"] = np.ascontiguousarray(wg2[0:128])
    sh["G2b"] = np.ascontiguousarray(wg2[128:160])
    mu = g("mu_shift")
    vecR = np.zeros((128, 8, 10), np.float32)
    rk = g("r_k").reshape(1024)
    for hp in range(8):
        s = slice(hp * 128, (hp + 1) * 128)
        vecR[:, hp, 0] = mu[0:1024][s]
        vecR[:, hp, 1] = mu[1024:2048][s]
        vecR[:, hp, 2] = mu[2048:3072][s]
        vecR[:, hp, 3] = g("w0")[s]
        vecR[:, hp, 4] = g("a0")[s]
        vecR[:, hp, 5] = g("k_k")[s]
        vecR[:, hp, 6] = g("k_a")[s]
        vecR[:, hp, 7] = rk[s]
        vecR[:, hp, 8] = g("lnx_w")[s]
        vecR[:, hp, 9] = g("lnx_b")[s]
    sh["vecR"] = vecR.reshape(128, 80)
    vecL = np.zeros((128, 3), np.float32)
    vecL[:, 0] = mu[3072:3200]
    vecL[:, 1] = mu[3200:3328]
    vecL[0:32, 2] = mu[3328:3360]
    sh["vecL"] = vecL
    cw, cb = g("conv_w"), g("conv_b")
    vecM = np.zeros((128, 8, 5), np.float32)
    for hp2 in range(4):
        for wi, base in ((0, 0), (1, 512)):
            s = slice(base + hp2 * 128, base + (hp2 + 1) * 128)
            ci = hp2 * 2 + wi
            for j in range(4):
                vecM[:, ci, j] = cw[j, s]
            vecM[:, ci, 4] = cb[s]
    sh["vecM"] = vecM.reshape(128, 40)
    sh["mhw"] = np.ascontiguousarray(g("mh_w").reshape(8, 128).T)
    sh["gb"] = np.ascontiguousarray(np.broadcast_to(np.concatenate([g("i_bias"), g("f_bias")])[None, :], (128, 16)))
    sh["bgate"] = np.ascontiguousarray(g("b_gate").reshape(32, 128).T)
    sh["WBR"] = _pack_lhsT(g("w_br"))
    sh["WBM"] = _pack_lhsT(g("w_bm"))
    sh["WOUT"] = _pack_lhsT(g("w_out"))
    sh["WPG"] = _pack_lhsT(g("w_pg"))
    sh["WPLE"] = _pack_lhsT(g("w_ple"))
    wgt = g("w_gate")
    sh["WG"] = np.ascontiguousarray(wgt.reshape(32, 16, 128, 4, 128).transpose(0, 3, 2, 1, 4)).reshape(32, 4, 128, 2048)
    wup = g("w_up")
    sh["WU"] = np.ascontiguousarray(wup.reshape(32, 16, 128, 4, 128).transpose(0, 3, 2, 1, 4)).reshape(32, 4, 128, 2048)
    wdn = g("w_down")
    sh["WD"] = np.ascontiguousarray(wdn.reshape(32, 4, 128, 4, 4, 128).transpose(0, 3, 2, 4, 1, 5)).reshape(32, 4, 128, 2048)
    wr = np.concatenate([g("w_rg"), g("w_re")], axis=1)
    sh["WR"] = np.ascontiguousarray(wr.reshape(16, 128, 36).transpose(1, 0, 2)).reshape(128, 16 * 36)
    sh["rb"] = np.ascontiguousarray(np.broadcast_to(np.concatenate([g("b_rg"), g("b_re")])[None, :], (128, 36)))
    lnp = np.zeros((128, 64), np.float32)
    for i, k in enumerate(("ln1_w", "ln1_b", "ln2_w", "ln2_b")):
        lnp[:, i * 16:(i + 1) * 16] = g(k).reshape(16, 128).T
    sh["lnp"] = lnp
    cst, mskA, mskB, mskC, ifull, sele = _consts()
    sh.update({"cst": cst, "mskA": mskA, "mskB": mskB, "mskC": mskC, "ifull": ifull, "sele": sele})
    return sh


def _prep_core(x, p, b, half, T, NPH2):
    S = x.shape[1]
    end = (half + 1) * NPH2
    start = end - T
    win = np.zeros((T, D), np.float32)
    valid = np.zeros((T,), np.float32)
    s0 = max(start, 0)
    win[s0 - start:] = x[b, s0:end]
    valid[s0 - start:] = 1.0
    xT = np.ascontiguousarray(win.T.reshape(16, 128, T).transpose(1, 0, 2))
    pp = p[0, b, end - NPH2:end]
    pT = np.ascontiguousarray(pp.T.reshape(2, 128, NPH2).transpose(1, 0, 2))
    vch = np.ascontiguousarray(np.broadcast_to(valid.reshape(T // 128, 128)[:, 0][None, :], (128, T // 128)))
    return {"xT": xT, "pT": pT, "valid": vch}


def kernel(**inputs):
    x = np.asarray(inputs["x"], dtype=np.float32)
    p = np.asarray(inputs["p"], dtype=np.float32)
    B, S, _ = x.shape
    T, NPH2 = S, S // 2
    sh = _prep_shared(inputs)
    nc = build(T, NPH2)
    in_maps = []
    for c in range(8):
        m = dict(sh)
        m.update(_prep_core(x, p, c // 2, c % 2, T, NPH2))
        in_maps.append(m)
    res = run_bass_kernel_spmd(nc, in_maps, core_ids=list(range(8)))
    out = np.zeros((B, S, D), np.float32)
    for c in range(8):
        oT = res.results[c]["outT"]
        b, half = c // 2, c % 2
        out[b, half * NPH2:(half + 1) * NPH2, :] = oT.transpose(2, 1, 0).reshape(NPH2, D)
    return out
```

```python
import numpy as np
import concourse.bass as bass
import concourse.mybir as mybir
from concourse.bass_utils import run_bass_kernel_spmd
from contextlib import ExitStack

F32 = mybir.dt.float32
BF16 = mybir.dt.bfloat16
F32R = mybir.dt.float32r
ALU = mybir.AluOpType
AF = mybir.ActivationFunctionType
AX = mybir.AxisListType

D = 2048
KC = 16
ALPHA = 2.0 ** 0.25
LN_EPS = 1e-5
R_GN_EPS = 64e-5
M_NORM_EPS = 1e-6
TM = 512
USE_R = False


class Res:
    __slots__ = ("lw", "rd", "excl")

    def __init__(self, excl=False):
        self.lw = None
        self.rd = {}
        self.excl = excl


class Tile:
    def __init__(self, t):
        self.t = t
        self.r = Res()

    def __getitem__(self, k):
        return self.t[k]


def _base_part(ap):
    bp = ap.base_partition
    return bp() if callable(bp) else bp


def _res(x):
    return x.r if isinstance(x, Tile) else x


class Prog:
    ENG = ("pe", "act", "dve", "pool", "sp")
    SAME_WIN = 3

    def __init__(self, nc, tag):
        self.nc = nc
        self.tag = tag
        self.ops = {e: [] for e in self.ENG}
        self.cnt = {}
        self.clock = {e: {} for e in self.ENG}
        self.snap = {}
        self.sems = {}
        self._pe_free = False
        for e in self.ENG:
            self._mksem(e)

    def _mksem(self, key):
        self.sems[key] = self.nc.alloc_semaphore("s%s_%s" % (self.tag, str(key).replace(" ", "")))
        self.cnt[key] = 0

    def _need(self, eng, ev, waits):
        if ev is None:
            return
        key, val = ev
        if key == eng:
            if eng == "pe" and self._pe_free:
                return
            if self.cnt[eng] + 1 - val <= self.SAME_WIN:
                waits[key] = max(waits.get(key, 0), val)
            return
        if self.clock[eng].get(key, 0) >= val:
            return
        waits[key] = max(waits.get(key, 0), val)

    def _absorb(self, eng, waits):
        ck = self.clock[eng]
        for key, val in waits.items():
            if key == eng:
                continue
            if ck.get(key, 0) < val:
                ck[key] = val
            sn = self.snap.get((key, val))
            if sn:
                for k2, v2 in sn.items():
                    if k2 != eng and ck.get(k2, 0) < v2:
                        ck[k2] = v2

    def _deps(self, eng, reads, writes):
        waits = {}
        for r in reads:
            self._need(eng, _res(r).lw, waits)
        for w in writes:
            w = _res(w)
            self._need(eng, w.lw, waits)
            for k, v in w.rd.items():
                self._need(eng, (k, v), waits)
        return waits

    def op(self, eng, fn, reads=(), writes=()):
        ex = [r for r in reads if _res(r).excl]
        if ex:
            reads = [r for r in reads if not _res(r).excl]
            writes = list(writes) + ex
        waits = self._deps(eng, reads, writes)
        self._absorb(eng, waits)
        self.cnt[eng] += 1
        val = self.cnt[eng]
        self.ops[eng].append((tuple(waits.items()), fn, eng, 1))
        self.snap[(eng, val)] = dict(self.clock[eng])
        for r in reads:
            _res(r).rd[eng] = val
        for w in writes:
            w = _res(w)
            w.lw = (eng, val)
            w.rd = {}

    def dma(self, q, fn, semkey, reads=(), writes=()):
        if semkey not in self.sems:
            self._mksem(semkey)
        waits = self._deps(q, reads, writes)
        if self.cnt[semkey] > 0:
            self._need(q, (semkey, self.cnt[semkey]), waits)
        self._absorb(q, waits)
        self.cnt[semkey] += 16
        val = self.cnt[semkey]
        self.ops[q].append((tuple(waits.items()), fn, semkey, 16))
        self.snap[(semkey, val)] = dict(self.clock[q])
        for r in reads:
            _res(r).rd[semkey] = val
        for w in writes:
            w = _res(w)
            w.lw = (semkey, val)
            w.rd = {}

    def finish(self):
        for e in self.ENG:
            waits = {k: v for k, v in self.cnt.items() if v > 0 and k != e}
            self.ops[e].append((tuple(waits.items()), None, None, 0))

    def emit(self):
        nc = self.nc
        waited = {e: set() for e in self.ENG}
        for e in self.ENG:
            for waits, fn, semkey, inc in self.ops[e]:
                for k, v in waits:
                    if k in waited:
                        waited[k].add(v)
        rank = {e: {v: i + 1 for i, v in enumerate(sorted(waited[e]))} for e in self.ENG}
        with nc.Block() as block:
            def run(e, engobj):
                ci = 0
                for waits, fn, semkey, inc in self.ops[e]:
                    for k, v in waits:
                        engobj.wait_ge(self.sems[k], rank[k][v] if k in rank else v)
                    if fn is None:
                        continue
                    if semkey == e:
                        ci += 1
                        if ci in rank[e]:
                            fn(engobj).then_inc(self.sems[e], 1)
                        else:
                            fn(engobj)
                    else:
                        fn(engobj).then_inc(self.sems[semkey], inc)

            @block.tensor
            def _(eng):
                run("pe", eng)

            @block.scalar
            def _(eng):
                run("act", eng)

            @block.vector
            def _(eng):
                run("dve", eng)

            @block.gpsimd
            def _(eng):
                run("pool", eng)

            @block.sync
            def _(eng):
                run("sp", eng)

    def mm(self, out, lhsT, rhs, start, stop, reads, writes, r=False, free=True):
        self._pe_free = free
        try:
            self._mm(out, lhsT, rhs, start, stop, reads, writes, r)
        finally:
            self._pe_free = False

    def _mm(self, out, lhsT, rhs, start, stop, reads, writes, r=False):
        if r and USE_R and _base_part(out) == 0:
            lhsT = lhsT.bitcast(F32R)
            rhs = rhs.bitcast(F32R)
        elif r:
            lhsT = lhsT.bitcast(F32)
            rhs = rhs.bitcast(F32)
        self.op("pe", lambda e: e.matmul(out, lhsT=lhsT, rhs=rhs, start=start, stop=stop), reads, writes)

    def act(self, out, in_, func, reads, writes, bias=None, scale=None):
        kw = {}
        if bias is not None:
            kw["bias"] = bias
        if scale is not None:
            kw["scale"] = scale
        self.op("act", lambda e: e.activation(out=out, in_=in_, func=func, **kw), reads, writes)

    def tt(self, out, in0, in1, op, reads, writes, eng="dve"):
        self.op(eng, lambda e: e.tensor_tensor(out=out, in0=in0, in1=in1, op=op), reads, writes)

    def ts(self, out, in0, s1, op0, reads, writes, s2=None, op1=None, eng="dve"):
        if op1 is None:
            self.op(eng, lambda e: e.tensor_scalar(out=out, in0=in0, scalar1=s1, scalar2=None, op0=op0), reads, writes)
        else:
            self.op(eng, lambda e: e.tensor_scalar(out=out, in0=in0, scalar1=s1, scalar2=s2, op0=op0, op1=op1), reads, writes)

    def stt(self, out, in0, scalar, in1, op0, op1, reads, writes, eng="dve"):
        self.op(eng, lambda e: e.scalar_tensor_tensor(out=out, in0=in0, scalar=scalar, in1=in1, op0=op0, op1=op1), reads, writes)

    def copy(self, out, in_, reads, writes, eng="dve"):
        self.op(eng, lambda e: e.tensor_copy(out=out, in_=in_), reads, writes)

    def recip(self, out, in_, reads, writes):
        self.op("dve", lambda e: e.reciprocal(out=out, in_=in_), reads, writes)

    def memset(self, out, val, writes, eng="dve"):
        self.op(eng, lambda e: e.memset(out, val), (), writes)

    def rsum(self, out, in_, reads, writes):
        self.op("dve", lambda e: e.reduce_sum(out=out, in_=in_, axis=AX.X), reads, writes)

    def rmax(self, out, in_, reads, writes):
        self.op("dve", lambda e: e.reduce_max(out=out, in_=in_, axis=AX.X), reads, writes)


def _chunks():
    ch = []
    for hp in range(8):
        ch.append(("r%d" % hp, 0 + hp * 128, 128))
        ch.append(("k%d" % hp, 1024 + hp * 128, 128))
        ch.append(("v%d" % hp, 2048 + hp * 128, 128))
    ch.append(("L0", 3072, 128))
    ch.append(("L1", 3200, 128))
    ch.append(("L2", 3328, 32))
    for hp in range(4):
        ch.append(("mq%d" % hp, 3360 + hp * 128, 128))
        ch.append(("mk%d" % hp, 3872 + hp * 128, 128))
    for h in range(8):
        ch.append(("mv%d" % h, 4384 + h * 128, 128))
        ch.append(("mo%d" % h, 5424 + h * 128, 128))
    ch.append(("mg", 5408, 16))
    for j in range(16):
        ch.append(("gr%d" % j, 6448 + j * 128, 128))
        ch.append(("gm%d" % j, 8496 + j * 128, 128))
    return ch


CHUNKS = _chunks()
CIDX = {c[0]: i for i, c in enumerate(CHUNKS)}
NCH = len(CHUNKS)


def build(T, NPH2, debug=False, parts=("lora", "rwkv", "mlstm", "ph2"), lvl=9):
    NMT = T // TM
    PH2_0 = T - NPH2
    nc = bass.Bass("TRN2", target_bir_lowering=False)

    def din(name, shape):
        return nc.dram_tensor(name, list(shape), F32, kind="ExternalInput").ap()

    xT_d = din("xT", [128, KC, T])
    pT_d = din("pT", [128, 2, NPH2])
    valid_d = din("valid", [128, T // 128])
    Wp_d = din("Wp", [NCH, 128, 2048])
    W2A_d = din("W2A", [128, 1024])
    G2a_d = din("G2a", [128, 1024])
    G2b_d = din("G2b", [32, 1024])
    vecR_d = din("vecR", [128, 80])
    vecL_d = din("vecL", [128, 3])
    vecM_d = din("vecM", [128, 40])
    mhw_d = din("mhw", [128, 8])
    gb_d = din("gb", [128, 16])
    bgate_d = din("bgate", [128, 32])
    WBR_d = din("WBR", [16, 128, 1024])
    WBM_d = din("WBM", [16, 128, 1024])
    WOUT_d = din("WOUT", [16, 128, 2048])
    WPG_d = din("WPG", [16, 128, 2048])
    WPLE_d = din("WPLE", [16, 128, 256])
    WG_d = din("WG", [32, 4, 128, 2048])
    WU_d = din("WU", [32, 4, 128, 2048])
    WD_d = din("WD", [32, 4, 128, 2048])
    WR_d = din("WR", [128, KC * 36])
    rb_d = din("rb", [128, 36])
    lnp_d = din("lnp", [128, 64])
    cst_d = din("cst", [128, 128 * 5])
    mskA_d = din("mskA", [128, 192])
    mskB_d = din("mskB", [128, 192])
    mskC_d = din("mskC", [128, 128])
    ifull_d = din("ifull", [128, 512])
    sele_d = din("sele", [32, 32 * 128])
    outT_d = nc.dram_tensor("outT", [128, KC, NPH2], F32, kind="ExternalOutput").ap()
    if debug:
        dbg_d = nc.dram_tensor("dbg", [128, 2 * 8 * NPH2], F32, kind="ExternalOutput").ap()

    def sb(name, shape, dt=F32):
        return Tile(nc.alloc_sbuf_tensor("sb_" + name, list(shape), dt))

    PS = nc.alloc_psum_tensor("PS", [128, 4096], F32)
    PSR = [Res(excl=True) for _ in range(8)]
    yrT = sb("yrT", [128, 8, NPH2], BF16)
    ymT = sb("ymT", [128, 8, NPH2], BF16)
    cst = sb("cst", [128, 640])
    ident = cst[:, 0:128]
    bones = cst[:, 128:256]
    tri = cst[:, 256:384]
    ones = cst[:, 384:512]
    NW = 4
    wslot = [sb("wslot%d" % i, [128, 2048], BF16) for i in range(NW)]
    xb = sb("xb", [128, KC, TM], BF16)

    state = {"ps": 0, "w": 0}

    def psum(nb=1):
        i = state["ps"]
        if i + nb > 8:
            i = 0
        state["ps"] = (i + nb) % 8
        return i, PS[:, i * 512:(i + nb) * 512], PSR[i:i + nb]

    def wload(P, src, ncols=2048):
        i = state["w"]
        state["w"] = (i + 1) % NW
        t = wslot[i]
        P.dma("pool", lambda e: e.dma_start(out=t[:, 0:ncols], in_=src), ("w", i), writes=[t])
        return t

    def cload(P, tile, src, q="sp", key="c"):
        P.dma(q, lambda e: e.dma_start(out=tile[:], in_=src), key, writes=[tile])

    with ExitStack() as es:
        def sbt(name, shape, dt=F32):
            return Tile(es.enter_context(nc.sbuf_tensor("sb_" + name, list(shape), dt)))

        P = Prog(nc, "a")
        cload(P, cst, cst_d)
        W2A = sbt("W2A", [128, 1024], BF16)
        G2a = sbt("G2a", [128, 1024], BF16)
        G2b = sbt("G2b", [32, 1024], BF16)
        cload(P, W2A, W2A_d, "pool", "c2")
        cload(P, G2a, G2a_d, "pool", "c2")
        cload(P, G2b, G2b_d, "pool", "c2")
        vecR = sbt("vecR", [128, 80]); cload(P, vecR, vecR_d)
        vecL = sbt("vecL", [128, 3]); cload(P, vecL, vecL_d)
        vecM = sbt("vecM", [128, 40]); cload(P, vecM, vecM_d)
        mhw = sbt("mhw", [128, 8]); cload(P, mhw, mhw_d)
        gb = sbt("gb", [128, 16]); cload(P, gb, gb_d)
        valid = sbt("valid", [128, T // 128]); cload(P, valid, valid_d)
        mskA = sbt("mskA", [128, 192]); cload(P, mskA, mskA_d)
        mskB = sbt("mskB", [128, 192]); cload(P, mskB, mskB_d)
        mskC = sbt("mskC", [128, 128]); cload(P, mskC, mskC_d)
        ifull = sbt("ifull", [128, 512]); cload(P, ifull, ifull_d)
        cstR = sbt("cstR", [128, 512], F32R)
        P.copy(cstR[:], cst[:, 0:512], [cst], [cstR])
        identR = cstR[:, 0:128]
        bonesR = cstR[:, 128:256]
        triR = cstR[:, 256:384]
        onesR = cstR[:, 384:512]
        scanm = sbt("scanm", [128, 512])
        P.memset(scanm[:], 1.0, [scanm])
        P.memset(scanm[:].rearrange("p (c l) -> p c l", l=64)[:, :, 0:1], 0.0, [scanm])

        ST = sbt("ST", [128, 8, 64], F32R)
        P.memset(ST[:].bitcast(F32), 0.0, [ST])
        CS = sbt("CS", [128, 4, 129], F32R)
        P.memset(CS[:].bitcast(F32), 0.0, [CS])
        carry = sbt("carry", [128, 32])
        P.memset(carry[:], 0.0, [carry])
        ccarry = sbt("ccarry", [128, 8, 3])
        P.memset(ccarry[:], 0.0, [ccarry])

        NSC = 15
        SC = [sbt("sc%d" % i, [128, 516]) for i in range(NSC)]
        epsR = sbt("epsR", [128, 1]); P.memset(epsR[:], R_GN_EPS, [epsR])
        epsM = sbt("epsM", [128, 1]); P.memset(epsM[:], M_NORM_EPS, [epsM])
        oneC = sbt("oneC", [128, 1]); P.memset(oneC[:], 1.0, [oneC])
        TL = sbt("TL", [128, TM], BF16)
        SG0 = sbt("SG0", [128, TM], BF16)
        SG1 = sbt("SG1", [32, TM], BF16)
        R3 = sbt("R3", [128, 8, 192], F32R)
        K2 = sbt("K2", [128, 8, 128], F32R)
        EA = sbt("EA", [128, 4, 2, 192], F32R)
        EB = sbt("EB", [128, 4, 2, 192], F32R)
        EC = sbt("EC", [128, 4, 2, 128], F32R)
        XA = sbt("XA", [128, 4, 2, 64], F32R); XTA = sbt("XTA", [128, 4, 2, 64], F32R)
        XB = sbt("XB", [128, 4, 2, 64], F32R); XTB = sbt("XTB", [128, 4, 2, 64], F32R)
        PM = sbt("PM", [128, 4, 2, 64], F32R)
        VmT = sbt("VmT", [128, 4, 2, 64], F32R)
        RT = sbt("RT", [128, 2, 64], F32R); UT = sbt("UT", [128, 2, 64], F32R)
        P.memset(R3[:].bitcast(F32), 0.0, [R3])
        for c in range(8):
            P.copy(R3[0:64, c, 128:192], ident[0:64, 0:64], [cst], [R3])
            P.copy(R3[64:128, c, 128:192], ident[64:128, 64:128], [cst], [R3])
        GI = sbt("GI", [128, 16]); EL = sbt("EL", [128, 8]); LL = sbt("LL", [128, 8], F32R)
        EAc = sbt("EAc", [128, 4, 8]); EKc = sbt("EKc", [128, 4, 8]); EALc = sbt("EALc", [128, 4, 8])
        tm8 = sbt("tm8", [128, 8])
        VP = sbt("VP", [128, 4, 129], F32R)
        Gm = sbt("Gm", [128, 128], F32R); kTk = sbt("kTk", [128, 64], F32R); hh = sbt("hh", [128, 128]); hn = sbt("hn", [128, 128], F32R)
        hsq = sbt("hsq", [128, 128])
        sm = sbt("sm", [128, 16])

        def inproj(cname, ncols=128, n0=0, nn=TM):
            w = wload(P, Wp_d[CIDX[cname]])
            wv = w[:].rearrange("p (k m) -> p k m", m=128)
            pi, pap, pr = psum()
            for kc in range(KC):
                P.mm(pap[0:ncols, 0:nn], wv[:, kc, 0:ncols], xb[:, kc, n0:n0 + nn], kc == 0, kc == KC - 1, [w, xb], pr)
            return pap, pr

        def shifted(cname, ci, mu_ap, zt, out_t, ncols=128, rnd=False):
            pap, pr = inproj(cname, ncols)
            P.copy(zt[0:ncols, 0:1], carry[0:ncols, ci:ci + 1], [carry], [zt])
            P.act(zt[0:ncols, 1:TM + 1], pap[0:ncols, 0:TM], AF.Copy, pr, [zt])
            P.copy(carry[0:ncols, ci:ci + 1], zt[0:ncols, TM:TM + 1], [zt], [carry])
            P.tt(out_t[0:ncols, 0:TM], zt[0:ncols, 0:TM], zt[0:ncols, 1:TM + 1], ALU.subtract, [zt], [out_t])
            oo = out_t[0:ncols, 0:TM].bitcast(F32R) if rnd else out_t[0:ncols, 0:TM]
            P.stt(oo, out_t[0:ncols, 0:TM], mu_ap, zt[0:ncols, 1:TM + 1], ALU.mult, ALU.add, [out_t, zt, vecR, vecL], [out_t])

        def bmm(in_ap, in_t):
            pi, pap, pr = psum()
            P.mm(pap[:, 0:TM], bones, in_ap, True, True, [cst, in_t], pr)
            return pap, pr

        for mt in range(NMT):
            t0 = mt * TM
            P.dma("pool", lambda e, t0=t0: e.dma_start(out=xb[:], in_=xT_d[:, :, t0:t0 + TM]), "xb", writes=[xb])
            inph2 = t0 >= PH2_0
            need_carry = (t0 + TM >= PH2_0)
            q0 = t0 - PH2_0
            z, o = SC[0], SC[1]
            shifted("L0", 24, vecL[:, 0:1], z, o)
            P.act(TL[0:64, :], o[0:64, 0:TM], AF.Tanh, [o], [TL])
            P.copy(TL[64:128, :], o[64:128, 0:TM], [o], [TL])
            if need_carry:
                shifted("L1", 25, vecL[:, 1:2], z, o)
                P.act(SG0[:, :], o[:, 0:TM], AF.Sigmoid, [o], [SG0])
                shifted("L2", 26, vecL[0:32, 2:3], z, o, ncols=32)
                P.act(SG1[:, :], o[0:32, 0:TM], AF.Sigmoid, [o], [SG1])
            for hp in (range(8) if "rwkv" in parts else []):
                cs = slice(hp * 128, (hp + 1) * 128)
                vr = lambda i: vecR[:, hp * 10 + i:hp * 10 + i + 1]
                rs, ks, vs = SC[2], SC[3], SC[4]
                if need_carry:
                    shifted("r%d" % hp, hp * 3 + 0, vr(0), SC[0], rs)
                shifted("k%d" % hp, hp * 3 + 1, vr(1), SC[0], ks)
                shifted("v%d" % hp, hp * 3 + 2, vr(2), SC[0], vs)
                _, pw, pwr = psum()
                P.mm(pw[:, 0:TM], W2A[0:64, cs], TL[0:64, :], True, True, [W2A, TL], pwr)
                _, pa, par_ = psum()
                P.mm(pa[:, 0:TM], W2A[64:128, cs], TL[64:128, :], True, True, [W2A, TL], par_)
                lw, aa, gg = SC[5], SC[6], SC[7]
                if inph2:
                    _, pg, pgr = psum()
                    P.mm(pg[:, 0:TM], G2a[:, cs], SG0[:, :], True, False, [G2a, SG0], pgr)
                    P.mm(pg[:, 0:TM], G2b[:, cs], SG1[:, :], False, True, [G2b, SG1], pgr)
                    P.act(gg[:, 0:TM], pg[:, 0:TM], AF.Copy, pgr, [gg])
                P.act(lw[:, 0:TM], pw[:, 0:TM], AF.Sigmoid, pwr + [vecR], [lw], bias=vr(3))
                P.ts(lw[:, 0:TM], lw[:, 0:TM], -float(np.exp(-0.5)), ALU.mult, [lw], [lw])
                P.act(aa[:, 0:TM], pa[:, 0:TM], AF.Sigmoid, par_ + [vecR], [aa], bias=vr(4))
                kk, sq, kap = SC[8], SC[9], SC[10]
                P.ts(kk[:, 0:TM], ks[:, 0:TM], vr(5), ALU.mult, [ks, vecR], [kk])
                P.tt(sq[:, 0:TM], kk[:, 0:TM], kk[:, 0:TM], ALU.mult, [kk], [sq])
                pss, pssr = bmm(sq[:, 0:TM], sq)
                P.act(sq[:, 0:TM], pss[:, 0:TM], AF.Sqrt, pssr, [sq])
                P.ts(sq[:, 0:TM], sq[:, 0:TM], 1e-12, ALU.max, [sq], [sq])
                P.recip(sq[:, 0:TM], sq[:, 0:TM], [sq], [sq])
                P.tt(kap[:, 0:TM], kk[:, 0:TM], sq[:, 0:TM], ALU.mult, [kk, sq], [kap])
                km, beta = SC[11], SC[12]
                P.ts(km[:, 0:TM], aa[:, 0:TM], -1.0, ALU.add, [aa, vecR], [km], s2=vr(6), op1=ALU.mult)
                P.stt(km[:, 0:TM], km[:, 0:TM], 1.0, ks[:, 0:TM], ALU.add, ALU.mult, [km, ks], [km])
                P.tt(beta[:, 0:TM], aa[:, 0:TM], kap[:, 0:TM], ALU.mult, [aa, kap], [beta])
                bon = SC[13]
                if inph2:
                    P.stt(bon[:, 0:TM], rs[:, 0:TM], vr(7), km[:, 0:TM], ALU.mult, ALU.mult, [rs, km, vecR], [bon])
                    pb, pbr = bmm(bon[:, 0:TM], bon)
                    P.tt(bon[:, 0:TM], pb[:, 0:TM], vs[:, 0:TM], ALU.mult, pbr + [vs], [bon])
                cc, ep, en, epv = SC[8], SC[14], SC[6], SC[9]
                P.op("dve", lambda e, cc=cc, lw=lw: e.tensor_tensor_scan(out=cc[:, 0:TM], data0=scanm[:, 0:TM], data1=lw[:, 0:TM],
                                                                         initial=0.0, op0=ALU.mult, op1=ALU.add), [scanm, lw], [cc])
                P.act(ep[:, 0:TM], cc[:, 0:TM], AF.Exp, [cc], [ep])
                P.act(en[:, 0:TM], cc[:, 0:TM], AF.Exp, [cc], [en], scale=-1.0)
                P.tt(epv[:, 0:TM], cc[:, 0:TM], lw[:, 0:TM], ALU.subtract, [cc, lw], [epv])
                P.act(epv[:, 0:TM], epv[:, 0:TM], AF.Exp, [epv], [epv])
                c3 = lambda t_: t_[:, 0:TM].rearrange("p (c l) -> p c l", l=64)
                P.tt(R3[:, :, 0:64], c3(kap), c3(epv), ALU.mult, [kap, epv], [R3])
                if inph2:
                    P.tt(R3[:, :, 64:128], c3(rs), c3(ep), ALU.mult, [rs, ep], [R3])
                P.tt(K2[:, :, 0:64], c3(km), c3(en), ALU.mult, [km, en], [K2])
                P.tt(K2[:, :, 64:128], c3(beta), c3(en), ALU.mult, [beta, en], [K2])
                for dc in range(4):
                    _, pt, ptr = psum()
                    P.mm(pt[:, 0:128], vs[:, dc * 128:(dc + 1) * 128], ident, True, True, [vs, cst], ptr)
                    P.copy(VmT[:, dc, :, :], pt[:, 0:128].rearrange("p (h v) -> p h v", v=64), ptr, [VmT])
                hr = lambda h: slice(h * 64, (h + 1) * 64)
                pr_ = lambda c: slice((c % 2) * 64, (c % 2) * 64 + 64)
                _, pA, pAr = psum(4)
                pAv = pA.rearrange("p (d h w) -> p d h w", d=4, h=2)
                for c in range(8):
                    for h in range(2):
                        P.mm(pAv[pr_(c), c // 2, h, 0:192], K2[hr(h), c, 0:64], R3[hr(h), c, 0:192], True, True, [K2, R3], pAr, r=True, free=False)
                for d_ in range(4):
                    for h in range(2):
                        P.tt(EA[:, d_, h, :], pAv[:, d_, h, 0:192], mskA[:], ALU.mult, pAr + [mskA], [EA])
                _, pB, pBr = psum(4)
                pBv = pB.rearrange("p (d h w) -> p d h w", d=4, h=2)
                for c in range(8):
                    for h in range(2):
                        P.mm(pBv[pr_(c), c // 2, h, 0:192], K2[hr(h), c, 64:128], R3[hr(h), c, 0:192], True, True, [K2, R3], pBr, r=True, free=False)
                for d_ in range(4):
                    for h in range(2):
                        P.tt(EB[:, d_, h, :], pBv[:, d_, h, 0:192], mskB[:], ALU.mult, pBr + [mskB], [EB])
                _, pC, pCr = psum(2)
                pCv = pC.rearrange("p (d h w) -> p d h w", d=4, h=2)
                for c in range(8):
                    for h in range(2):
                        P.mm(pCv[pr_(c), c // 2, h, 0:128], R3[hr(h), c, 0:64], K2[hr(h), c, 0:128], True, True, [K2, R3], pCr, r=True, free=False)
                for d_ in range(4):
                    for h in range(2):
                        P.tt(EC[:, d_, h, :], pCv[:, d_, h, :], mskC[:], ALU.mult, pCr + [mskC], [EC])
                X = (EB, lambda c, h: EB[pr_(c), c // 2, h, 0:64])
                XT = (EC, lambda c, h: EC[pr_(c), c // 2, h, 64:128])
                P.tt(PM[:], EB[:, :, :, 0:64], ifull[:].rearrange("p (d h w) -> p d h w", d=4, h=2), ALU.add, [EB, ifull], [PM])
                bufs = [(XA, XTA), (XB, XTB)]
                for it in range(5):
                    nX, nXT = bufs[it % 2]
                    last = it == 4
                    _, p2, p2r = psum()
                    p2v = p2.rearrange("p (d h w) -> p d h w", d=4, h=2)
                    for c in range(8):
                        for h in range(2):
                            P.mm(p2v[pr_(c), c // 2, h, :], X[1](c, h), XT[1](c, h), True, True, [X[0], XT[0]], p2r, r=True)
                    P.act(nXT[:].rearrange("p d h w -> p (d h w)"), p2, AF.Copy, p2r, [nXT])
                    if not last:
                        _, p1, p1r = psum()
                        p1v = p1.rearrange("p (d h w) -> p d h w", d=4, h=2)
                        for c in range(8):
                            for h in range(2):
                                P.mm(p1v[pr_(c), c // 2, h, :], XT[1](c, h), X[1](c, h), True, True, [X[0], XT[0]], p1r, r=True)
                        P.act(nX[:].rearrange("p d h w -> p (d h w)"), p1, AF.Copy, p1r, [nX])
                    _, p3, p3r = psum()
                    p3v = p3.rearrange("p (d h w) -> p d h w", d=4, h=2)
                    for c in range(8):
                        for h in range(2):
                            P.mm(p3v[pr_(c), c // 2, h, :], nXT[pr_(c), c // 2, h, :], PM[pr_(c), c // 2, h, :], True, True, [nXT, PM], p3r, r=True)
                    P.tt(PM[:].rearrange("p d h w -> p (d h w)"), PM[:].rearrange("p d h w -> p (d h w)"), p3, ALU.add, p3r + [PM], [PM])
                    X = (nX, lambda c, h, nX=nX: nX[pr_(c), c // 2, h, :])
                    XT = (nXT, lambda c, h, nXT=nXT: nXT[pr_(c), c // 2, h, :])
                yb = SC[3]
                for c in range(8):
                    rows = pr_(c)
                    dc = c // 2
                    _, p1, p1r = psum()
                    for h in range(2):
                        P.mm(p1[rows, h * 64:(h + 1) * 64], R3[hr(h), c, 0:64], ST[hr(h), hp, :], True, False, [R3, ST], p1r, r=True, free=False)
                        P.mm(p1[rows, h * 64:(h + 1) * 64], EA[rows, dc, h, 0:64], VmT[rows, dc, h, :], False, True, [EA, VmT], p1r, r=True, free=False)
                    P.act(RT[rows, :, :], p1[rows, 0:128].rearrange("p (h v) -> p h v", v=64), AF.Copy, p1r, [RT])
                    _, p2, p2r = psum()
                    for h in range(2):
                        P.mm(p2[rows, h * 64:(h + 1) * 64], PM[rows, dc, h, :], RT[rows, h, :], True, True, [PM, RT], p2r, r=True, free=False)
                    P.copy(UT[rows, :, :], p2[rows, 0:128].rearrange("p (h v) -> p h v", v=64), p2r, [UT])
                    if inph2:
                        _, pY, pYr = psum()
                    _, pS, pSr = psum()
                    for h in range(2):
                        if inph2:
                            P.mm(pY[hr(h), 0:64], ST[hr(h), hp, :], R3[hr(h), c, 64:128], True, False, [ST, R3], pYr, r=True, free=False)
                            P.mm(pY[hr(h), 0:64], VmT[rows, dc, h, :], EA[rows, dc, h, 64:128], False, False, [VmT, EA], pYr, r=True, free=False)
                            P.mm(pY[hr(h), 0:64], UT[rows, h, :], EB[rows, dc, h, 64:128], False, True, [UT, EB], pYr, r=True, free=False)
                        idh = ident[hr(h), hr(h)]
                        P.mm(pS[hr(h), 0:64], idh, ST[hr(h), hp, :].bitcast(F32), True, False, [cst, ST], pSr, free=False)
                        P.mm(pS[hr(h), 0:64], EA[rows, dc, h, 128:192], VmT[rows, dc, h, :], False, False, [EA, VmT], pSr, r=True, free=False)
                        P.mm(pS[hr(h), 0:64], EB[rows, dc, h, 128:192], UT[rows, h, :], False, True, [EB, UT], pSr, r=True, free=False)
                    if inph2:
                        P.act(yb[:, c * 64:(c + 1) * 64], pY[:, 0:64], AF.Copy, pYr, [yb])
                    P.ts(ST[:, hp, :], pS[:, 0:64], ep[:, c * 64 + 63:c * 64 + 64], ALU.mult, pSr + [ep], [ST])
                if inph2:
                    pm, pmr = bmm(yb[:, 0:TM], yb)
                    mean, dd, var = SC[10], SC[11], SC[12]
                    P.act(mean[:, 0:TM], pm[:, 0:TM], AF.Copy, pmr, [mean], scale=1.0 / 64)
                    P.tt(dd[:, 0:TM], yb[:, 0:TM], mean[:, 0:TM], ALU.subtract, [yb, mean], [dd])
                    P.tt(var[:, 0:TM], dd[:, 0:TM], dd[:, 0:TM], ALU.mult, [dd], [var])
                    pq, pqr = bmm(var[:, 0:TM], var)
                    P.act(var[:, 0:TM], pq[:, 0:TM], AF.Sqrt, pqr + [epsR], [var], scale=1.0 / 64, bias=epsR[:, 0:1])
                    P.recip(var[:, 0:TM], var[:, 0:TM], [var], [var])
                    P.tt(dd[:, 0:TM], dd[:, 0:TM], var[:, 0:TM], ALU.mult, [dd, var], [dd])
                    P.act(dd[:, 0:TM], dd[:, 0:TM], AF.Identity, [dd, vecR], [dd], scale=vr(8), bias=vr(9))
                    P.tt(dd[:, 0:TM], dd[:, 0:TM], bon[:, 0:TM], ALU.add, [dd, bon], [dd])
                    P.tt(yrT[:, hp, q0:q0 + TM], dd[:, 0:TM], gg[:, 0:TM], ALU.mult, [dd, gg], [yrT])
            wmg = wload(P, Wp_d[CIDX["mg"]])
            wmgv = wmg[:].rearrange("p (k m) -> p k m", m=128)
            for ck in (range(4) if "mlstm" in parts else []):
                _, pgt, pgtr = psum()
                for kc in range(KC):
                    P.mm(pgt[:, 0:16], xb[:, kc, ck * 128:(ck + 1) * 128], wmgv[:, kc, 0:16], kc == 0, kc == KC - 1, [xb, wmg], pgtr)
                P.tt(GI[:], pgt[:, 0:16], gb[:], ALU.add, pgtr + [gb], [GI])
                P.act(EL[:], GI[:, 8:16], AF.Exp, [GI], [EL], scale=-1.0)
                P.act(LL[:], EL[:], AF.Ln, [EL, oneC], [LL], bias=oneC[:, 0:1])
                _, pc, pcr = psum()
                P.mm(pc[:, 0:8], triR, LL[:], True, True, [cstR, LL], pcr, r=True)
                P.mm(pc[:, 8:16], onesR, LL[:], True, True, [cstR, LL], pcr, r=True)
                P.act(EAc[:, ck, :], pc[:, 0:8], AF.Exp, pcr, [EAc], scale=-1.0)
                P.tt(tm8[:], GI[:, 0:8], pc[:, 0:8], ALU.add, pcr + [GI], [tm8])
                P.act(EKc[:, ck, :], tm8[:], AF.Exp, [tm8], [EKc])
                P.act(tm8[:], pc[:, 8:16], AF.Exp, pcr, [tm8], scale=-1.0)
                gck = mt * 4 + ck
                P.ts(EALc[:, ck, :], tm8[:], valid[:, gck:gck + 1], ALU.mult, [tm8, valid], [EALc])
            for hp2 in (range(4) if "mlstm" in parts else []):
                qf, kf = SC[2], SC[3]
                for which, dst in (("mq", qf), ("mk", kf)):
                    if which == "mq" and not need_carry:
                        continue
                    ci = hp2 * 2 + (0 if which == "mq" else 1)
                    vm = lambda i: vecM[:, ci * 5 + i:ci * 5 + i + 1]
                    pap, pr = inproj("%s%d" % (which, hp2))
                    zc = SC[0]
                    P.copy(zc[:, 0:3], ccarry[:, ci, :], [ccarry], [zc])
                    P.act(zc[:, 3:TM + 3], pap[:, 0:TM], AF.Copy, pr, [zc])
                    P.copy(ccarry[:, ci, :], zc[:, TM:TM + 3], [zc], [ccarry])
                    acc = SC[1]
                    P.ts(acc[:, 0:TM], zc[:, 0:TM], vm(0), ALU.mult, [zc, vecM], [acc], s2=vm(4), op1=ALU.add)
                    for j in range(1, 4):
                        P.stt(acc[:, 0:TM], zc[:, j:j + TM], vm(j), acc[:, 0:TM], ALU.mult, ALU.add, [zc, acc, vecM], [acc])
                    if which == "mk":
                        P.act(dst[:, 0:TM], acc[:, 0:TM], AF.Silu, [acc], [dst])
                        P.ts(dst[:, 0:TM], dst[:, 0:TM], 0.125, ALU.mult, [dst], [dst])
                    else:
                        P.act(dst[:, 0:TM], acc[:, 0:TM], AF.Silu, [acc], [dst])
                for hh_ in range(2):
                    h = hp2 * 2 + hh_
                    hrows = slice(hh_ * 64, hh_ * 64 + 64)
                    so = SC[4]
                    if inph2:
                        pap, pr = inproj("mo%d" % h)
                        P.act(so[:, 0:TM], pap[:, 0:TM], AF.Sigmoid, pr, [so])
                    wv_ = wload(P, Wp_d[CIDX["mv%d" % h]])
                    wvv = wv_[:].rearrange("p (k m) -> p k m", m=128)
                    for ck in range(4):
                        _, pv, pvr = psum()
                        for kc in range(KC):
                            P.mm(pv[:, 0:128], xb[:, kc, ck * 128:(ck + 1) * 128], wvv[:, kc, :], kc == 0, kc == KC - 1, [xb, wv_], pvr)
                        P.act(VP[:, ck, 0:128], pv[:, 0:128], AF.Copy, pvr, [VP])
                    P.memset(VP[:, :, 128:129].bitcast(F32), 1.0, [VP])
                    for ck in range(4):
                        tk = slice(ck * 128, (ck + 1) * 128)
                        if inph2:
                            _, pG, pGr = psum()
                            P.mm(pG[:, 0:128], kf[hrows, tk], qf[hrows, tk], True, True, [kf, qf], pGr)
                            P.stt(Gm[:], pG[:, 0:128], EKc[:, ck, h:h + 1], tri, ALU.mult, ALU.mult, pGr + [EKc, cst], [Gm])
                        _, pK, pKr = psum()
                        P.mm(pK[:, 0:64], kf[hrows, tk], ident[hrows, hrows], True, True, [kf, cst], pKr)
                        P.ts(kTk[:], pK[:, 0:64], EKc[:, ck, h:h + 1], ALU.mult, pKr + [EKc], [kTk])
                        if inph2:
                            _, pN, pNr = psum()
                            P.mm(pN[:, 0:129], Gm[:], VP[:, ck, :], True, False, [Gm, VP], pNr, r=True)
                            P.mm(pN[:, 0:129], qf[hrows, tk], CS[hrows, hp2, :].bitcast(F32), False, True, [qf, CS], pNr)
                        _, pS, pSr = psum()
                        P.mm(pS[hrows, 0:129], ident[hrows, hrows], CS[hrows, hp2, :].bitcast(F32), True, False, [cst, CS], pSr)
                        P.mm(pS[hrows, 0:129], kTk[:], VP[:, ck, :], False, True, [kTk, VP], pSr, r=True)
                        if inph2:
                            P.tt(sm[:, 0:1], pN[:, 128:129], EAc[:, ck, h:h + 1], ALU.mult, pNr + [EAc], [sm])
                            P.act(sm[:, 1:2], sm[:, 0:1], AF.Abs, [sm], [sm])
                            P.ts(sm[:, 1:2], sm[:, 1:2], 1.0, ALU.max, [sm], [sm])
                            P.recip(sm[:, 2:3], sm[:, 1:2], [sm], [sm])
                            P.tt(sm[:, 3:4], sm[:, 2:3], EAc[:, ck, h:h + 1], ALU.mult, [sm, EAc], [sm])
                            P.ts(hh[:], pN[:, 0:128], sm[:, 3:4], ALU.mult, pNr + [sm], [hh])
                            P.rsum(sm[:, 4:5], hh[:], [hh], [sm])
                            P.ts(sm[:, 5:6], sm[:, 4:5], 1.0 / 128, ALU.mult, [sm], [sm])
                            P.ts(hn[:], hh[:], sm[:, 5:6], ALU.subtract, [hh, sm], [hn])
                            P.tt(hsq[:], hn[:], hn[:], ALU.mult, [hn], [hsq])
                            P.rsum(sm[:, 6:7], hsq[:], [hsq], [sm])
                            P.act(sm[:, 7:8], sm[:, 6:7], AF.Sqrt, [sm, epsM], [sm], scale=1.0 / 128, bias=epsM[:, 0:1])
                            P.recip(sm[:, 8:9], sm[:, 7:8], [sm], [sm])
                            P.ts(hn[:], hn[:], sm[:, 8:9], ALU.mult, [hn, sm], [hn])
                            _, pT_, pTr = psum()
                            P.mm(pT_[:, 0:128], hn[:], identR, True, True, [hn, cstR], pTr, r=True)
                            P.stt(ymT[:, h, q0 + ck * 128:q0 + (ck + 1) * 128], pT_[:, 0:128], mhw[:, h:h + 1], so[:, tk],
                                  ALU.mult, ALU.mult, pTr + [mhw, so], [ymT])
                        P.ts(CS[hrows, hp2, :], pS[hrows, 0:129], EALc[hrows, ck, h:h + 1], ALU.mult, pSr + [EALc], [CS])
        if debug:
            dtile = sbt("dtile", [128, NPH2])
            for i, src in enumerate([yrT, ymT]):
                for j in range(8):
                    P.copy(dtile[:], src[:, j, :], [src], [dtile])
                    off = (i * 8 + j) * NPH2
                    P.dma("sp", lambda e, off=off: e.dma_start(out=dbg_d[:, off:off + NPH2], in_=dtile[:]), "dbg", reads=[dtile])
        P.finish()
        P.emit()

    for r_ in PSR + [t_.r for t_ in [yrT, ymT, cst, xb] + wslot]:
        r_.lw = None
        r_.rd = {}

    with ExitStack() as es:
        def sbt(name, shape, dt=F32):
            return Tile(es.enter_context(nc.sbuf_tensor("sb_" + name, list(shape), dt)))

        P = Prog(nc, "b")
        epsL = sbt("epsL", [128, 1]); P.memset(epsL[:], LN_EPS, [epsL])
        bgate = sbt("bgate", [128, 32]); cload(P, bgate, bgate_d)
        lnp = sbt("lnp", [128, 64]); cload(P, lnp, lnp_d)
        WR = sbt("WR", [128, KC, 36]); cload(P, WR, WR_d.rearrange("p (k m) -> p k m", m=36))
        rb = sbt("rb", [128, 36]); cload(P, rb, rb_d)
        sele = sbt("sele", [32, 32, 128]); cload(P, sele, sele_d.rearrange("p (e m) -> p e m", m=128))
        ZB = sbt("ZB", [128, KC, TM])
        mrg = sbt("mrg", [128, KC, TM], BF16)
        x1T = sbt("x1T", [128, KC, TM], BF16)
        pTb = sbt("pTb", [128, 2, TM], BF16)
        hT = sbt("hT", [128, 4, TM], BF16)
        TS = [sbt("ts%d" % i, [128, TM]) for i in range(6)]
        COEFT = sbt("COEFT", [32, TM])
        cbt = sbt("cbt", [128, TM])
        LG = sbt("LG", [128, 36]); R8 = sbt("R8", [128, 40]); LE2 = sbt("LE2", [128, 32]); OH1 = sbt("OH1", [128, 32])
        OH2 = sbt("OH2", [128, 32]); COEF = sbt("COEF", [128, 32]); rs_ = sbt("rs_", [128, 16])

        def inproj2(cname):
            w = wload(P, Wp_d[CIDX[cname]])
            wv = w[:].rearrange("p (k m) -> p k m", m=128)
            _, pap, pr = psum()
            for kc in range(KC):
                P.mm(pap[:, 0:TM], wv[:, kc, :], xb[:, kc, :], kc == 0, kc == KC - 1, [w, xb], pr)
            return pap, pr

        def proj(src_d, nk, rhs_t, rhs_fn):
            w = wload(P, src_d) if nk == 16 else None
            return w

        def layernorm(wcol, bcol):
            _, psu, psur = psum()
            _, psq, psqr = psum()
            for j in range(KC):
                sq = TS[j % 2]
                P.act(sq[:], ZB[:, j, :], AF.Square, [ZB], [sq])
                P.mm(psu[:, 0:TM], ones, ZB[:, j, :], j == 0, j == KC - 1, [cst, ZB], psur)
                P.mm(psq[:, 0:TM], ones, sq[:], j == 0, j == KC - 1, [cst, sq], psqr)
            mean, rstd, msq = TS[2], TS[3], TS[4]
            P.act(mean[:], psu[:, 0:TM], AF.Copy, psur, [mean], scale=1.0 / D)
            P.tt(msq[:], mean[:], mean[:], ALU.mult, [mean], [msq])
            P.stt(rstd[:], psq[:, 0:TM], 1.0 / D, msq[:], ALU.mult, ALU.subtract, psqr + [msq], [rstd])
            P.act(rstd[:], rstd[:], AF.Sqrt, [rstd, epsL], [rstd], bias=epsL[:, 0:1])
            P.recip(rstd[:], rstd[:], [rstd], [rstd])
            for j in range(KC):
                d_ = TS[j % 2]
                P.tt(d_[:], ZB[:, j, :], mean[:], ALU.subtract, [ZB, mean], [d_])
                P.tt(d_[:], d_[:], rstd[:], ALU.mult, [d_, rstd], [d_])
                P.act(ZB[:, j, :], d_[:], AF.Identity, [d_, lnp], [ZB], scale=lnp[:, wcol + j:wcol + j + 1], bias=lnp[:, bcol + j:bcol + j + 1])

        for tt_ in (range(NPH2 // TM) if "ph2" in parts else []):
            q0 = tt_ * TM
            g0 = PH2_0 + q0
            P.dma("pool", lambda e, g0=g0: e.dma_start(out=xb[:], in_=xT_d[:, :, g0:g0 + TM]), "xb", writes=[xb])
            P.dma("sp", lambda e, g0=g0: e.dma_start(out=ZB[:], in_=xT_d[:, :, g0:g0 + TM]), "zb", writes=[ZB])
            P.dma("pool", lambda e, q0=q0: e.dma_start(out=pTb[:], in_=pT_d[:, :, q0:q0 + TM]), "ptb", writes=[pTb])
            for j in (range(KC) if lvl >= 1 else []):
                pgr, pgrr = inproj2("gr%d" % j)
                sgr = TS[0]
                P.act(sgr[:], pgr[:, 0:TM], AF.Sigmoid, pgrr + [bgate], [sgr], bias=bgate[:, j:j + 1])
                pgm, pgmr = inproj2("gm%d" % j)
                sgm = TS[1]
                P.act(sgm[:], pgm[:, 0:TM], AF.Sigmoid, pgmr + [bgate], [sgm], bias=bgate[:, 16 + j:17 + j])
                w = wload(P, WBR_d[j], 1024)
                wv = w[:, 0:1024].rearrange("p (k m) -> p k m", m=128)
                _, ppr, pprr = psum()
                for kc in range(8):
                    P.mm(ppr[:, 0:TM], wv[:, kc, :], yrT[:, kc, q0:q0 + TM], kc == 0, kc == 7, [w, yrT], pprr)
                P.tt(sgr[:], sgr[:], ppr[:, 0:TM], ALU.mult, pprr + [sgr], [sgr])
                w = wload(P, WBM_d[j], 1024)
                wv = w[:, 0:1024].rearrange("p (k m) -> p k m", m=128)
                _, ppm, ppmr = psum()
                for kc in range(8):
                    P.mm(ppm[:, 0:TM], wv[:, kc, :], ymT[:, kc, q0:q0 + TM], kc == 0, kc == 7, [w, ymT], ppmr)
                P.tt(sgm[:], sgm[:], ppm[:, 0:TM], ALU.mult, ppmr + [sgm], [sgm])
                P.tt(mrg[:, j, :], sgr[:], sgm[:], ALU.add, [sgr, sgm], [mrg])
            for j in (range(KC) if lvl >= 2 else []):
                w = wload(P, WOUT_d[j])
                wv = w[:].rearrange("p (k m) -> p k m", m=128)
                _, pm, pmr = psum()
                for kc in range(KC):
                    P.mm(pm[:, 0:TM], wv[:, kc, :], mrg[:, kc, :], kc == 0, kc == KC - 1, [w, mrg], pmr)
                P.stt(ZB[:, j, :], ZB[:, j, :], ALPHA, pm[:, 0:TM], ALU.mult, ALU.add, pmr + [ZB], [ZB])
            if lvl >= 3:
                layernorm(0, 16)
            for j in range(KC):
                P.copy(x1T[:, j, :], ZB[:, j, :], [ZB], [x1T])
            for ts_ in (range(TM // 128) if lvl >= 4 else []):
                tk = slice(ts_ * 128, (ts_ + 1) * 128)
                _, pl, plr = psum()
                for j in range(KC):
                    P.mm(pl[:, 0:36], ZB[:, j, tk], WR[:, j, :], j == 0, j == KC - 1, [ZB, WR], plr)
                P.tt(LG[:], pl[:, 0:36], rb[:], ALU.add, plr + [rb], [LG])
                P.rmax(R8[:, 0:1], LG[:, 0:4], [LG], [R8])
                P.ts(R8[:, 4:8], LG[:, 0:4], R8[:, 0:1], ALU.is_equal, [LG, R8], [R8])
                P.ts(R8[:, 1:2], R8[:, 0:1], -1.0, ALU.mult, [R8], [R8])
                P.act(R8[:, 8:12], LG[:, 0:4], AF.Exp, [LG, R8], [R8], bias=R8[:, 1:2])
                P.rsum(R8[:, 2:3], R8[:, 8:12], [R8], [R8])
                P.recip(R8[:, 3:4], R8[:, 2:3], [R8], [R8])
                P.ts(R8[:, 12:16], R8[:, 4:8], -1.0, ALU.add, [R8], [R8], s2=1e30, op1=ALU.mult)
                for g in range(4):
                    P.ts(LE2[:, g * 8:(g + 1) * 8], LG[:, 4 + g * 8:12 + g * 8], R8[:, 12 + g:13 + g], ALU.add, [LG, R8], [LE2])
                P.rmax(R8[:, 16:17], LE2[:], [LE2], [R8])
                P.ts(OH1[:], LE2[:], R8[:, 16:17], ALU.is_equal, [LE2, R8], [OH1])
                P.stt(LE2[:], OH1[:], -1e30, LE2[:], ALU.mult, ALU.add, [OH1, LE2], [LE2])
                P.rmax(R8[:, 17:18], LE2[:], [LE2], [R8])
                P.ts(OH2[:], LE2[:], R8[:, 17:18], ALU.is_equal, [LE2, R8], [OH2])
                P.tt(R8[:, 18:19], R8[:, 17:18], R8[:, 16:17], ALU.subtract, [R8], [R8])
                P.act(R8[:, 19:20], R8[:, 18:19], AF.Exp, [R8], [R8])
                P.ts(R8[:, 20:21], R8[:, 19:20], 1.0, ALU.add, [R8], [R8])
                P.recip(R8[:, 21:22], R8[:, 20:21], [R8], [R8])
                P.tt(R8[:, 22:23], R8[:, 21:22], R8[:, 3:4], ALU.mult, [R8], [R8])
                P.tt(R8[:, 23:24], R8[:, 22:23], R8[:, 19:20], ALU.mult, [R8], [R8])
                P.ts(COEF[:], OH1[:], R8[:, 22:23], ALU.mult, [OH1, R8], [COEF])
                P.stt(COEF[:], OH2[:], R8[:, 23:24], COEF[:], ALU.mult, ALU.add, [OH2, R8, COEF], [COEF])
                _, pct, pctr = psum()
                P.mm(pct[0:32, 0:128], COEF[:], ident, True, True, [COEF, cst], pctr)
                P.copy(COEFT[:, tk], pct[0:32, 0:128], pctr, [COEFT])
            for j in (range(KC) if lvl >= 5 else []):
                w = wload(P, WPG_d[j])
                wv = w[:].rearrange("p (k m) -> p k m", m=128)
                _, pp, ppr_ = psum()
                for kc in range(KC):
                    P.mm(pp[:, 0:TM], wv[:, kc, :], x1T[:, kc, :], kc == 0, kc == KC - 1, [w, x1T], ppr_)
                sg = TS[0]
                P.act(sg[:], pp[:, 0:TM], AF.Sigmoid, ppr_, [sg])
                w = wload(P, WPLE_d[j], 256)
                wv = w[:, 0:256].rearrange("p (k m) -> p k m", m=128)
                _, pq, pqr = psum()
                for kc in range(2):
                    P.mm(pq[:, 0:TM], wv[:, kc, :], pTb[:, kc, :], kc == 0, kc == 1, [w, pTb], pqr)
                P.tt(sg[:], sg[:], pq[:, 0:TM], ALU.mult, pqr + [sg], [sg])
                P.stt(ZB[:, j, :], ZB[:, j, :], ALPHA, sg[:], ALU.mult, ALU.add, [ZB, sg], [ZB])
            for e_ in (range(32) if lvl >= 6 else []):
                _, pcb, pcbr = psum()
                P.mm(pcb[:, 0:TM], sele[:, e_, :], COEFT[:], True, True, [sele, COEFT], pcbr)
                P.act(cbt[:], pcb[:, 0:TM], AF.Copy, pcbr, [cbt])
                for f in range(4):
                    w = wload(P, WG_d[e_, f])
                    wv = w[:].rearrange("p (k m) -> p k m", m=128)
                    _, pg, pgr_ = psum()
                    for kc in range(KC):
                        P.mm(pg[:, 0:TM], wv[:, kc, :], x1T[:, kc, :], kc == 0, kc == KC - 1, [w, x1T], pgr_)
                    w2 = wload(P, WU_d[e_, f])
                    wv2 = w2[:].rearrange("p (k m) -> p k m", m=128)
                    _, pu, pur = psum()
                    for kc in range(KC):
                        P.mm(pu[:, 0:TM], wv2[:, kc, :], x1T[:, kc, :], kc == 0, kc == KC - 1, [w2, x1T], pur)
                    sg = TS[f % 2]
                    P.act(sg[:], pg[:, 0:TM], AF.Silu, pgr_, [sg])
                    P.tt(sg[:], sg[:], pu[:, 0:TM], ALU.mult, pur + [sg], [sg])
                    P.tt(hT[:, f, :], sg[:], cbt[:], ALU.mult, [sg, cbt], [hT])
                for dg in range(4):
                    w = wload(P, WD_d[e_, dg])
                    wv = w[:].rearrange("p (c k m) -> p c k m", c=4, k=4)
                    for dcc in range(4):
                        j = dg * 4 + dcc
                        _, pd, pdr = psum()
                        for kc in range(4):
                            P.mm(pd[:, 0:TM], wv[:, dcc, kc, :], hT[:, kc, :], kc == 0, kc == 3, [w, hT], pdr)
                        P.tt(ZB[:, j, :], ZB[:, j, :], pd[:, 0:TM], ALU.add, pdr + [ZB], [ZB])
            if lvl >= 7:
                layernorm(32, 48)
            P.dma("sp", lambda e, q0=q0: e.dma_start(out=outT_d[:, :, q0:q0 + TM], in_=ZB[:]), "out", reads=[ZB])
        P.finish()
        P.emit()
    return nc


def _pack_lhsT(W):
    K, N = W.shape
    return np.ascontiguousarray(W.reshape(K // 128, 128, N // 128, 128).transpose(2, 1, 0, 3)).reshape(N // 128, 128, K)


def _consts():
    p = np.arange(128)
    ident = np.eye(128, dtype=np.float32)
    bones = (p[:, None] // 64 == p[None, :] // 64).astype(np.float32)
    tri = (p[:, None] <= p[None, :]).astype(np.float32)
    ones = np.ones((128, 128), np.float32)
    cst = np.concatenate([ident, bones, tri, ones, np.zeros((128, 128), np.float32)], axis=1)
    j = (p % 64)[:, None]
    t = np.arange(64)[None, :]
    lt = (j < t).astype(np.float32)
    le = (j <= t).astype(np.float32)
    one = np.ones((128, 64), np.float32)
    zero = np.zeros((128, 64), np.float32)
    mskA = np.concatenate([lt, le, one], axis=1)
    mskB = -mskA
    gt = (j > t).astype(np.float32)
    mskC = np.concatenate([gt, -gt], axis=1)
    i64 = (j == t).astype(np.float32)
    ifull = np.tile(i64, (1, 8))
    sele = np.zeros((32, 32, 128), np.float32)
    for e in range(32):
        sele[e, e, :] = 1.0
    return cst, mskA, mskB, mskC, ifull, sele.reshape(32, 32 * 128)


def _prep_shared(inp):
    g = lambda k: np.asarray(inp[k], dtype=np.float32)[0]
    w_in = g("w_in")
    Wp = np.zeros((NCH, 128, 2048), np.float32)
    for i, (_, c0, n) in enumerate(CHUNKS):
        blk = np.zeros((2048, 128), np.float32)
        blk[:, :n] = w_in[:, c0:c0 + n]
        Wp[i] = blk.reshape(16, 128, 128).transpose(1, 0, 2).reshape(128, 2048)
    sh = {"Wp": Wp}
    sh["W2A"] = np.ascontiguousarray(np.concatenate([g("w_w2"), g("w_a2")], axis=0))
    wg2 = g("w_g2")
    sh["G2a"] = np.ascontiguousarray(wg2[0:128])
    sh["G2b"] = np.ascontiguousarray(wg2[128:160])
    mu = g("mu_shift")
    vecR = np.zeros((128, 8, 10), np.float32)
    rk = g("r_k").reshape(1024)
    for hp in range(8):
        s = slice(hp * 128, (hp + 1) * 128)
        vecR[:, hp, 0] = mu[0:1024][s]
        vecR[:, hp, 1] = mu[1024:2048][s]
        vecR[:, hp, 2] = mu[2048:3072][s]
        vecR[:, hp, 3] = g("w0")[s]
        vecR[:, hp, 4] = g("a0")[s]
        vecR[:, hp, 5] = g("k_k")[s]
        vecR[:, hp, 6] = g("k_a")[s]
        vecR[:, hp, 7] = rk[s]
        vecR[:, hp, 8] = g("lnx_w")[s]
        vecR[:, hp, 9] = g("lnx_b")[s]
    sh["vecR"] = vecR.reshape(128, 80)
    vecL = np.zeros((128, 3), np.float32)
    vecL[:, 0] = mu[3072:3200]
    vecL[:, 1] = mu[3200:3328]
    vecL[0:32, 2] = mu[3328:3360]
    sh["vecL"] = vecL
    cw, cb = g("conv_w"), g("conv_b")
    vecM = np.zeros((128, 8, 5), np.float32)
    for hp2 in range(4):
        for wi, base in ((0, 0), (1, 512)):
            s = slice(base + hp2 * 128, base + (hp2 + 1) * 128)
            ci = hp2 * 2 + wi
            for j in range(4):
                vecM[:, ci, j] = cw[j, s]
            vecM[:, ci, 4] = cb[s]
    sh["vecM"] = vecM.reshape(128, 40)
    sh["mhw"] = np.ascontiguousarray(g("mh_w").reshape(8, 128).T)
    sh["gb"] = np.ascontiguousarray(np.broadcast_to(np.concatenate([g("i_bias"), g("f_bias")])[None, :], (128, 16)))
    sh["bgate"] = np.ascontiguousarray(g("b_gate").reshape(32, 128).T)
    sh["WBR"] = _pack_lhsT(g("w_br"))
    sh["WBM"] = _pack_lhsT(g("w_bm"))
    sh["WOUT"] = _pack_lhsT(g("w_out"))
    sh["WPG"] = _pack_lhsT(g("w_pg"))
    sh["WPLE"] = _pack_lhsT(g("w_ple"))
    wgt = g("w_gate")
    sh["WG"] = np.ascontiguousarray(wgt.reshape(32, 16, 128, 4, 128).transpose(0, 3, 2, 1, 4)).reshape(32, 4, 128, 2048)
    wup = g("w_up")
    sh["WU"] = np.ascontiguousarray(wup.reshape(32, 16, 128, 4, 128).transpose(0, 3, 2, 1, 4)).reshape(32, 4, 128, 2048)
    wdn = g("w_down")
    sh["WD"] = np.ascontiguousarray(wdn.reshape(32, 4, 128, 4, 4, 128).transpose(0, 3, 2, 4, 1, 5)).reshape(32, 4, 128, 2048)
    wr = np.concatenate([g("w_rg"), g("w_re")], axis=1)
    sh["WR"] = np.ascontiguousarray(wr.reshape(16, 128, 36).transpose(1, 0, 2)).reshape(128, 16 * 36)
    sh["rb"] = np.ascontiguousarray(np.broadcast_to(np.concatenate([g("b_rg"), g("b_re")])[None, :], (128, 36)))
    lnp = np.zeros((128, 64), np.float32)
    for i, k in enumerate(("ln1_w", "ln1_b", "ln2_w", "ln2_b")):
        lnp[:, i * 16:(i + 1) * 16] = g(k).reshape(16, 128).T
    sh["lnp"] = lnp
    cst, mskA, mskB, mskC, ifull, sele = _consts()
    sh.update({"cst": cst, "mskA": mskA, "mskB": mskB, "mskC": mskC, "ifull": ifull, "sele": sele})
    return sh


def _prep_core(x, p, b, half, T, NPH2):
    S = x.shape[1]
    end = (half + 1) * NPH2
    start = end - T
    win = np.zeros((T, D), np.float32)
    valid = np.zeros((T,), np.float32)
    s0 = max(start, 0)
    win[s0 - start:] = x[b, s0:end]
    valid[s0 - start:] = 1.0
    xT = np.ascontiguousarray(win.T.reshape(16, 128, T).transpose(1, 0, 2))
    pp = p[0, b, end - NPH2:end]
    pT = np.ascontiguousarray(pp.T.reshape(2, 128, NPH2).transpose(1, 0, 2))
    vch = np.ascontiguousarray(np.broadcast_to(valid.reshape(T // 128, 128)[:, 0][None, :], (128, T // 128)))
    return {"xT": xT, "pT": pT, "valid": vch}


def kernel(**inputs):
    x = np.asarray(inputs["x"], dtype=np.float32)
    p = np.asarray(inputs["p"], dtype=np.float32)
    B, S, _ = x.shape
    T, NPH2 = S, S // 2
    sh = _prep_shared(inputs)
    nc = build(T, NPH2)
    in_maps = []
    for c in range(8):
        m = dict(sh)
        m.update(_prep_core(x, p, c // 2, c % 2, T, NPH2))
        in_maps.append(m)
    res = run_bass_kernel_spmd(nc, in_maps, core_ids=list(range(8)))
    out = np.zeros((B, S, D), np.float32)
    for c in range(8):
        oT = res.results[c]["outT"]
        b, half = c // 2, c % 2
        out[b, half * NPH2:(half + 1) * NPH2, :] = oT.transpose(2, 1, 0).reshape(NPH2, D)
    return out
```

```python
import numpy as np
import concourse.bass as bass
import concourse.mybir as mybir
from concourse.bass_utils import run_bass_kernel_spmd
from contextlib import ExitStack

F32 = mybir.dt.float32
BF16 = mybir.dt.bfloat16
F32R = mybir.dt.float32r
ALU = mybir.AluOpType
AF = mybir.ActivationFunctionType
AX = mybir.AxisListType

D = 2048
KC = 16
ALPHA = 2.0 ** 0.25
LN_EPS = 1e-5
R_GN_EPS = 64e-5
M_NORM_EPS = 1e-6
TM = 512
USE_R = False


class Res:
    __slots__ = ("lw", "rd", "excl")

    def __init__(self, excl=False):
        self.lw = None
        self.rd = {}
        self.excl = excl


class Tile:
    def __init__(self, t):
        self.t = t
        self.r = Res()

    def __getitem__(self, k):
        return self.t[k]


def _base_part(ap):
    bp = ap.base_partition
    return bp() if callable(bp) else bp


def _res(x):
    return x.r if isinstance(x, Tile) else x


class Prog:
    ENG = ("pe", "act", "dve", "pool", "sp")
    SAME_WIN = 3

    def __init__(self, nc, tag):
        self.nc = nc
        self.tag = tag
        self.ops = {e: [] for e in self.ENG}
        self.cnt = {}
        self.clock = {e: {} for e in self.ENG}
        self.snap = {}
        self.sems = {}
        self._pe_free = False
        for e in self.ENG:
            self._mksem(e)

    def _mksem(self, key):
        self.sems[key] = self.nc.alloc_semaphore("s%s_%s" % (self.tag, str(key).replace(" ", "")))
        self.cnt[key] = 0

    def _need(self, eng, ev, waits):
        if ev is None:
            return
        key, val = ev
        if key == eng:
            if eng == "pe" and self._pe_free:
                return
            if self.cnt[eng] + 1 - val <= self.SAME_WIN:
                waits[key] = max(waits.get(key, 0), val)
            return
        if self.clock[eng].get(key, 0) >= val:
            return
        waits[key] = max(waits.get(key, 0), val)

    def _absorb(self, eng, waits):
        ck = self.clock[eng]
        for key, val in waits.items():
            if key == eng:
                continue
            if ck.get(key, 0) < val:
                ck[key] = val
            sn = self.snap.get((key, val))
            if sn:
                for k2, v2 in sn.items():
                    if k2 != eng and ck.get(k2, 0) < v2:
                        ck[k2] = v2

    def _deps(self, eng, reads, writes):
        waits = {}
        for r in reads:
            self._need(eng, _res(r).lw, waits)
        for w in writes:
            w = _res(w)
            self._need(eng, w.lw, waits)
            for k, v in w.rd.items():
                self._need(eng, (k, v), waits)
        return waits

    def op(self, eng, fn, reads=(), writes=()):
        ex = [r for r in reads if _res(r).excl]
        if ex:
            reads = [r for r in reads if not _res(r).excl]
            writes = list(writes) + ex
        waits = self._deps(eng, reads, writes)
        self._absorb(eng, waits)
        self.cnt[eng] += 1
        val = self.cnt[eng]
        self.ops[eng].append((tuple(waits.items()), fn, eng, 1))
        self.snap[(eng, val)] = dict(self.clock[eng])
        for r in reads:
            _res(r).rd[eng] = val
        for w in writes:
            w = _res(w)
            w.lw = (eng, val)
            w.rd = {}

    def dma(self, q, fn, semkey, reads=(), writes=()):
        if semkey not in self.sems:
            self._mksem(semkey)
        waits = self._deps(q, reads, writes)
        if self.cnt[semkey] > 0:
            self._need(q, (semkey, self.cnt[semkey]), waits)
        self._absorb(q, waits)
        self.cnt[semkey] += 16
        val = self.cnt[semkey]
        self.ops[q].append((tuple(waits.items()), fn, semkey, 16))
        self.snap[(semkey, val)] = dict(self.clock[q])
        for r in reads:
            _res(r).rd[semkey] = val
        for w in writes:
            w = _res(w)
            w.lw = (semkey, val)
            w.rd = {}

    def finish(self):
        for e in self.ENG:
            waits = {k: v for k, v in self.cnt.items() if v > 0 and k != e}
            self.ops[e].append((tuple(waits.items()), None, None, 0))

    def emit(self):
        nc = self.nc
        waited = {e: set() for e in self.ENG}
        for e in self.ENG:
            for waits, fn, semkey, inc in self.ops[e]:
                for k, v in waits:
                    if k in waited:
                        waited[k].add(v)
        rank = {e: {v: i + 1 for i, v in enumerate(sorted(waited[e]))} for e in self.ENG}
        with nc.Block() as block:
            def run(e, engobj):
                ci = 0
                for waits, fn, semkey, inc in self.ops[e]:
                    for k, v in waits:
                        engobj.wait_ge(self.sems[k], rank[k][v] if k in rank else v)
                    if fn is None:
                        continue
                    if semkey == e:
                        ci += 1
                        if ci in rank[e]:
                            fn(engobj).then_inc(self.sems[e], 1)
                        else:
                            fn(engobj)
                    else:
                        fn(engobj).then_inc(self.sems[semkey], inc)

            @block.tensor
            def _(eng):
                run("pe", eng)

            @block.scalar
            def _(eng):
                run("act", eng)

            @block.vector
            def _(eng):
                run("dve", eng)

            @block.gpsimd
            def _(eng):
                run("pool", eng)

            @block.sync
            def _(eng):
                run("sp", eng)

    def mm(self, out, lhsT, rhs, start, stop, reads, writes, r=False, free=True):
        self._pe_free = free
        try:
            self._mm(out, lhsT, rhs, start, stop, reads, writes, r)
        finally:
            self._pe_free = False

    def _mm(self, out, lhsT, rhs, start, stop, reads, writes, r=False):
        if r and USE_R and _base_part(out) == 0:
            lhsT = lhsT.bitcast(F32R)
            rhs = rhs.bitcast(F32R)
        elif r:
            lhsT = lhsT.bitcast(F32)
            rhs = rhs.bitcast(F32)
        self.op("pe", lambda e: e.matmul(out, lhsT=lhsT, rhs=rhs, start=start, stop=stop), reads, writes)

    def act(self, out, in_, func, reads, writes, bias=None, scale=None):
        kw = {}
        if bias is not None:
            kw["bias"] = bias
        if scale is not None:
            kw["scale"] = scale
        self.op("act", lambda e: e.activation(out=out, in_=in_, func=func, **kw), reads, writes)

    def tt(self, out, in0, in1, op, reads, writes, eng="dve"):
        self.op(eng, lambda e: e.tensor_tensor(out=out, in0=in0, in1=in1, op=op), reads, writes)

    def ts(self, out, in0, s1, op0, reads, writes, s2=None, op1=None, eng="dve"):
        if op1 is None:
            self.op(eng, lambda e: e.tensor_scalar(out=out, in0=in0, scalar1=s1, scalar2=None, op0=op0), reads, writes)
        else:
            self.op(eng, lambda e: e.tensor_scalar(out=out, in0=in0, scalar1=s1, scalar2=s2, op0=op0, op1=op1), reads, writes)

    def stt(self, out, in0, scalar, in1, op0, op1, reads, writes, eng="dve"):
        self.op(eng, lambda e: e.scalar_tensor_tensor(out=out, in0=in0, scalar=scalar, in1=in1, op0=op0, op1=op1), reads, writes)

    def copy(self, out, in_, reads, writes, eng="dve"):
        self.op(eng, lambda e: e.tensor_copy(out=out, in_=in_), reads, writes)

    def recip(self, out, in_, reads, writes):
        self.op("dve", lambda e: e.reciprocal(out=out, in_=in_), reads, writes)

    def memset(self, out, val, writes, eng="dve"):
        self.op(eng, lambda e: e.memset(out, val), (), writes)

    def rsum(self, out, in_, reads, writes):
        self.op("dve", lambda e: e.reduce_sum(out=out, in_=in_, axis=AX.X), reads, writes)

    def rmax(self, out, in_, reads, writes):
        self.op("dve", lambda e: e.reduce_max(out=out, in_=in_, axis=AX.X), reads, writes)


def _chunks():
    ch = []
    for hp in range(8):
        ch.append(("r%d" % hp, 0 + hp * 128, 128))
        ch.append(("k%d" % hp, 1024 + hp * 128, 128))
        ch.append(("v%d" % hp, 2048 + hp * 128, 128))
    ch.append(("L0", 3072, 128))
    ch.append(("L1", 3200, 128))
    ch.append(("L2", 3328, 32))
    for hp in range(4):
        ch.append(("mq%d" % hp, 3360 + hp * 128, 128))
        ch.append(("mk%d" % hp, 3872 + hp * 128, 128))
    for h in range(8):
        ch.append(("mv%d" % h, 4384 + h * 128, 128))
        ch.append(("mo%d" % h, 5424 + h * 128, 128))
    ch.append(("mg", 5408, 16))
    for j in range(16):
        ch.append(("gr%d" % j, 6448 + j * 128, 128))
        ch.append(("gm%d" % j, 8496 + j * 128, 128))
    return ch


CHUNKS = _chunks()
CIDX = {c[0]: i for i, c in enumerate(CHUNKS)}
NCH = len(CHUNKS)


def build(T, NPH2, debug=False, parts=("lora", "rwkv", "mlstm", "ph2"), lvl=9):
    NMT = T // TM
    PH2_0 = T - NPH2
    nc = bass.Bass("TRN2", target_bir_lowering=False)

    def din(name, shape):
        return nc.dram_tensor(name, list(shape), F32, kind="ExternalInput").ap()

    xT_d = din("xT", [128, KC, T])
    pT_d = din("pT", [128, 2, NPH2])
    valid_d = din("valid", [128, T // 128])
    Wp_d = din("Wp", [NCH, 128, 2048])
    W2A_d = din("W2A", [128, 1024])
    G2a_d = din("G2a", [128, 1024])
    G2b_d = din("G2b", [32, 1024])
    vecR_d = din("vecR", [128, 80])
    vecL_d = din("vecL", [128, 3])
    vecM_d = din("vecM", [128, 40])
    mhw_d = din("mhw", [128, 8])
    gb_d = din("gb", [128, 16])
    bgate_d = din("bgate", [128, 32])
    WBR_d = din("WBR", [16, 128, 1024])
    WBM_d = din("WBM", [16, 128, 1024])
    WOUT_d = din("WOUT", [16, 128, 2048])
    WPG_d = din("WPG", [16, 128, 2048])
    WPLE_d = din("WPLE", [16, 128, 256])
    WG_d = din("WG", [32, 4, 128, 2048])
    WU_d = din("WU", [32, 4, 128, 2048])
    WD_d = din("WD", [32, 4, 128, 2048])
    WR_d = din("WR", [128, KC * 36])
    rb_d = din("rb", [128, 36])
    lnp_d = din("lnp", [128, 64])
    cst_d = din("cst", [128, 128 * 5])
    mskA_d = din("mskA", [128, 192])
    mskB_d = din("mskB", [128, 192])
    mskC_d = din("mskC", [128, 128])
    ifull_d = din("ifull", [128, 512])
    sele_d = din("sele", [32, 32 * 128])
    outT_d = nc.dram_tensor("outT", [128, KC, NPH2], F32, kind="ExternalOutput").ap()
    if debug:
        dbg_d = nc.dram_tensor("dbg", [128, 2 * 8 * NPH2], F32, kind="ExternalOutput").ap()

    def sb(name, shape, dt=F32):
        return Tile(nc.alloc_sbuf_tensor("sb_" + name, list(shape), dt))

    PS = nc.alloc_psum_tensor("PS", [128, 4096], F32)
    PSR = [Res(excl=True) for _ in range(8)]
    yrT = sb("yrT", [128, 8, NPH2], BF16)
    ymT = sb("ymT", [128, 8, NPH2], BF16)
    cst = sb("cst", [128, 640])
    ident = cst[:, 0:128]
    bones = cst[:, 128:256]
    tri = cst[:, 256:384]
    ones = cst[:, 384:512]
    NW = 4
    wslot = [sb("wslot%d" % i, [128, 2048], BF16) for i in range(NW)]
    xb = sb("xb", [128, KC, TM], BF16)

    state = {"ps": 0, "w": 0}

    def psum(nb=1):
        i = state["ps"]
        if i + nb > 8:
            i = 0
        state["ps"] = (i + nb) % 8
        return i, PS[:, i * 512:(i + nb) * 512], PSR[i:i + nb]

    def wload(P, src, ncols=2048):
        i = state["w"]
        state["w"] = (i + 1) % NW
        t = wslot[i]
        P.dma("pool", lambda e: e.dma_start(out=t[:, 0:ncols], in_=src), ("w", i), writes=[t])
        return t

    def cload(P, tile, src, q="sp", key="c"):
        P.dma(q, lambda e: e.dma_start(out=tile[:], in_=src), key, writes=[tile])

    with ExitStack() as es:
        def sbt(name, shape, dt=F32):
            return Tile(es.enter_context(nc.sbuf_tensor("sb_" + name, list(shape), dt)))

        P = Prog(nc, "a")
        cload(P, cst, cst_d)
        W2A = sbt("W2A", [128, 1024], BF16)
        G2a = sbt("G2a", [128, 1024], BF16)
        G2b = sbt("G2b", [32, 1024], BF16)
        cload(P, W2A, W2A_d, "pool", "c2")
        cload(P, G2a, G2a_d, "pool", "c2")
        cload(P, G2b, G2b_d, "pool", "c2")
        vecR = sbt("vecR", [128, 80]); cload(P, vecR, vecR_d)
        vecL = sbt("vecL", [128, 3]); cload(P, vecL, vecL_d)
        vecM = sbt("vecM", [128, 40]); cload(P, vecM, vecM_d)
        mhw = sbt("mhw", [128, 8]); cload(P, mhw, mhw_d)
        gb = sbt("gb", [128, 16]); cload(P, gb, gb_d)
        valid = sbt("valid", [128, T // 128]); cload(P, valid, valid_d)
        mskA = sbt("mskA", [128, 192]); cload(P, mskA, mskA_d)
        mskB = sbt("mskB", [128, 192]); cload(P, mskB, mskB_d)
        mskC = sbt("mskC", [128, 128]); cload(P, mskC, mskC_d)
        ifull = sbt("ifull", [128, 512]); cload(P, ifull, ifull_d)
        cstR = sbt("cstR", [128, 512], F32R)
        P.copy(cstR[:], cst[:, 0:512], [cst], [cstR])
        identR = cstR[:, 0:128]
        bonesR = cstR[:, 128:256]
        triR = cstR[:, 256:384]
        onesR = cstR[:, 384:512]
        scanm = sbt("scanm", [128, 512])
        P.memset(scanm[:], 1.0, [scanm])
        P.memset(scanm[:].rearrange("p (c l) -> p c l", l=64)[:, :, 0:1], 0.0, [scanm])

        ST = sbt("ST", [128, 8, 64], F32R)
        P.memset(ST[:].bitcast(F32), 0.0, [ST])
        CS = sbt("CS", [128, 4, 129], F32R)
        P.memset(CS[:].bitcast(F32), 0.0, [CS])
        carry = sbt("carry", [128, 32])
        P.memset(carry[:], 0.0, [carry])
        ccarry = sbt("ccarry", [128, 8, 3])
        P.memset(ccarry[:], 0.0, [ccarry])

        NSC = 15
        SC = [sbt("sc%d" % i, [128, 516]) for i in range(NSC)]
        epsR = sbt("epsR", [128, 1]); P.memset(epsR[:], R_GN_EPS, [epsR])
        epsM = sbt("epsM", [128, 1]); P.memset(epsM[:], M_NORM_EPS, [epsM])
        oneC = sbt("oneC", [128, 1]); P.memset(oneC[:], 1.0, [oneC])
        TL = sbt("TL", [128, TM], BF16)
        SG0 = sbt("SG0", [128, TM], BF16)
        SG1 = sbt("SG1", [32, TM], BF16)
        R3 = sbt("R3", [128, 8, 192], F32R)
        K2 = sbt("K2", [128, 8, 128], F32R)
        EA = sbt("EA", [128, 4, 2, 192], F32R)
        EB = sbt("EB", [128, 4, 2, 192], F32R)
        EC = sbt("EC", [128, 4, 2, 128], F32R)
        XA = sbt("XA", [128, 4, 2, 64], F32R); XTA = sbt("XTA", [128, 4, 2, 64], F32R)
        XB = sbt("XB", [128, 4, 2, 64], F32R); XTB = sbt("XTB", [128, 4, 2, 64], F32R)
        PM = sbt("PM", [128, 4, 2, 64], F32R)
        VmT = sbt("VmT", [128, 4, 2, 64], F32R)
        RT = sbt("RT", [128, 2, 64], F32R); UT = sbt("UT", [128, 2, 64], F32R)
        P.memset(R3[:].bitcast(F32), 0.0, [R3])
        for c in range(8):
            P.copy(R3[0:64, c, 128:192], ident[0:64, 0:64], [cst], [R3])
            P.copy(R3[64:128, c, 128:192], ident[64:128, 64:128], [cst], [R3])
        GI = sbt("GI", [128, 16]); EL = sbt("EL", [128, 8]); LL = sbt("LL", [128, 8], F32R)
        EAc = sbt("EAc", [128, 4, 8]); EKc = sbt("EKc", [128, 4, 8]); EALc = sbt("EALc", [128, 4, 8])
        tm8 = sbt("tm8", [128, 8])
        VP = sbt("VP", [128, 4, 129], F32R)
        Gm = sbt("Gm", [128, 128], F32R); kTk = sbt("kTk", [128, 64], F32R); hh = sbt("hh", [128, 128]); hn = sbt("hn", [128, 128], F32R)
        hsq = sbt("hsq", [128, 128])
        sm = sbt("sm", [128, 16])

        def inproj(cname, ncols=128, n0=0, nn=TM):
            w = wload(P, Wp_d[CIDX[cname]])
            wv = w[:].rearrange("p (k m) -> p k m", m=128)
            pi, pap, pr = psum()
            for kc in range(KC):
                P.mm(pap[0:ncols, 0:nn], wv[:, kc, 0:ncols], xb[:, kc, n0:n0 + nn], kc == 0, kc == KC - 1, [w, xb], pr)
            return pap, pr

        def shifted(cname, ci, mu_ap, zt, out_t, ncols=128, rnd=False):
            pap, pr = inproj(cname, ncols)
            P.copy(zt[0:ncols, 0:1], carry[0:ncols, ci:ci + 1], [carry], [zt])
            P.act(zt[0:ncols, 1:TM + 1], pap[0:ncols, 0:TM], AF.Copy, pr, [zt])
            P.copy(carry[0:ncols, ci:ci + 1], zt[0:ncols, TM:TM + 1], [zt], [carry])
            P.tt(out_t[0:ncols, 0:TM], zt[0:ncols, 0:TM], zt[0:ncols, 1:TM + 1], ALU.subtract, [zt], [out_t])
            oo = out_t[0:ncols, 0:TM].bitcast(F32R) if rnd else out_t[0:ncols, 0:TM]
            P.stt(oo, out_t[0:ncols, 0:TM], mu_ap, zt[0:ncols, 1:TM + 1], ALU.mult, ALU.add, [out_t, zt, vecR, vecL], [out_t])

        def bmm(in_ap, in_t):
            pi, pap, pr = psum()
            P.mm(pap[:, 0:TM], bones, in_ap, True, True, [cst, in_t], pr)
            return pap, pr

        for mt in range(NMT):
            t0 = mt * TM
            P.dma("pool", lambda e, t0=t0: e.dma_start(out=xb[:], in_=xT_d[:, :, t0:t0 + TM]), "xb", writes=[xb])
            inph2 = t0 >= PH2_0
            need_carry = (t0 + TM >= PH2_0)
            q0 = t0 - PH2_0
            z, o = SC[0], SC[1]
            shifted("L0", 24, vecL[:, 0:1], z, o)
            P.act(TL[0:64, :], o[0:64, 0:TM], AF.Tanh, [o], [TL])
            P.copy(TL[64:128, :], o[64:128, 0:TM], [o], [TL])
            if need_carry:
                shifted("L1", 25, vecL[:, 1:2], z, o)
                P.act(SG0[:, :], o[:, 0:TM], AF.Sigmoid, [o], [SG0])
                shifted("L2", 26, vecL[0:32, 2:3], z, o, ncols=32)
                P.act(SG1[:, :], o[0:32, 0:TM], AF.Sigmoid, [o], [SG1])
            for hp in (range(8) if "rwkv" in parts else []):
                cs = slice(hp * 128, (hp + 1) * 128)
                vr = lambda i: vecR[:, hp * 10 + i:hp * 10 + i + 1]
                rs, ks, vs = SC[2], SC[3], SC[4]
                if need_carry:
                    shifted("r%d" % hp, hp * 3 + 0, vr(0), SC[0], rs)
                shifted("k%d" % hp, hp * 3 + 1, vr(1), SC[0], ks)
                shifted("v%d" % hp, hp * 3 + 2, vr(2), SC[0], vs)
                _, pw, pwr = psum()
                P.mm(pw[:, 0:TM], W2A[0:64, cs], TL[0:64, :], True, True, [W2A, TL], pwr)
                _, pa, par_ = psum()
                P.mm(pa[:, 0:TM], W2A[64:128, cs], TL[64:128, :], True, True, [W2A, TL], par_)
                lw, aa, gg = SC[5], SC[6], SC[7]
                if inph2:
                    _, pg, pgr = psum()
                    P.mm(pg[:, 0:TM], G2a[:, cs], SG0[:, :], True, False, [G2a, SG0], pgr)
                    P.mm(pg[:, 0:TM], G2b[:, cs], SG1[:, :], False, True, [G2b, SG1], pgr)
                    P.act(gg[:, 0:TM], pg[:, 0:TM], AF.Copy, pgr, [gg])
                P.act(lw[:, 0:TM], pw[:, 0:TM], AF.Sigmoid, pwr + [vecR], [lw], bias=vr(3))
                P.ts(lw[:, 0:TM], lw[:, 0:TM], -float(np.exp(-0.5)), ALU.mult, [lw], [lw])
                P.act(aa[:, 0:TM], pa[:, 0:TM], AF.Sigmoid, par_ + [vecR], [aa], bias=vr(4))
                kk, sq, kap = SC[8], SC[9], SC[10]
                P.ts(kk[:, 0:TM], ks[:, 0:TM], vr(5), ALU.mult, [ks, vecR], [kk])
                P.tt(sq[:, 0:TM], kk[:, 0:TM], kk[:, 0:TM], ALU.mult, [kk], [sq])
                pss, pssr = bmm(sq[:, 0:TM], sq)
                P.act(sq[:, 0:TM], pss[:, 0:TM], AF.Sqrt, pssr, [sq])
                P.ts(sq[:, 0:TM], sq[:, 0:TM], 1e-12, ALU.max, [sq], [sq])
                P.recip(sq[:, 0:TM], sq[:, 0:TM], [sq], [sq])
                P.tt(kap[:, 0:TM], kk[:, 0:TM], sq[:, 0:TM], ALU.mult, [kk, sq], [kap])
                km, beta = SC[11], SC[12]
                P.ts(km[:, 0:TM], aa[:, 0:TM], -1.0, ALU.add, [aa, vecR], [km], s2=vr(6), op1=ALU.mult)
                P.stt(km[:, 0:TM], km[:, 0:TM], 1.0, ks[:, 0:TM], ALU.add, ALU.mult, [km, ks], [km])
                P.tt(beta[:, 0:TM], aa[:, 0:TM], kap[:, 0:TM], ALU.mult, [aa, kap], [beta])
                bon = SC[13]
                if inph2:
                    P.stt(bon[:, 0:TM], rs[:, 0:TM], vr(7), km[:, 0:TM], ALU.mult, ALU.mult, [rs, km, vecR], [bon])
                    pb, pbr = bmm(bon[:, 0:TM], bon)
                    P.tt(bon[:, 0:TM], pb[:, 0:TM], vs[:, 0:TM], ALU.mult, pbr + [vs], [bon])
                cc, ep, en, epv = SC[8], SC[14], SC[6], SC[9]
                P.op("dve", lambda e, cc=cc, lw=lw: e.tensor_tensor_scan(out=cc[:, 0:TM], data0=scanm[:, 0:TM], data1=lw[:, 0:TM],
                                                                         initial=0.0, op0=ALU.mult, op1=ALU.add), [scanm, lw], [cc])
                P.act(ep[:, 0:TM], cc[:, 0:TM], AF.Exp, [cc], [ep])
                P.act(en[:, 0:TM], cc[:, 0:TM], AF.Exp, [cc], [en], scale=-1.0)
                P.tt(epv[:, 0:TM], cc[:, 0:TM], lw[:, 0:TM], ALU.subtract, [cc, lw], [epv])
                P.act(epv[:, 0:TM], epv[:, 0:TM], AF.Exp, [epv], [epv])
                c3 = lambda t_: t_[:, 0:TM].rearrange("p (c l) -> p c l", l=64)
                P.tt(R3[:, :, 0:64], c3(kap), c3(epv), ALU.mult, [kap, epv], [R3])
                if inph2:
                    P.tt(R3[:, :, 64:128], c3(rs), c3(ep), ALU.mult, [rs, ep], [R3])
                P.tt(K2[:, :, 0:64], c3(km), c3(en), ALU.mult, [km, en], [K2])
                P.tt(K2[:, :, 64:128], c3(beta), c3(en), ALU.mult, [beta, en], [K2])
                for dc in range(4):
                    _, pt, ptr = psum()
                    P.mm(pt[:, 0:128], vs[:, dc * 128:(dc + 1) * 128], ident, True, True, [vs, cst], ptr)
                    P.copy(VmT[:, dc, :, :], pt[:, 0:128].rearrange("p (h v) -> p h v", v=64), ptr, [VmT])
                hr = lambda h: slice(h * 64, (h + 1) * 64)
                pr_ = lambda c: slice((c % 2) * 64, (c % 2) * 64 + 64)
                _, pA, pAr = psum(4)
                pAv = pA.rearrange("p (d h w) -> p d h w", d=4, h=2)
                for h in range(2):
                    for c in range(8):
                        P.mm(pAv[pr_(c), c // 2, h, 0:192], K2[hr(h), c, 0:64], R3[hr(h), c, 0:192], True, True, [K2, R3], pAr, r=True, free=(c > 0))
                for d_ in range(4):
                    for h in range(2):
                        P.tt(EA[:, d_, h, :], pAv[:, d_, h, 0:192], mskA[:], ALU.mult, pAr + [mskA], [EA])
                _, pB, pBr = psum(4)
                pBv = pB.rearrange("p (d h w) -> p d h w", d=4, h=2)
                for h in range(2):
                    for c in range(8):
                        P.mm(pBv[pr_(c), c // 2, h, 0:192], K2[hr(h), c, 64:128], R3[hr(h), c, 0:192], True, True, [K2, R3], pBr, r=True, free=(c > 0))
                for d_ in range(4):
                    for h in range(2):
                        P.tt(EB[:, d_, h, :], pBv[:, d_, h, 0:192], mskB[:], ALU.mult, pBr + [mskB], [EB])
                _, pC, pCr = psum(2)
                pCv = pC.rearrange("p (d h w) -> p d h w", d=4, h=2)
                for h in range(2):
                    for c in range(8):
                        P.mm(pCv[pr_(c), c // 2, h, 0:128], R3[hr(h), c, 0:64], K2[hr(h), c, 0:128], True, True, [K2, R3], pCr, r=True, free=(c > 0))
                for d_ in range(4):
                    for h in range(2):
                        P.tt(EC[:, d_, h, :], pCv[:, d_, h, :], mskC[:], ALU.mult, pCr + [mskC], [EC])
                X = (EB, lambda c, h: EB[pr_(c), c // 2, h, 0:64])
                XT = (EC, lambda c, h: EC[pr_(c), c // 2, h, 64:128])
                P.tt(PM[:], EB[:, :, :, 0:64], ifull[:].rearrange("p (d h w) -> p d h w", d=4, h=2), ALU.add, [EB, ifull], [PM])
                bufs = [(XA, XTA), (XB, XTB)]
                for it in range(5):
                    nX, nXT = bufs[it % 2]
                    last = it == 4
                    _, p2, p2r = psum()
                    p2v = p2.rearrange("p (d h w) -> p d h w", d=4, h=2)
                    for c in range(8):
                        for h in range(2):
                            P.mm(p2v[pr_(c), c // 2, h, :], X[1](c, h), XT[1](c, h), True, True, [X[0], XT[0]], p2r, r=True)
                    P.act(nXT[:].rearrange("p d h w -> p (d h w)"), p2, AF.Copy, p2r, [nXT])
                    if not last:
                        _, p1, p1r = psum()
                        p1v = p1.rearrange("p (d h w) -> p d h w", d=4, h=2)
                        for c in range(8):
                            for h in range(2):
                                P.mm(p1v[pr_(c), c // 2, h, :], XT[1](c, h), X[1](c, h), True, True, [X[0], XT[0]], p1r, r=True)
                        P.act(nX[:].rearrange("p d h w -> p (d h w)"), p1, AF.Copy, p1r, [nX])
                    _, p3, p3r = psum()
                    p3v = p3.rearrange("p (d h w) -> p d h w", d=4, h=2)
                    for c in range(8):
                        for h in range(2):
                            P.mm(p3v[pr_(c), c // 2, h, :], nXT[pr_(c), c // 2, h, :], PM[pr_(c), c // 2, h, :], True, True, [nXT, PM], p3r, r=True)
                    P.tt(PM[:].rearrange("p d h w -> p (d h w)"), PM[:].rearrange("p d h w -> p (d h w)"), p3, ALU.add, p3r + [PM], [PM])
                    X = (nX, lambda c, h, nX=nX: nX[pr_(c), c // 2, h, :])
                    XT = (nXT, lambda c, h, nXT=nXT: nXT[pr_(c), c // 2, h, :])
                yb = SC[3]
                for c in range(8):
                    rows = pr_(c)
                    dc = c // 2
                    _, p1, p1r = psum()
                    for h in range(2):
                        P.mm(p1[rows, h * 64:(h + 1) * 64], R3[hr(h), c, 0:64], ST[hr(h), hp, :], True, False, [R3, ST], p1r, r=True, free=False)
                        P.mm(p1[rows, h * 64:(h + 1) * 64], EA[rows, dc, h, 0:64], VmT[rows, dc, h, :], False, True, [EA, VmT], p1r, r=True, free=False)
                    P.act(RT[rows, :, :], p1[rows, 0:128].rearrange("p (h v) -> p h v", v=64), AF.Copy, p1r, [RT])
                    _, p2, p2r = psum()
                    for h in range(2):
                        P.mm(p2[rows, h * 64:(h + 1) * 64], PM[rows, dc, h, :], RT[rows, h, :], True, True, [PM, RT], p2r, r=True, free=False)
                    P.copy(UT[rows, :, :], p2[rows, 0:128].rearrange("p (h v) -> p h v", v=64), p2r, [UT])
                    if inph2:
                        _, pY, pYr = psum()
                    _, pS, pSr = psum()
                    for h in range(2):
                        if inph2:
                            P.mm(pY[hr(h), 0:64], ST[hr(h), hp, :], R3[hr(h), c, 64:128], True, False, [ST, R3], pYr, r=True, free=False)
                            P.mm(pY[hr(h), 0:64], VmT[rows, dc, h, :], EA[rows, dc, h, 64:128], False, False, [VmT, EA], pYr, r=True, free=False)
                            P.mm(pY[hr(h), 0:64], UT[rows, h, :], EB[rows, dc, h, 64:128], False, True, [UT, EB], pYr, r=True, free=False)
                        idh = ident[hr(h), hr(h)]
                        P.mm(pS[hr(h), 0:64], idh, ST[hr(h), hp, :].bitcast(F32), True, False, [cst, ST], pSr, free=False)
                        P.mm(pS[hr(h), 0:64], EA[rows, dc, h, 128:192], VmT[rows, dc, h, :], False, False, [EA, VmT], pSr, r=True, free=False)
                        P.mm(pS[hr(h), 0:64], EB[rows, dc, h, 128:192], UT[rows, h, :], False, True, [EB, UT], pSr, r=True, free=False)
                    if inph2:
                        P.act(yb[:, c * 64:(c + 1) * 64], pY[:, 0:64], AF.Copy, pYr, [yb])
                    P.ts(ST[:, hp, :], pS[:, 0:64], ep[:, c * 64 + 63:c * 64 + 64], ALU.mult, pSr + [ep], [ST])
                if inph2:
                    pm, pmr = bmm(yb[:, 0:TM], yb)
                    mean, dd, var = SC[10], SC[11], SC[12]
                    P.act(mean[:, 0:TM], pm[:, 0:TM], AF.Copy, pmr, [mean], scale=1.0 / 64)
                    P.tt(dd[:, 0:TM], yb[:, 0:TM], mean[:, 0:TM], ALU.subtract, [yb, mean], [dd])
                    P.tt(var[:, 0:TM], dd[:, 0:TM], dd[:, 0:TM], ALU.mult, [dd], [var])
                    pq, pqr = bmm(var[:, 0:TM], var)
                    P.act(var[:, 0:TM], pq[:, 0:TM], AF.Sqrt, pqr + [epsR], [var], scale=1.0 / 64, bias=epsR[:, 0:1])
                    P.recip(var[:, 0:TM], var[:, 0:TM], [var], [var])
                    P.tt(dd[:, 0:TM], dd[:, 0:TM], var[:, 0:TM], ALU.mult, [dd, var], [dd])
                    P.act(dd[:, 0:TM], dd[:, 0:TM], AF.Identity, [dd, vecR], [dd], scale=vr(8), bias=vr(9))
                    P.tt(dd[:, 0:TM], dd[:, 0:TM], bon[:, 0:TM], ALU.add, [dd, bon], [dd])
                    P.tt(yrT[:, hp, q0:q0 + TM], dd[:, 0:TM], gg[:, 0:TM], ALU.mult, [dd, gg], [yrT])
            wmg = wload(P, Wp_d[CIDX["mg"]])
            wmgv = wmg[:].rearrange("p (k m) -> p k m", m=128)
            for ck in (range(4) if "mlstm" in parts else []):
                _, pgt, pgtr = psum()
                for kc in range(KC):
                    P.mm(pgt[:, 0:16], xb[:, kc, ck * 128:(ck + 1) * 128], wmgv[:, kc, 0:16], kc == 0, kc == KC - 1, [xb, wmg], pgtr)
                P.tt(GI[:], pgt[:, 0:16], gb[:], ALU.add, pgtr + [gb], [GI])
                P.act(EL[:], GI[:, 8:16], AF.Exp, [GI], [EL], scale=-1.0)
                P.act(LL[:], EL[:], AF.Ln, [EL, oneC], [LL], bias=oneC[:, 0:1])
                _, pc, pcr = psum()
                P.mm(pc[:, 0:8], triR, LL[:], True, True, [cstR, LL], pcr, r=True)
                P.mm(pc[:, 8:16], onesR, LL[:], True, True, [cstR, LL], pcr, r=True)
                P.act(EAc[:, ck, :], pc[:, 0:8], AF.Exp, pcr, [EAc], scale=-1.0)
                P.tt(tm8[:], GI[:, 0:8], pc[:, 0:8], ALU.add, pcr + [GI], [tm8])
                P.act(EKc[:, ck, :], tm8[:], AF.Exp, [tm8], [EKc])
                P.act(tm8[:], pc[:, 8:16], AF.Exp, pcr, [tm8], scale=-1.0)
                gck = mt * 4 + ck
                P.ts(EALc[:, ck, :], tm8[:], valid[:, gck:gck + 1], ALU.mult, [tm8, valid], [EALc])
            for hp2 in (range(4) if "mlstm" in parts else []):
                qf, kf = SC[2], SC[3]
                for which, dst in (("mq", qf), ("mk", kf)):
                    if which == "mq" and not need_carry:
                        continue
                    ci = hp2 * 2 + (0 if which == "mq" else 1)
                    vm = lambda i: vecM[:, ci * 5 + i:ci * 5 + i + 1]
                    pap, pr = inproj("%s%d" % (which, hp2))
                    zc = SC[0]
                    P.copy(zc[:, 0:3], ccarry[:, ci, :], [ccarry], [zc])
                    P.act(zc[:, 3:TM + 3], pap[:, 0:TM], AF.Copy, pr, [zc])
                    P.copy(ccarry[:, ci, :], zc[:, TM:TM + 3], [zc], [ccarry])
                    acc = SC[1]
                    P.ts(acc[:, 0:TM], zc[:, 0:TM], vm(0), ALU.mult, [zc, vecM], [acc], s2=vm(4), op1=ALU.add)
                    for j in range(1, 4):
                        P.stt(acc[:, 0:TM], zc[:, j:j + TM], vm(j), acc[:, 0:TM], ALU.mult, ALU.add, [zc, acc, vecM], [acc])
                    if which == "mk":
                        P.act(dst[:, 0:TM], acc[:, 0:TM], AF.Silu, [acc], [dst])
                        P.ts(dst[:, 0:TM], dst[:, 0:TM], 0.125, ALU.mult, [dst], [dst])
                    else:
                        P.act(dst[:, 0:TM], acc[:, 0:TM], AF.Silu, [acc], [dst])
                for hh_ in range(2):
                    h = hp2 * 2 + hh_
                    hrows = slice(hh_ * 64, hh_ * 64 + 64)
                    so = SC[4]
                    if inph2:
                        pap, pr = inproj("mo%d" % h)
                        P.act(so[:, 0:TM], pap[:, 0:TM], AF.Sigmoid, pr, [so])
                    wv_ = wload(P, Wp_d[CIDX["mv%d" % h]])
                    wvv = wv_[:].rearrange("p (k m) -> p k m", m=128)
                    for ck in range(4):
                        _, pv, pvr = psum()
                        for kc in range(KC):
                            P.mm(pv[:, 0:128], xb[:, kc, ck * 128:(ck + 1) * 128], wvv[:, kc, :], kc == 0, kc == KC - 1, [xb, wv_], pvr)
                        P.act(VP[:, ck, 0:128], pv[:, 0:128], AF.Copy, pvr, [VP])
                    P.memset(VP[:, :, 128:129].bitcast(F32), 1.0, [VP])
                    for ck in range(4):
                        tk = slice(ck * 128, (ck + 1) * 128)
                        if inph2:
                            _, pG, pGr = psum()
                            P.mm(pG[:, 0:128], kf[hrows, tk], qf[hrows, tk], True, True, [kf, qf], pGr)
                            P.stt(Gm[:], pG[:, 0:128], EKc[:, ck, h:h + 1], tri, ALU.mult, ALU.mult, pGr + [EKc, cst], [Gm])
                        _, pK, pKr = psum()
                        P.mm(pK[:, 0:64], kf[hrows, tk], ident[hrows, hrows], True, True, [kf, cst], pKr)
                        P.ts(kTk[:], pK[:, 0:64], EKc[:, ck, h:h + 1], ALU.mult, pKr + [EKc], [kTk])
                        if inph2:
                            _, pN, pNr = psum()
                            P.mm(pN[:, 0:129], Gm[:], VP[:, ck, :], True, False, [Gm, VP], pNr, r=True)
                            P.mm(pN[:, 0:129], qf[hrows, tk], CS[hrows, hp2, :].bitcast(F32), False, True, [qf, CS], pNr)
                        _, pS, pSr = psum()
                        P.mm(pS[hrows, 0:129], ident[hrows, hrows], CS[hrows, hp2, :].bitcast(F32), True, False, [cst, CS], pSr)
                        P.mm(pS[hrows, 0:129], kTk[:], VP[:, ck, :], False, True, [kTk, VP], pSr, r=True)
                        if inph2:
                            P.tt(sm[:, 0:1], pN[:, 128:129], EAc[:, ck, h:h + 1], ALU.mult, pNr + [EAc], [sm])
                            P.act(sm[:, 1:2], sm[:, 0:1], AF.Abs, [sm], [sm])
                            P.ts(sm[:, 1:2], sm[:, 1:2], 1.0, ALU.max, [sm], [sm])
                            P.recip(sm[:, 2:3], sm[:, 1:2], [sm], [sm])
                            P.tt(sm[:, 3:4], sm[:, 2:3], EAc[:, ck, h:h + 1], ALU.mult, [sm, EAc], [sm])
                            P.ts(hh[:], pN[:, 0:128], sm[:, 3:4], ALU.mult, pNr + [sm], [hh])
                            P.rsum(sm[:, 4:5], hh[:], [hh], [sm])
                            P.ts(sm[:, 5:6], sm[:, 4:5], 1.0 / 128, ALU.mult, [sm], [sm])
                            P.ts(hn[:], hh[:], sm[:, 5:6], ALU.subtract, [hh, sm], [hn])
                            P.tt(hsq[:], hn[:], hn[:], ALU.mult, [hn], [hsq])
                            P.rsum(sm[:, 6:7], hsq[:], [hsq], [sm])
                            P.act(sm[:, 7:8], sm[:, 6:7], AF.Sqrt, [sm, epsM], [sm], scale=1.0 / 128, bias=epsM[:, 0:1])
                            P.recip(sm[:, 8:9], sm[:, 7:8], [sm], [sm])
                            P.ts(hn[:], hn[:], sm[:, 8:9], ALU.mult, [hn, sm], [hn])
                            _, pT_, pTr = psum()
                            P.mm(pT_[:, 0:128], hn[:], identR, True, True, [hn, cstR], pTr, r=True)
                            P.stt(ymT[:, h, q0 + ck * 128:q0 + (ck + 1) * 128], pT_[:, 0:128], mhw[:, h:h + 1], so[:, tk],
                                  ALU.mult, ALU.mult, pTr + [mhw, so], [ymT])
                        P.ts(CS[hrows, hp2, :], pS[hrows, 0:129], EALc[hrows, ck, h:h + 1], ALU.mult, pSr + [EALc], [CS])
        if debug:
            dtile = sbt("dtile", [128, NPH2])
            for i, src in enumerate([yrT, ymT]):
                for j in range(8):
                    P.copy(dtile[:], src[:, j, :], [src], [dtile])
                    off = (i * 8 + j) * NPH2
                    P.dma("sp", lambda e, off=off: e.dma_start(out=dbg_d[:, off:off + NPH2], in_=dtile[:]), "dbg", reads=[dtile])
        P.finish()
        P.emit()

    for r_ in PSR + [t_.r for t_ in [yrT, ymT, cst, xb] + wslot]:
        r_.lw = None
        r_.rd = {}

    with ExitStack() as es:
        def sbt(name, shape, dt=F32):
            return Tile(es.enter_context(nc.sbuf_tensor("sb_" + name, list(shape), dt)))

        P = Prog(nc, "b")
        epsL = sbt("epsL", [128, 1]); P.memset(epsL[:], LN_EPS, [epsL])
        bgate = sbt("bgate", [128, 32]); cload(P, bgate, bgate_d)
        lnp = sbt("lnp", [128, 64]); cload(P, lnp, lnp_d)
        WR = sbt("WR", [128, KC, 36]); cload(P, WR, WR_d.rearrange("p (k m) -> p k m", m=36))
        rb = sbt("rb", [128, 36]); cload(P, rb, rb_d)
        sele = sbt("sele", [32, 32, 128]); cload(P, sele, sele_d.rearrange("p (e m) -> p e m", m=128))
        ZB = sbt("ZB", [128, KC, TM])
        mrg = sbt("mrg", [128, KC, TM], BF16)
        x1T = sbt("x1T", [128, KC, TM], BF16)
        pTb = sbt("pTb", [128, 2, TM], BF16)
        hT = sbt("hT", [128, 4, TM], BF16)
        TS = [sbt("ts%d" % i, [128, TM]) for i in range(6)]
        COEFT = sbt("COEFT", [32, TM])
        cbt = sbt("cbt", [128, TM])
        LG = sbt("LG", [128, 36]); R8 = sbt("R8", [128, 40]); LE2 = sbt("LE2", [128, 32]); OH1 = sbt("OH1", [128, 32])
        OH2 = sbt("OH2", [128, 32]); COEF = sbt("COEF", [128, 32]); rs_ = sbt("rs_", [128, 16])

        def inproj2(cname):
            w = wload(P, Wp_d[CIDX[cname]])
            wv = w[:].rearrange("p (k m) -> p k m", m=128)
            _, pap, pr = psum()
            for kc in range(KC):
                P.mm(pap[:, 0:TM], wv[:, kc, :], xb[:, kc, :], kc == 0, kc == KC - 1, [w, xb], pr)
            return pap, pr

        def proj(src_d, nk, rhs_t, rhs_fn):
            w = wload(P, src_d) if nk == 16 else None
            return w

        def layernorm(wcol, bcol):
            _, psu, psur = psum()
            _, psq, psqr = psum()
            for j in range(KC):
                sq = TS[j % 2]
                P.act(sq[:], ZB[:, j, :], AF.Square, [ZB], [sq])
                P.mm(psu[:, 0:TM], ones, ZB[:, j, :], j == 0, j == KC - 1, [cst, ZB], psur)
                P.mm(psq[:, 0:TM], ones, sq[:], j == 0, j == KC - 1, [cst, sq], psqr)
            mean, rstd, msq = TS[2], TS[3], TS[4]
            P.act(mean[:], psu[:, 0:TM], AF.Copy, psur, [mean], scale=1.0 / D)
            P.tt(msq[:], mean[:], mean[:], ALU.mult, [mean], [msq])
            P.stt(rstd[:], psq[:, 0:TM], 1.0 / D, msq[:], ALU.mult, ALU.subtract, psqr + [msq], [rstd])
            P.act(rstd[:], rstd[:], AF.Sqrt, [rstd, epsL], [rstd], bias=epsL[:, 0:1])
            P.recip(rstd[:], rstd[:], [rstd], [rstd])
            for j in range(KC):
                d_ = TS[j % 2]
                P.tt(d_[:], ZB[:, j, :], mean[:], ALU.subtract, [ZB, mean], [d_])
                P.tt(d_[:], d_[:], rstd[:], ALU.mult, [d_, rstd], [d_])
                P.act(ZB[:, j, :], d_[:], AF.Identity, [d_, lnp], [ZB], scale=lnp[:, wcol + j:wcol + j + 1], bias=lnp[:, bcol + j:bcol + j + 1])

        for tt_ in (range(NPH2 // TM) if "ph2" in parts else []):
            q0 = tt_ * TM
            g0 = PH2_0 + q0
            P.dma("pool", lambda e, g0=g0: e.dma_start(out=xb[:], in_=xT_d[:, :, g0:g0 + TM]), "xb", writes=[xb])
            P.dma("sp", lambda e, g0=g0: e.dma_start(out=ZB[:], in_=xT_d[:, :, g0:g0 + TM]), "zb", writes=[ZB])
            P.dma("pool", lambda e, q0=q0: e.dma_start(out=pTb[:], in_=pT_d[:, :, q0:q0 + TM]), "ptb", writes=[pTb])
            for j in (range(KC) if lvl >= 1 else []):
                pgr, pgrr = inproj2("gr%d" % j)
                sgr = TS[0]
                P.act(sgr[:], pgr[:, 0:TM], AF.Sigmoid, pgrr + [bgate], [sgr], bias=bgate[:, j:j + 1])
                pgm, pgmr = inproj2("gm%d" % j)
                sgm = TS[1]
                P.act(sgm[:], pgm[:, 0:TM], AF.Sigmoid, pgmr + [bgate], [sgm], bias=bgate[:, 16 + j:17 + j])
                w = wload(P, WBR_d[j], 1024)
                wv = w[:, 0:1024].rearrange("p (k m) -> p k m", m=128)
                _, ppr, pprr = psum()
                for kc in range(8):
                    P.mm(ppr[:, 0:TM], wv[:, kc, :], yrT[:, kc, q0:q0 + TM], kc == 0, kc == 7, [w, yrT], pprr)
                P.tt(sgr[:], sgr[:], ppr[:, 0:TM], ALU.mult, pprr + [sgr], [sgr])
                w = wload(P, WBM_d[j], 1024)
                wv = w[:, 0:1024].rearrange("p (k m) -> p k m", m=128)
                _, ppm, ppmr = psum()
                for kc in range(8):
                    P.mm(ppm[:, 0:TM], wv[:, kc, :], ymT[:, kc, q0:q0 + TM], kc == 0, kc == 7, [w, ymT], ppmr)
                P.tt(sgm[:], sgm[:], ppm[:, 0:TM], ALU.mult, ppmr + [sgm], [sgm])
                P.tt(mrg[:, j, :], sgr[:], sgm[:], ALU.add, [sgr, sgm], [mrg])
            for j in (range(KC) if lvl >= 2 else []):
                w = wload(P, WOUT_d[j])
                wv = w[:].rearrange("p (k m) -> p k m", m=128)
                _, pm, pmr = psum()
                for kc in range(KC):
                    P.mm(pm[:, 0:TM], wv[:, kc, :], mrg[:, kc, :], kc == 0, kc == KC - 1, [w, mrg], pmr)
                P.stt(ZB[:, j, :], ZB[:, j, :], ALPHA, pm[:, 0:TM], ALU.mult, ALU.add, pmr + [ZB], [ZB])
            if lvl >= 3:
                layernorm(0, 16)
            for j in range(KC):
                P.copy(x1T[:, j, :], ZB[:, j, :], [ZB], [x1T])
            for ts_ in (range(TM // 128) if lvl >= 4 else []):
                tk = slice(ts_ * 128, (ts_ + 1) * 128)
                _, pl, plr = psum()
                for j in range(KC):
                    P.mm(pl[:, 0:36], ZB[:, j, tk], WR[:, j, :], j == 0, j == KC - 1, [ZB, WR], plr)
                P.tt(LG[:], pl[:, 0:36], rb[:], ALU.add, plr + [rb], [LG])
                P.rmax(R8[:, 0:1], LG[:, 0:4], [LG], [R8])
                P.ts(R8[:, 4:8], LG[:, 0:4], R8[:, 0:1], ALU.is_equal, [LG, R8], [R8])
                P.ts(R8[:, 1:2], R8[:, 0:1], -1.0, ALU.mult, [R8], [R8])
                P.act(R8[:, 8:12], LG[:, 0:4], AF.Exp, [LG, R8], [R8], bias=R8[:, 1:2])
                P.rsum(R8[:, 2:3], R8[:, 8:12], [R8], [R8])
                P.recip(R8[:, 3:4], R8[:, 2:3], [R8], [R8])
                P.ts(R8[:, 12:16], R8[:, 4:8], -1.0, ALU.add, [R8], [R8], s2=1e30, op1=ALU.mult)
                for g in range(4):
                    P.ts(LE2[:, g * 8:(g + 1) * 8], LG[:, 4 + g * 8:12 + g * 8], R8[:, 12 + g:13 + g], ALU.add, [LG, R8], [LE2])
                P.rmax(R8[:, 16:17], LE2[:], [LE2], [R8])
                P.ts(OH1[:], LE2[:], R8[:, 16:17], ALU.is_equal, [LE2, R8], [OH1])
                P.stt(LE2[:], OH1[:], -1e30, LE2[:], ALU.mult, ALU.add, [OH1, LE2], [LE2])
                P.rmax(R8[:, 17:18], LE2[:], [LE2], [R8])
                P.ts(OH2[:], LE2[:], R8[:, 17:18], ALU.is_equal, [LE2, R8], [OH2])
                P.tt(R8[:, 18:19], R8[:, 17:18], R8[:, 16:17], ALU.subtract, [R8], [R8])
                P.act(R8[:, 19:20], R8[:, 18:19], AF.Exp, [R8], [R8])
                P.ts(R8[:, 20:21], R8[:, 19:20], 1.0, ALU.add, [R8], [R8])
                P.recip(R8[:, 21:22], R8[:, 20:21], [R8], [R8])
                P.tt(R8[:, 22:23], R8[:, 21:22], R8[:, 3:4], ALU.mult, [R8], [R8])
                P.tt(R8[:, 23:24], R8[:, 22:23], R8[:, 19:20], ALU.mult, [R8], [R8])
                P.ts(COEF[:], OH1[:], R8[:, 22:23], ALU.mult, [OH1, R8], [COEF])
                P.stt(COEF[:], OH2[:], R8[:, 23:24], COEF[:], ALU.mult, ALU.add, [OH2, R8, COEF], [COEF])
                _, pct, pctr = psum()
                P.mm(pct[0:32, 0:128], COEF[:], ident, True, True, [COEF, cst], pctr)
                P.copy(COEFT[:, tk], pct[0:32, 0:128], pctr, [COEFT])
            for j in (range(KC) if lvl >= 5 else []):
                w = wload(P, WPG_d[j])
                wv = w[:].rearrange("p (k m) -> p k m", m=128)
                _, pp, ppr_ = psum()
                for kc in range(KC):
                    P.mm(pp[:, 0:TM], wv[:, kc, :], x1T[:, kc, :], kc == 0, kc == KC - 1, [w, x1T], ppr_)
                sg = TS[0]
                P.act(sg[:], pp[:, 0:TM], AF.Sigmoid, ppr_, [sg])
                w = wload(P, WPLE_d[j], 256)
                wv = w[:, 0:256].rearrange("p (k m) -> p k m", m=128)
                _, pq, pqr = psum()
                for kc in range(2):
                    P.mm(pq[:, 0:TM], wv[:, kc, :], pTb[:, kc, :], kc == 0, kc == 1, [w, pTb], pqr)
                P.tt(sg[:], sg[:], pq[:, 0:TM], ALU.mult, pqr + [sg], [sg])
                P.stt(ZB[:, j, :], ZB[:, j, :], ALPHA, sg[:], ALU.mult, ALU.add, [ZB, sg], [ZB])
            for e_ in (range(32) if lvl >= 6 else []):
                _, pcb, pcbr = psum()
                P.mm(pcb[:, 0:TM], sele[:, e_, :], COEFT[:], True, True, [sele, COEFT], pcbr)
                P.act(cbt[:], pcb[:, 0:TM], AF.Copy, pcbr, [cbt])
                for f in range(4):
                    w = wload(P, WG_d[e_, f])
                    wv = w[:].rearrange("p (k m) -> p k m", m=128)
                    _, pg, pgr_ = psum()
                    for kc in range(KC):
                        P.mm(pg[:, 0:TM], wv[:, kc, :], x1T[:, kc, :], kc == 0, kc == KC - 1, [w, x1T], pgr_)
                    w2 = wload(P, WU_d[e_, f])
                    wv2 = w2[:].rearrange("p (k m) -> p k m", m=128)
                    _, pu, pur = psum()
                    for kc in range(KC):
                        P.mm(pu[:, 0:TM], wv2[:, kc, :], x1T[:, kc, :], kc == 0, kc == KC - 1, [w2, x1T], pur)
                    sg = TS[f % 2]
                    P.act(sg[:], pg[:, 0:TM], AF.Silu, pgr_, [sg])
                    P.tt(sg[:], sg[:], pu[:, 0:TM], ALU.mult, pur + [sg], [sg])
                    P.tt(hT[:, f, :], sg[:], cbt[:], ALU.mult, [sg, cbt], [hT])
                for dg in range(4):
                    w = wload(P, WD_d[e_, dg])
                    wv = w[:].rearrange("p (c k m) -> p c k m", c=4, k=4)
                    for dcc in range(4):
                        j = dg * 4 + dcc
                        _, pd, pdr = psum()
                        for kc in range(4):
                            P.mm(pd[:, 0:TM], wv[:, dcc, kc, :], hT[:, kc, :], kc == 0, kc == 3, [w, hT], pdr)
                        P.tt(ZB[:, j, :], ZB[:, j, :], pd[:, 0:TM], ALU.add, pdr + [ZB], [ZB])
            if lvl >= 7:
                layernorm(32, 48)
            P.dma("sp", lambda e, q0=q0: e.dma_start(out=outT_d[:, :, q0:q0 + TM], in_=ZB[:]), "out", reads=[ZB])
        P.finish()
        P.emit()
    return nc


def _pack_lhsT(W):
    K, N = W.shape
    return np.ascontiguousarray(W.reshape(K // 128, 128, N // 128, 128).transpose(2, 1, 0, 3)).reshape(N // 128, 128, K)


def _consts():
    p = np.arange(128)
    ident = np.eye(128, dtype=np.float32)
    bones = (p[:, None] // 64 == p[None, :] // 64).astype(np.float32)
    tri = (p[:, None] <= p[None, :]).astype(np.float32)
    ones = np.ones((128, 128), np.float32)
    cst = np.concatenate([ident, bones, tri, ones, np.zeros((128, 128), np.float32)], axis=1)
    j = (p % 64)[:, None]
    t = np.arange(64)[None, :]
    lt = (j < t).astype(np.float32)
    le = (j <= t).astype(np.float32)
    one = np.ones((128, 64), np.float32)
    zero = np.zeros((128, 64), np.float32)
    mskA = np.concatenate([lt, le, one], axis=1)
    mskB = -mskA
    gt = (j > t).astype(np.float32)
    mskC = np.concatenate([gt, -gt], axis=1)
    i64 = (j == t).astype(np.float32)
    ifull = np.tile(i64, (1, 8))
    sele = np.zeros((32, 32, 128), np.float32)
    for e in range(32):
        sele[e, e, :] = 1.0
    return cst, mskA, mskB, mskC, ifull, sele.reshape(32, 32 * 128)


def _prep_shared(inp):
    g = lambda k: np.asarray(inp[k], dtype=np.float32)[0]
    w_in = g("w_in")
    Wp = np.zeros((NCH, 128, 2048), np.float32)
    for i, (_, c0, n) in enumerate(CHUNKS):
        blk = np.zeros((2048, 128), np.float32)
        blk[:, :n] = w_in[:, c0:c0 + n]
        Wp[i] = blk.reshape(16, 128, 128).transpose(1, 0, 2).reshape(128, 2048)
    sh = {"Wp": Wp}
    sh["W2A"] = np.ascontiguousarray(np.concatenate([g("w_w2"), g("w_a2")], axis=0))
    wg2 = g("w_g2")
    sh["G2a"] = np.ascontiguousarray(wg2[0:128])
    sh["G2b"] = np.ascontiguousarray(wg2[128:160])
    mu = g("mu_shift")
    vecR = np.zeros((128, 8, 10), np.float32)
    rk = g("r_k").reshape(1024)
    for hp in range(8):
        s = slice(hp * 128, (hp + 1) * 128)
        vecR[:, hp, 0] = mu[0:1024][s]
        vecR[:, hp, 1] = mu[1024:2048][s]
        vecR[:, hp, 2] = mu[2048:3072][s]
        vecR[:, hp, 3] = g("w0")[s]
        vecR[:, hp, 4] = g("a0")[s]
        vecR[:, hp, 5] = g("k_k")[s]
        vecR[:, hp, 6] = g("k_a")[s]
        vecR[:, hp, 7] = rk[s]
        vecR[:, hp, 8] = g("lnx_w")[s]
        vecR[:, hp, 9] = g("lnx_b")[s]
    sh["vecR"] = vecR.reshape(128, 80)
    vecL = np.zeros((128, 3), np.float32)
    vecL[:, 0] = mu[3072:3200]
    vecL[:, 1] = mu[3200:3328]
    vecL[0:32, 2] = mu[3328:3360]
    sh["vecL"] = vecL
    cw, cb = g("conv_w"), g("conv_b")
    vecM = np.zeros((128, 8, 5), np.float32)
    for hp2 in range(4):
        for wi, base in ((0, 0), (1, 512)):
            s = slice(base + hp2 * 128, base + (hp2 + 1) * 128)
            ci = hp2 * 2 + wi
            for j in range(4):
                vecM[:, ci, j] = cw[j, s]
            vecM[:, ci, 4] = cb[s]
    sh["vecM"] = vecM.reshape(128, 40)
    sh["mhw"] = np.ascontiguousarray(g("mh_w").reshape(8, 128).T)
    sh["gb"] = np.ascontiguousarray(np.broadcast_to(np.concatenate([g("i_bias"), g("f_bias")])[None, :], (128, 16)))
    sh["bgate"] = np.ascontiguousarray(g("b_gate").reshape(32, 128).T)
    sh["WBR"] = _pack_lhsT(g("w_br"))
    sh["WBM"] = _pack_lhsT(g("w_bm"))
    sh["WOUT"] = _pack_lhsT(g("w_out"))
    sh["WPG"] = _pack_lhsT(g("w_pg"))
    sh["WPLE"] = _pack_lhsT(g("w_ple"))
    wgt = g("w_gate")
    sh["WG"] = np.ascontiguousarray(wgt.reshape(32, 16, 128, 4, 128).transpose(0, 3, 2, 1, 4)).reshape(32, 4, 128, 2048)
    wup = g("w_up")
    sh["WU"] = np.ascontiguousarray(wup.reshape(32, 16, 128, 4, 128).transpose(0, 3, 2, 1, 4)).reshape(32, 4, 128, 2048)
    wdn = g("w_down")
    sh["WD"] = np.ascontiguousarray(wdn.reshape(32, 4, 128, 4, 4, 128).transpose(0, 3, 2, 4, 1, 5)).reshape(32, 4, 128, 2048)
    wr = np.concatenate([g("w_rg"), g("w_re")], axis=1)
    sh["WR"] = np.ascontiguousarray(wr.reshape(16, 128, 36).transpose(1, 0, 2)).reshape(128, 16 * 36)
    sh["rb"] = np.ascontiguousarray(np.broadcast_to(np.concatenate([g("b_rg"), g("b_re")])[None, :], (128, 36)))
    lnp = np.zeros((128, 64), np.float32)
    for i, k in enumerate(("ln1_w", "ln1_b", "ln2_w", "ln2_b")):
        lnp[:, i * 16:(i + 1) * 16] = g(k).reshape(16, 128).T
    sh["lnp"] = lnp
    cst, mskA, mskB, mskC, ifull, sele = _consts()
    sh.update({"cst": cst, "mskA": mskA, "mskB": mskB, "mskC": mskC, "ifull": ifull, "sele": sele})
    return sh


def _prep_core(x, p, b, half, T, NPH2):
    S = x.shape[1]
    end = (half + 1) * NPH2
    start = end - T
    win = np.zeros((T, D), np.float32)
    valid = np.zeros((T,), np.float32)
    s0 = max(start, 0)
    win[s0 - start:] = x[b, s0:end]
    valid[s0 - start:] = 1.0
    xT = np.ascontiguousarray(win.T.reshape(16, 128, T).transpose(1, 0, 2))
    pp = p[0, b, end - NPH2:end]
    pT = np.ascontiguousarray(pp.T.reshape(2, 128, NPH2).transpose(1, 0, 2))
    vch = np.ascontiguousarray(np.broadcast_to(valid.reshape(T // 128, 128)[:, 0][None, :], (128, T // 128)))
    return {"xT": xT, "pT": pT, "valid": vch}


def kernel(**inputs):
    x = np.asarray(inputs["x"], dtype=np.float32)
    p = np.asarray(inputs["p"], dtype=np.float32)
    B, S, _ = x.shape
    T, NPH2 = S, S // 2
    sh = _prep_shared(inputs)
    nc = build(T, NPH2)
    in_maps = []
    for c in range(8):
        m = dict(sh)
        m.update(_prep_core(x, p, c // 2, c % 2, T, NPH2))
        in_maps.append(m)
    res = run_bass_kernel_spmd(nc, in_maps, core_ids=list(range(8)))
    out = np.zeros((B, S, D), np.float32)
    for c in range(8):
        oT = res.results[c]["outT"]
        b, half = c // 2, c % 2
        out[b, half * NPH2:(half + 1) * NPH2, :] = oT.transpose(2, 1, 0).reshape(NPH2, D)
    return out
```

```python
import numpy as np
import concourse.bass as bass
import concourse.mybir as mybir
from concourse.bass_utils import run_bass_kernel_spmd
from contextlib import ExitStack

F32 = mybir.dt.float32
BF16 = mybir.dt.bfloat16
F32R = mybir.dt.float32r
ALU = mybir.AluOpType
AF = mybir.ActivationFunctionType
AX = mybir.AxisListType

D = 2048
KC = 16
ALPHA = 2.0 ** 0.25
LN_EPS = 1e-5
R_GN_EPS = 64e-5
M_NORM_EPS = 1e-6
TM = 512
USE_R = False


class Res:
    __slots__ = ("lw", "rd", "excl")

    def __init__(self, excl=False):
        self.lw = None
        self.rd = {}
        self.excl = excl


class Tile:
    def __init__(self, t):
        self.t = t
        self.r = Res()

    def __getitem__(self, k):
        return self.t[k]


def _base_part(ap):
    bp = ap.base_partition
    return bp() if callable(bp) else bp


def _res(x):
    return x.r if isinstance(x, Tile) else x


class Prog:
    ENG = ("pe", "act", "dve", "pool", "sp")
    SAME_WIN = 3

    def __init__(self, nc, tag):
        self.nc = nc
        self.tag = tag
        self.ops = {e: [] for e in self.ENG}
        self.cnt = {}
        self.clock = {e: {} for e in self.ENG}
        self.snap = {}
        self.sems = {}
        self._pe_free = False
        for e in self.ENG:
            self._mksem(e)

    def _mksem(self, key):
        self.sems[key] = self.nc.alloc_semaphore("s%s_%s" % (self.tag, str(key).replace(" ", "")))
        self.cnt[key] = 0

    def _need(self, eng, ev, waits):
        if ev is None:
            return
        key, val = ev
        if key == eng:
            if eng == "pe" and self._pe_free:
                return
            if self.cnt[eng] + 1 - val <= self.SAME_WIN:
                waits[key] = max(waits.get(key, 0), val)
            return
        if self.clock[eng].get(key, 0) >= val:
            return
        waits[key] = max(waits.get(key, 0), val)

    def _absorb(self, eng, waits):
        ck = self.clock[eng]
        for key, val in waits.items():
            if key == eng:
                continue
            if ck.get(key, 0) < val:
                ck[key] = val
            sn = self.snap.get((key, val))
            if sn:
                for k2, v2 in sn.items():
                    if k2 != eng and ck.get(k2, 0) < v2:
                        ck[k2] = v2

    def _deps(self, eng, reads, writes):
        waits = {}
        for r in reads:
            self._need(eng, _res(r).lw, waits)
        for w in writes:
            w = _res(w)
            self._need(eng, w.lw, waits)
            for k, v in w.rd.items():
                self._need(eng, (k, v), waits)
        return waits

    def op(self, eng, fn, reads=(), writes=()):
        ex = [r for r in reads if _res(r).excl]
        if ex:
            reads = [r for r in reads if not _res(r).excl]
            writes = list(writes) + ex
        waits = self._deps(eng, reads, writes)
        self._absorb(eng, waits)
        self.cnt[eng] += 1
        val = self.cnt[eng]
        self.ops[eng].append((tuple(waits.items()), fn, eng, 1))
        self.snap[(eng, val)] = dict(self.clock[eng])
        for r in reads:
            _res(r).rd[eng] = val
        for w in writes:
            w = _res(w)
            w.lw = (eng, val)
            w.rd = {}

    def dma(self, q, fn, semkey, reads=(), writes=()):
        if semkey not in self.sems:
            self._mksem(semkey)
        waits = self._deps(q, reads, writes)
        if self.cnt[semkey] > 0:
            self._need(q, (semkey, self.cnt[semkey]), waits)
        self._absorb(q, waits)
        self.cnt[semkey] += 16
        val = self.cnt[semkey]
        self.ops[q].append((tuple(waits.items()), fn, semkey, 16))
        self.snap[(semkey, val)] = dict(self.clock[q])
        for r in reads:
            _res(r).rd[semkey] = val
        for w in writes:
            w = _res(w)
            w.lw = (semkey, val)
            w.rd = {}

    def finish(self):
        for e in self.ENG:
            waits = {k: v for k, v in self.cnt.items() if v > 0 and k != e}
            self.ops[e].append((tuple(waits.items()), None, None, 0))

    def emit(self):
        nc = self.nc
        waited = {e: set() for e in self.ENG}
        for e in self.ENG:
            for waits, fn, semkey, inc in self.ops[e]:
                for k, v in waits:
                    if k in waited:
                        waited[k].add(v)
        rank = {e: {v: i + 1 for i, v in enumerate(sorted(waited[e]))} for e in self.ENG}
        with nc.Block() as block:
            def run(e, engobj):
                ci = 0
                for waits, fn, semkey, inc in self.ops[e]:
                    for k, v in waits:
                        engobj.wait_ge(self.sems[k], rank[k][v] if k in rank else v)
                    if fn is None:
                        continue
                    if semkey == e:
                        ci += 1
                        if ci in rank[e]:
                            fn(engobj).then_inc(self.sems[e], 1)
                        else:
                            fn(engobj)
                    else:
                        fn(engobj).then_inc(self.sems[semkey], inc)

            @block.tensor
            def _(eng):
                run("pe", eng)

            @block.scalar
            def _(eng):
                run("act", eng)

            @block.vector
            def _(eng):
                run("dve", eng)

            @block.gpsimd
            def _(eng):
                run("pool", eng)

            @block.sync
            def _(eng):
                run("sp", eng)

    def mm(self, out, lhsT, rhs, start, stop, reads, writes, r=False, free=True):
        self._pe_free = free
        try:
            self._mm(out, lhsT, rhs, start, stop, reads, writes, r)
        finally:
            self._pe_free = False

    def _mm(self, out, lhsT, rhs, start, stop, reads, writes, r=False):
        if r and USE_R and _base_part(out) == 0:
            lhsT = lhsT.bitcast(F32R)
            rhs = rhs.bitcast(F32R)
        elif r:
            lhsT = lhsT.bitcast(F32)
            rhs = rhs.bitcast(F32)
        self.op("pe", lambda e: e.matmul(out, lhsT=lhsT, rhs=rhs, start=start, stop=stop), reads, writes)

    def act(self, out, in_, func, reads, writes, bias=None, scale=None):
        kw = {}
        if bias is not None:
            kw["bias"] = bias
        if scale is not None:
            kw["scale"] = scale
        self.op("act", lambda e: e.activation(out=out, in_=in_, func=func, **kw), reads, writes)

    def tt(self, out, in0, in1, op, reads, writes, eng="dve"):
        self.op(eng, lambda e: e.tensor_tensor(out=out, in0=in0, in1=in1, op=op), reads, writes)

    def ts(self, out, in0, s1, op0, reads, writes, s2=None, op1=None, eng="dve"):
        if op1 is None:
            self.op(eng, lambda e: e.tensor_scalar(out=out, in0=in0, scalar1=s1, scalar2=None, op0=op0), reads, writes)
        else:
            self.op(eng, lambda e: e.tensor_scalar(out=out, in0=in0, scalar1=s1, scalar2=s2, op0=op0, op1=op1), reads, writes)

    def stt(self, out, in0, scalar, in1, op0, op1, reads, writes, eng="dve"):
        self.op(eng, lambda e: e.scalar_tensor_tensor(out=out, in0=in0, scalar=scalar, in1=in1, op0=op0, op1=op1), reads, writes)

    def copy(self, out, in_, reads, writes, eng="dve"):
        self.op(eng, lambda e: e.tensor_copy(out=out, in_=in_), reads, writes)

    def recip(self, out, in_, reads, writes):
        self.op("dve", lambda e: e.reciprocal(out=out, in_=in_), reads, writes)

    def memset(self, out, val, writes, eng="dve"):
        self.op(eng, lambda e: e.memset(out, val), (), writes)

    def rsum(self, out, in_, reads, writes):
        self.op("dve", lambda e: e.reduce_sum(out=out, in_=in_, axis=AX.X), reads, writes)

    def rmax(self, out, in_, reads, writes):
        self.op("dve", lambda e: e.reduce_max(out=out, in_=in_, axis=AX.X), reads, writes)


def _chunks():
    ch = []
    for hp in range(8):
        ch.append(("r%d" % hp, 0 + hp * 128, 128))
        ch.append(("k%d" % hp, 1024 + hp * 128, 128))
        ch.append(("v%d" % hp, 2048 + hp * 128, 128))
    ch.append(("L0", 3072, 128))
    ch.append(("L1", 3200, 128))
    ch.append(("L2", 3328, 32))
    for hp in range(4):
        ch.append(("mq%d" % hp, 3360 + hp * 128, 128))
        ch.append(("mk%d" % hp, 3872 + hp * 128, 128))
    for h in range(8):
        ch.append(("mv%d" % h, 4384 + h * 128, 128))
        ch.append(("mo%d" % h, 5424 + h * 128, 128))
    ch.append(("mg", 5408, 16))
    for j in range(16):
        ch.append(("gr%d" % j, 6448 + j * 128, 128))
        ch.append(("gm%d" % j, 8496 + j * 128, 128))
    return ch


CHUNKS = _chunks()
CIDX = {c[0]: i for i, c in enumerate(CHUNKS)}
NCH = len(CHUNKS)


def build(T, NPH2, debug=False, parts=("lora", "rwkv", "mlstm", "ph2"), lvl=9):
    NMT = T // TM
    PH2_0 = T - NPH2
    nc = bass.Bass("TRN2", target_bir_lowering=False)

    def din(name, shape):
        return nc.dram_tensor(name, list(shape), F32, kind="ExternalInput").ap()

    xT_d = din("xT", [128, KC, T])
    pT_d = din("pT", [128, 2, NPH2])
    valid_d = din("valid", [128, T // 128])
    Wp_d = din("Wp", [NCH, 128, 2048])
    W2A_d = din("W2A", [128, 1024])
    G2a_d = din("G2a", [128, 1024])
    G2b_d = din("G2b", [32, 1024])
    vecR_d = din("vecR", [128, 80])
    vecL_d = din("vecL", [128, 3])
    vecM_d = din("vecM", [128, 40])
    mhw_d = din("mhw", [128, 8])
    gb_d = din("gb", [128, 16])
    bgate_d = din("bgate", [128, 32])
    WBR_d = din("WBR", [16, 128, 1024])
    WBM_d = din("WBM", [16, 128, 1024])
    WOUT_d = din("WOUT", [16, 128, 2048])
    WPG_d = din("WPG", [16, 128, 2048])
    WPLE_d = din("WPLE", [16, 128, 256])
    WG_d = din("WG", [32, 4, 128, 2048])
    WU_d = din("WU", [32, 4, 128, 2048])
    WD_d = din("WD", [32, 4, 128, 2048])
    WR_d = din("WR", [128, KC * 36])
    rb_d = din("rb", [128, 36])
    lnp_d = din("lnp", [128, 64])
    cst_d = din("cst", [128, 128 * 5])
    mskA_d = din("mskA", [128, 192])
    mskB_d = din("mskB", [128, 192])
    mskC_d = din("mskC", [128, 128])
    ifull_d = din("ifull", [128, 512])
    sele_d = din("sele", [32, 32 * 128])
    outT_d = nc.dram_tensor("outT", [128, KC, NPH2], F32, kind="ExternalOutput").ap()
    if debug:
        dbg_d = nc.dram_tensor("dbg", [128, 2 * 8 * NPH2], F32, kind="ExternalOutput").ap()

    def sb(name, shape, dt=F32):
        return Tile(nc.alloc_sbuf_tensor("sb_" + name, list(shape), dt))

    PS = nc.alloc_psum_tensor("PS", [128, 4096], F32)
    PSR = [Res(excl=True) for _ in range(8)]
    yrT = sb("yrT", [128, 8, NPH2], BF16)
    ymT = sb("ymT", [128, 8, NPH2], BF16)
    cst = sb("cst", [128, 640])
    ident = cst[:, 0:128]
    bones = cst[:, 128:256]
    tri = cst[:, 256:384]
    ones = cst[:, 384:512]
    NW = 4
    wslot = [sb("wslot%d" % i, [128, 2048], BF16) for i in range(NW)]
    xb = sb("xb", [128, KC, TM], BF16)

    state = {"ps": 0, "w": 0}

    def psum(nb=1):
        i = state["ps"]
        if i + nb > 8:
            i = 0
        state["ps"] = (i + nb) % 8
        return i, PS[:, i * 512:(i + nb) * 512], PSR[i:i + nb]

    def wload(P, src, ncols=2048):
        i = state["w"]
        state["w"] = (i + 1) % NW
        t = wslot[i]
        P.dma("pool", lambda e: e.dma_start(out=t[:, 0:ncols], in_=src), ("w", i), writes=[t])
        return t

    def cload(P, tile, src, q="sp", key="c"):
        P.dma(q, lambda e: e.dma_start(out=tile[:], in_=src), key, writes=[tile])

    with ExitStack() as es:
        def sbt(name, shape, dt=F32):
            return Tile(es.enter_context(nc.sbuf_tensor("sb_" + name, list(shape), dt)))

        P = Prog(nc, "a")
        cload(P, cst, cst_d)
        W2A = sbt("W2A", [128, 1024], BF16)
        G2a = sbt("G2a", [128, 1024], BF16)
        G2b = sbt("G2b", [32, 1024], BF16)
        cload(P, W2A, W2A_d, "pool", "c2")
        cload(P, G2a, G2a_d, "pool", "c2")
        cload(P, G2b, G2b_d, "pool", "c2")
        vecR = sbt("vecR", [128, 80]); cload(P, vecR, vecR_d)
        vecL = sbt("vecL", [128, 3]); cload(P, vecL, vecL_d)
        vecM = sbt("vecM", [128, 40]); cload(P, vecM, vecM_d)
        mhw = sbt("mhw", [128, 8]); cload(P, mhw, mhw_d)
        gb = sbt("gb", [128, 16]); cload(P, gb, gb_d)
        valid = sbt("valid", [128, T // 128]); cload(P, valid, valid_d)
        mskA = sbt("mskA", [128, 192]); cload(P, mskA, mskA_d)
        mskB = sbt("mskB", [128, 192]); cload(P, mskB, mskB_d)
        mskC = sbt("mskC", [128, 128]); cload(P, mskC, mskC_d)
        ifull = sbt("ifull", [128, 512]); cload(P, ifull, ifull_d)
        cstR = sbt("cstR", [128, 512], F32R)
        P.copy(cstR[:], cst[:, 0:512], [cst], [cstR])
        identR = cstR[:, 0:128]
        bonesR = cstR[:, 128:256]
        triR = cstR[:, 256:384]
        onesR = cstR[:, 384:512]
        scanm = sbt("scanm", [128, 512])
        P.memset(scanm[:], 1.0, [scanm])
        P.memset(scanm[:].rearrange("p (c l) -> p c l", l=64)[:, :, 0:1], 0.0, [scanm])

        ST = sbt("ST", [128, 8, 64], F32R)
        P.memset(ST[:].bitcast(F32), 0.0, [ST])
        CS = sbt("CS", [128, 4, 129], F32R)
        P.memset(CS[:].bitcast(F32), 0.0, [CS])
        carry = sbt("carry", [128, 32])
        P.memset(carry[:], 0.0, [carry])
        ccarry = sbt("ccarry", [128, 8, 3])
        P.memset(ccarry[:], 0.0, [ccarry])

        NSC = 15
        SC = [sbt("sc%d" % i, [128, 516]) for i in range(NSC)]
        epsR = sbt("epsR", [128, 1]); P.memset(epsR[:], R_GN_EPS, [epsR])
        epsM = sbt("epsM", [128, 1]); P.memset(epsM[:], M_NORM_EPS, [epsM])
        oneC = sbt("oneC", [128, 1]); P.memset(oneC[:], 1.0, [oneC])
        TL = sbt("TL", [128, TM], BF16)
        SG0 = sbt("SG0", [128, TM], BF16)
        SG1 = sbt("SG1", [32, TM], BF16)
        R3 = sbt("R3", [128, 8, 192], F32R)
        K2 = sbt("K2", [128, 8, 128], F32R)
        EA = sbt("EA", [128, 4, 2, 192], F32R)
        EB = sbt("EB", [128, 4, 2, 192], F32R)
        EC = sbt("EC", [128, 4, 2, 128], F32R)
        XA = sbt("XA", [128, 4, 2, 64], F32R); XTA = sbt("XTA", [128, 4, 2, 64], F32R)
        XB = sbt("XB", [128, 4, 2, 64], F32R); XTB = sbt("XTB", [128, 4, 2, 64], F32R)
        PM = sbt("PM", [128, 4, 2, 64], F32R)
        VmT = sbt("VmT", [128, 4, 2, 64], F32R)
        RT = sbt("RT", [128, 2, 64], F32R); UT = sbt("UT", [128, 2, 64], F32R)
        P.memset(R3[:].bitcast(F32), 0.0, [R3])
        for c in range(8):
            P.copy(R3[0:64, c, 128:192], ident[0:64, 0:64], [cst], [R3])
            P.copy(R3[64:128, c, 128:192], ident[64:128, 64:128], [cst], [R3])
        GI = sbt("GI", [128, 16]); EL = sbt("EL", [128, 8]); LL = sbt("LL", [128, 8], F32R)
        EAc = sbt("EAc", [128, 4, 8]); EKc = sbt("EKc", [128, 4, 8]); EALc = sbt("EALc", [128, 4, 8])
        tm8 = sbt("tm8", [128, 8])
        VP = sbt("VP", [128, 4, 129], F32R)
        Gm = sbt("Gm", [128, 128], F32R); kTk = sbt("kTk", [128, 64], F32R); hh = sbt("hh", [128, 128]); hn = sbt("hn", [128, 128], F32R)
        hsq = sbt("hsq", [128, 128])
        sm = sbt("sm", [128, 16])

        def inproj(cname, ncols=128, n0=0, nn=TM):
            w = wload(P, Wp_d[CIDX[cname]])
            wv = w[:].rearrange("p (k m) -> p k m", m=128)
            pi, pap, pr = psum()
            for kc in range(KC):
                P.mm(pap[0:ncols, 0:nn], wv[:, kc, 0:ncols], xb[:, kc, n0:n0 + nn], kc == 0, kc == KC - 1, [w, xb], pr)
            return pap, pr

        def shifted(cname, ci, mu_ap, zt, out_t, ncols=128, rnd=False):
            pap, pr = inproj(cname, ncols)
            P.copy(zt[0:ncols, 0:1], carry[0:ncols, ci:ci + 1], [carry], [zt])
            P.act(zt[0:ncols, 1:TM + 1], pap[0:ncols, 0:TM], AF.Copy, pr, [zt])
            P.copy(carry[0:ncols, ci:ci + 1], zt[0:ncols, TM:TM + 1], [zt], [carry])
            P.tt(out_t[0:ncols, 0:TM], zt[0:ncols, 0:TM], zt[0:ncols, 1:TM + 1], ALU.subtract, [zt], [out_t])
            oo = out_t[0:ncols, 0:TM].bitcast(F32R) if rnd else out_t[0:ncols, 0:TM]
            P.stt(oo, out_t[0:ncols, 0:TM], mu_ap, zt[0:ncols, 1:TM + 1], ALU.mult, ALU.add, [out_t, zt, vecR, vecL], [out_t])

        def bmm(in_ap, in_t):
            pi, pap, pr = psum()
            P.mm(pap[:, 0:TM], bones, in_ap, True, True, [cst, in_t], pr)
            return pap, pr

        for mt in range(NMT):
            t0 = mt * TM
            P.dma("pool", lambda e, t0=t0: e.dma_start(out=xb[:], in_=xT_d[:, :, t0:t0 + TM]), "xb", writes=[xb])
            inph2 = t0 >= PH2_0
            need_carry = (t0 + TM >= PH2_0)
            q0 = t0 - PH2_0
            z, o = SC[0], SC[1]
            shifted("L0", 24, vecL[:, 0:1], z, o)
            P.act(TL[0:64, :], o[0:64, 0:TM], AF.Tanh, [o], [TL])
            P.copy(TL[64:128, :], o[64:128, 0:TM], [o], [TL])
            if need_carry:
                shifted("L1", 25, vecL[:, 1:2], z, o)
                P.act(SG0[:, :], o[:, 0:TM], AF.Sigmoid, [o], [SG0])
                shifted("L2", 26, vecL[0:32, 2:3], z, o, ncols=32)
                P.act(SG1[:, :], o[0:32, 0:TM], AF.Sigmoid, [o], [SG1])
            for hp in (range(8) if "rwkv" in parts else []):
                cs = slice(hp * 128, (hp + 1) * 128)
                vr = lambda i: vecR[:, hp * 10 + i:hp * 10 + i + 1]
                rs, ks, vs = SC[2], SC[3], SC[4]
                if need_carry:
                    shifted("r%d" % hp, hp * 3 + 0, vr(0), SC[0], rs)
                shifted("k%d" % hp, hp * 3 + 1, vr(1), SC[0], ks)
                shifted("v%d" % hp, hp * 3 + 2, vr(2), SC[0], vs)
                _, pw, pwr = psum()
                P.mm(pw[:, 0:TM], W2A[0:64, cs], TL[0:64, :], True, True, [W2A, TL], pwr)
                _, pa, par_ = psum()
                P.mm(pa[:, 0:TM], W2A[64:128, cs], TL[64:128, :], True, True, [W2A, TL], par_)
                lw, aa, gg = SC[5], SC[6], SC[7]
                if inph2:
                    _, pg, pgr = psum()
                    P.mm(pg[:, 0:TM], G2a[:, cs], SG0[:, :], True, False, [G2a, SG0], pgr)
                    P.mm(pg[:, 0:TM], G2b[:, cs], SG1[:, :], False, True, [G2b, SG1], pgr)
                    P.act(gg[:, 0:TM], pg[:, 0:TM], AF.Copy, pgr, [gg])
                P.act(lw[:, 0:TM], pw[:, 0:TM], AF.Sigmoid, pwr + [vecR], [lw], bias=vr(3))
                P.ts(lw[:, 0:TM], lw[:, 0:TM], -float(np.exp(-0.5)), ALU.mult, [lw], [lw])
                P.act(aa[:, 0:TM], pa[:, 0:TM], AF.Sigmoid, par_ + [vecR], [aa], bias=vr(4))
                kk, sq, kap = SC[8], SC[9], SC[10]
                P.ts(kk[:, 0:TM], ks[:, 0:TM], vr(5), ALU.mult, [ks, vecR], [kk])
                P.tt(sq[:, 0:TM], kk[:, 0:TM], kk[:, 0:TM], ALU.mult, [kk], [sq])
                pss, pssr = bmm(sq[:, 0:TM], sq)
                P.act(sq[:, 0:TM], pss[:, 0:TM], AF.Sqrt, pssr, [sq])
                P.ts(sq[:, 0:TM], sq[:, 0:TM], 1e-12, ALU.max, [sq], [sq])
                P.recip(sq[:, 0:TM], sq[:, 0:TM], [sq], [sq])
                P.tt(kap[:, 0:TM], kk[:, 0:TM], sq[:, 0:TM], ALU.mult, [kk, sq], [kap])
                km, beta = SC[11], SC[12]
                P.ts(km[:, 0:TM], aa[:, 0:TM], -1.0, ALU.add, [aa, vecR], [km], s2=vr(6), op1=ALU.mult)
                P.stt(km[:, 0:TM], km[:, 0:TM], 1.0, ks[:, 0:TM], ALU.add, ALU.mult, [km, ks], [km])
                P.tt(beta[:, 0:TM], aa[:, 0:TM], kap[:, 0:TM], ALU.mult, [aa, kap], [beta])
                bon = SC[13]
                if inph2:
                    P.stt(bon[:, 0:TM], rs[:, 0:TM], vr(7), km[:, 0:TM], ALU.mult, ALU.mult, [rs, km, vecR], [bon])
                    pb, pbr = bmm(bon[:, 0:TM], bon)
                    P.tt(bon[:, 0:TM], pb[:, 0:TM], vs[:, 0:TM], ALU.mult, pbr + [vs], [bon])
                cc, ep, en, epv = SC[8], SC[14], SC[6], SC[9]
                P.op("dve", lambda e, cc=cc, lw=lw: e.tensor_tensor_scan(out=cc[:, 0:TM], data0=scanm[:, 0:TM], data1=lw[:, 0:TM],
                                                                         initial=0.0, op0=ALU.mult, op1=ALU.add), [scanm, lw], [cc])
                P.act(ep[:, 0:TM], cc[:, 0:TM], AF.Exp, [cc], [ep])
                P.act(en[:, 0:TM], cc[:, 0:TM], AF.Exp, [cc], [en], scale=-1.0)
                P.tt(epv[:, 0:TM], cc[:, 0:TM], lw[:, 0:TM], ALU.subtract, [cc, lw], [epv])
                P.act(epv[:, 0:TM], epv[:, 0:TM], AF.Exp, [epv], [epv])
                c3 = lambda t_: t_[:, 0:TM].rearrange("p (c l) -> p c l", l=64)
                P.tt(R3[:, :, 0:64], c3(kap), c3(epv), ALU.mult, [kap, epv], [R3])
                if inph2:
                    P.tt(R3[:, :, 64:128], c3(rs), c3(ep), ALU.mult, [rs, ep], [R3])
                P.tt(K2[:, :, 0:64], c3(km), c3(en), ALU.mult, [km, en], [K2])
                P.tt(K2[:, :, 64:128], c3(beta), c3(en), ALU.mult, [beta, en], [K2])
                for dc in range(4):
                    _, pt, ptr = psum()
                    P.mm(pt[:, 0:128], vs[:, dc * 128:(dc + 1) * 128], ident, True, True, [vs, cst], ptr)
                    P.copy(VmT[:, dc, :, :], pt[:, 0:128].rearrange("p (h v) -> p h v", v=64), ptr, [VmT])
                hr = lambda h: slice(h * 64, (h + 1) * 64)
                pr_ = lambda c: slice((c % 2) * 64, (c % 2) * 64 + 64)
                _, pA, pAr = psum(4)
                pAv = pA.rearrange("p (d h w) -> p d h w", d=4, h=2)
                for h in range(2):
                    for c in range(8):
                        P.mm(pAv[pr_(c), c // 2, h, 0:192], K2[hr(h), c, 0:64], R3[hr(h), c, 0:192], True, True, [K2, R3], pAr, r=True, free=(c > 0))
                for d_ in range(4):
                    for h in range(2):
                        P.tt(EA[:, d_, h, :], pAv[:, d_, h, 0:192], mskA[:], ALU.mult, pAr + [mskA], [EA])
                _, pB, pBr = psum(4)
                pBv = pB.rearrange("p (d h w) -> p d h w", d=4, h=2)
                for h in range(2):
                    for c in range(8):
                        P.mm(pBv[pr_(c), c // 2, h, 0:192], K2[hr(h), c, 64:128], R3[hr(h), c, 0:192], True, True, [K2, R3], pBr, r=True, free=(c > 0))
                for d_ in range(4):
                    for h in range(2):
                        P.tt(EB[:, d_, h, :], pBv[:, d_, h, 0:192], mskB[:], ALU.mult, pBr + [mskB], [EB])
                _, pC, pCr = psum(2)
                pCv = pC.rearrange("p (d h w) -> p d h w", d=4, h=2)
                for h in range(2):
                    for c in range(8):
                        P.mm(pCv[pr_(c), c // 2, h, 0:128], R3[hr(h), c, 0:64], K2[hr(h), c, 0:128], True, True, [K2, R3], pCr, r=True, free=(c > 0))
                for d_ in range(4):
                    for h in range(2):
                        P.tt(EC[:, d_, h, :], pCv[:, d_, h, :], mskC[:], ALU.mult, pCr + [mskC], [EC])
                X = (EB, lambda c, h: EB[pr_(c), c // 2, h, 0:64])
                XT = (EC, lambda c, h: EC[pr_(c), c // 2, h, 64:128])
                P.tt(PM[:], EB[:, :, :, 0:64], ifull[:].rearrange("p (d h w) -> p d h w", d=4, h=2), ALU.add, [EB, ifull], [PM])
                bufs = [(XA, XTA), (XB, XTB)]
                for it in range(5):
                    nX, nXT = bufs[it % 2]
                    last = it == 4
                    _, p2, p2r = psum()
                    p2v = p2.rearrange("p (d h w) -> p d h w", d=4, h=2)
                    for c in range(8):
                        for h in range(2):
                            P.mm(p2v[pr_(c), c // 2, h, :], X[1](c, h), XT[1](c, h), True, True, [X[0], XT[0]], p2r, r=True)
                    P.act(nXT[:].rearrange("p d h w -> p (d h w)"), p2, AF.Copy, p2r, [nXT])
                    if not last:
                        _, p1, p1r = psum()
                        p1v = p1.rearrange("p (d h w) -> p d h w", d=4, h=2)
                        for c in range(8):
                            for h in range(2):
                                P.mm(p1v[pr_(c), c // 2, h, :], XT[1](c, h), X[1](c, h), True, True, [X[0], XT[0]], p1r, r=True)
                        P.act(nX[:].rearrange("p d h w -> p (d h w)"), p1, AF.Copy, p1r, [nX])
                    _, p3, p3r = psum()
                    p3v = p3.rearrange("p (d h w) -> p d h w", d=4, h=2)
                    for c in range(8):
                        for h in range(2):
                            P.mm(p3v[pr_(c), c // 2, h, :], nXT[pr_(c), c // 2, h, :], PM[pr_(c), c // 2, h, :], True, True, [nXT, PM], p3r, r=True)
                    P.tt(PM[:].rearrange("p d h w -> p (d h w)"), PM[:].rearrange("p d h w -> p (d h w)"), p3, ALU.add, p3r + [PM], [PM])
                    X = (nX, lambda c, h, nX=nX: nX[pr_(c), c // 2, h, :])
                    XT = (nXT, lambda c, h, nXT=nXT: nXT[pr_(c), c // 2, h, :])
                yb = SC[3]
                for c in range(8):
                    rows = pr_(c)
                    dc = c // 2
                    _, p1, p1r = psum()
                    for h in range(2):
                        P.mm(p1[rows, h * 64:(h + 1) * 64], R3[hr(h), c, 0:64], ST[hr(h), hp, :], True, False, [R3, ST], p1r, r=True, free=False)
                        P.mm(p1[rows, h * 64:(h + 1) * 64], EA[rows, dc, h, 0:64], VmT[rows, dc, h, :], False, True, [EA, VmT], p1r, r=True, free=False)
                    P.act(RT[rows, :, :], p1[rows, 0:128].rearrange("p (h v) -> p h v", v=64), AF.Copy, p1r, [RT])
                    _, p2, p2r = psum()
                    for h in range(2):
                        P.mm(p2[rows, h * 64:(h + 1) * 64], PM[rows, dc, h, :], RT[rows, h, :], True, True, [PM, RT], p2r, r=True, free=(h == 1))
                    P.copy(UT[rows, :, :], p2[rows, 0:128].rearrange("p (h v) -> p h v", v=64), p2r, [UT])
                    if inph2:
                        _, pY, pYr = psum()
                    _, pS, pSr = psum()
                    for h in range(2):
                        if inph2:
                            P.mm(pY[hr(h), 0:64], ST[hr(h), hp, :], R3[hr(h), c, 64:128], True, False, [ST, R3], pYr, r=True, free=False)
                            P.mm(pY[hr(h), 0:64], VmT[rows, dc, h, :], EA[rows, dc, h, 64:128], False, False, [VmT, EA], pYr, r=True, free=False)
                            P.mm(pY[hr(h), 0:64], UT[rows, h, :], EB[rows, dc, h, 64:128], False, True, [UT, EB], pYr, r=True, free=True)
                        idh = ident[hr(h), hr(h)]
                        P.mm(pS[hr(h), 0:64], idh, ST[hr(h), hp, :].bitcast(F32), True, False, [cst, ST], pSr, free=False)
                        P.mm(pS[hr(h), 0:64], EA[rows, dc, h, 128:192], VmT[rows, dc, h, :], False, False, [EA, VmT], pSr, r=True, free=False)
                        P.mm(pS[hr(h), 0:64], EB[rows, dc, h, 128:192], UT[rows, h, :], False, True, [EB, UT], pSr, r=True, free=True)
                    if inph2:
                        P.act(yb[:, c * 64:(c + 1) * 64], pY[:, 0:64], AF.Copy, pYr, [yb])
                    P.ts(ST[:, hp, :], pS[:, 0:64], ep[:, c * 64 + 63:c * 64 + 64], ALU.mult, pSr + [ep], [ST])
                if inph2:
                    pm, pmr = bmm(yb[:, 0:TM], yb)
                    mean, dd, var = SC[10], SC[11], SC[12]
                    P.act(mean[:, 0:TM], pm[:, 0:TM], AF.Copy, pmr, [mean], scale=1.0 / 64)
                    P.tt(dd[:, 0:TM], yb[:, 0:TM], mean[:, 0:TM], ALU.subtract, [yb, mean], [dd])
                    P.tt(var[:, 0:TM], dd[:, 0:TM], dd[:, 0:TM], ALU.mult, [dd], [var])
                    pq, pqr = bmm(var[:, 0:TM], var)
                    P.act(var[:, 0:TM], pq[:, 0:TM], AF.Sqrt, pqr + [epsR], [var], scale=1.0 / 64, bias=epsR[:, 0:1])
                    P.recip(var[:, 0:TM], var[:, 0:TM], [var], [var])
                    P.tt(dd[:, 0:TM], dd[:, 0:TM], var[:, 0:TM], ALU.mult, [dd, var], [dd])
                    P.act(dd[:, 0:TM], dd[:, 0:TM], AF.Identity, [dd, vecR], [dd], scale=vr(8), bias=vr(9))
                    P.tt(dd[:, 0:TM], dd[:, 0:TM], bon[:, 0:TM], ALU.add, [dd, bon], [dd])
                    P.tt(yrT[:, hp, q0:q0 + TM], dd[:, 0:TM], gg[:, 0:TM], ALU.mult, [dd, gg], [yrT])
            wmg = wload(P, Wp_d[CIDX["mg"]])
            wmgv = wmg[:].rearrange("p (k m) -> p k m", m=128)
            for ck in (range(4) if "mlstm" in parts else []):
                _, pgt, pgtr = psum()
                for kc in range(KC):
                    P.mm(pgt[:, 0:16], xb[:, kc, ck * 128:(ck + 1) * 128], wmgv[:, kc, 0:16], kc == 0, kc == KC - 1, [xb, wmg], pgtr)
                P.tt(GI[:], pgt[:, 0:16], gb[:], ALU.add, pgtr + [gb], [GI])
                P.act(EL[:], GI[:, 8:16], AF.Exp, [GI], [EL], scale=-1.0)
                P.act(LL[:], EL[:], AF.Ln, [EL, oneC], [LL], bias=oneC[:, 0:1])
                _, pc, pcr = psum()
                P.mm(pc[:, 0:8], triR, LL[:], True, True, [cstR, LL], pcr, r=True)
                P.mm(pc[:, 8:16], onesR, LL[:], True, True, [cstR, LL], pcr, r=True)
                P.act(EAc[:, ck, :], pc[:, 0:8], AF.Exp, pcr, [EAc], scale=-1.0)
                P.tt(tm8[:], GI[:, 0:8], pc[:, 0:8], ALU.add, pcr + [GI], [tm8])
                P.act(EKc[:, ck, :], tm8[:], AF.Exp, [tm8], [EKc])
                P.act(tm8[:], pc[:, 8:16], AF.Exp, pcr, [tm8], scale=-1.0)
                gck = mt * 4 + ck
                P.ts(EALc[:, ck, :], tm8[:], valid[:, gck:gck + 1], ALU.mult, [tm8, valid], [EALc])
            for hp2 in (range(4) if "mlstm" in parts else []):
                qf, kf = SC[2], SC[3]
                for which, dst in (("mq", qf), ("mk", kf)):
                    if which == "mq" and not need_carry:
                        continue
                    ci = hp2 * 2 + (0 if which == "mq" else 1)
                    vm = lambda i: vecM[:, ci * 5 + i:ci * 5 + i + 1]
                    pap, pr = inproj("%s%d" % (which, hp2))
                    zc = SC[0]
                    P.copy(zc[:, 0:3], ccarry[:, ci, :], [ccarry], [zc])
                    P.act(zc[:, 3:TM + 3], pap[:, 0:TM], AF.Copy, pr, [zc])
                    P.copy(ccarry[:, ci, :], zc[:, TM:TM + 3], [zc], [ccarry])
                    acc = SC[1]
                    P.ts(acc[:, 0:TM], zc[:, 0:TM], vm(0), ALU.mult, [zc, vecM], [acc], s2=vm(4), op1=ALU.add)
                    for j in range(1, 4):
                        P.stt(acc[:, 0:TM], zc[:, j:j + TM], vm(j), acc[:, 0:TM], ALU.mult, ALU.add, [zc, acc, vecM], [acc])
                    if which == "mk":
                        P.act(dst[:, 0:TM], acc[:, 0:TM], AF.Silu, [acc], [dst])
                        P.ts(dst[:, 0:TM], dst[:, 0:TM], 0.125, ALU.mult, [dst], [dst])
                    else:
                        P.act(dst[:, 0:TM], acc[:, 0:TM], AF.Silu, [acc], [dst])
                for hh_ in range(2):
                    h = hp2 * 2 + hh_
                    hrows = slice(hh_ * 64, hh_ * 64 + 64)
                    so = SC[4]
                    if inph2:
                        pap, pr = inproj("mo%d" % h)
                        P.act(so[:, 0:TM], pap[:, 0:TM], AF.Sigmoid, pr, [so])
                    wv_ = wload(P, Wp_d[CIDX["mv%d" % h]])
                    wvv = wv_[:].rearrange("p (k m) -> p k m", m=128)
                    for ck in range(4):
                        _, pv, pvr = psum()
                        for kc in range(KC):
                            P.mm(pv[:, 0:128], xb[:, kc, ck * 128:(ck + 1) * 128], wvv[:, kc, :], kc == 0, kc == KC - 1, [xb, wv_], pvr)
                        P.act(VP[:, ck, 0:128], pv[:, 0:128], AF.Copy, pvr, [VP])
                    P.memset(VP[:, :, 128:129].bitcast(F32), 1.0, [VP])
                    for ck in range(4):
                        tk = slice(ck * 128, (ck + 1) * 128)
                        if inph2:
                            _, pG, pGr = psum()
                            P.mm(pG[:, 0:128], kf[hrows, tk], qf[hrows, tk], True, True, [kf, qf], pGr)
                            P.stt(Gm[:], pG[:, 0:128], EKc[:, ck, h:h + 1], tri, ALU.mult, ALU.mult, pGr + [EKc, cst], [Gm])
                        _, pK, pKr = psum()
                        P.mm(pK[:, 0:64], kf[hrows, tk], ident[hrows, hrows], True, True, [kf, cst], pKr)
                        P.ts(kTk[:], pK[:, 0:64], EKc[:, ck, h:h + 1], ALU.mult, pKr + [EKc], [kTk])
                        if inph2:
                            _, pN, pNr = psum()
                            P.mm(pN[:, 0:129], Gm[:], VP[:, ck, :], True, False, [Gm, VP], pNr, r=True)
                            P.mm(pN[:, 0:129], qf[hrows, tk], CS[hrows, hp2, :].bitcast(F32), False, True, [qf, CS], pNr)
                        _, pS, pSr = psum()
                        P.mm(pS[hrows, 0:129], ident[hrows, hrows], CS[hrows, hp2, :].bitcast(F32), True, False, [cst, CS], pSr)
                        P.mm(pS[hrows, 0:129], kTk[:], VP[:, ck, :], False, True, [kTk, VP], pSr, r=True)
                        if inph2:
                            P.tt(sm[:, 0:1], pN[:, 128:129], EAc[:, ck, h:h + 1], ALU.mult, pNr + [EAc], [sm])
                            P.act(sm[:, 1:2], sm[:, 0:1], AF.Abs, [sm], [sm])
                            P.ts(sm[:, 1:2], sm[:, 1:2], 1.0, ALU.max, [sm], [sm])
                            P.recip(sm[:, 2:3], sm[:, 1:2], [sm], [sm])
                            P.tt(sm[:, 3:4], sm[:, 2:3], EAc[:, ck, h:h + 1], ALU.mult, [sm, EAc], [sm])
                            P.ts(hh[:], pN[:, 0:128], sm[:, 3:4], ALU.mult, pNr + [sm], [hh])
                            P.rsum(sm[:, 4:5], hh[:], [hh], [sm])
                            P.ts(sm[:, 5:6], sm[:, 4:5], 1.0 / 128, ALU.mult, [sm], [sm])
                            P.ts(hn[:], hh[:], sm[:, 5:6], ALU.subtract, [hh, sm], [hn])
                            P.tt(hsq[:], hn[:], hn[:], ALU.mult, [hn], [hsq])
                            P.rsum(sm[:, 6:7], hsq[:], [hsq], [sm])
                            P.act(sm[:, 7:8], sm[:, 6:7], AF.Sqrt, [sm, epsM], [sm], scale=1.0 / 128, bias=epsM[:, 0:1])
                            P.recip(sm[:, 8:9], sm[:, 7:8], [sm], [sm])
                            P.ts(hn[:], hn[:], sm[:, 8:9], ALU.mult, [hn, sm], [hn])
                            _, pT_, pTr = psum()
                            P.mm(pT_[:, 0:128], hn[:], identR, True, True, [hn, cstR], pTr, r=True)
                            P.stt(ymT[:, h, q0 + ck * 128:q0 + (ck + 1) * 128], pT_[:, 0:128], mhw[:, h:h + 1], so[:, tk],
                                  ALU.mult, ALU.mult, pTr + [mhw, so], [ymT])
                        P.ts(CS[hrows, hp2, :], pS[hrows, 0:129], EALc[hrows, ck, h:h + 1], ALU.mult, pSr + [EALc], [CS])
        if debug:
            dtile = sbt("dtile", [128, NPH2])
            for i, src in enumerate([yrT, ymT]):
                for j in range(8):
                    P.copy(dtile[:], src[:, j, :], [src], [dtile])
                    off = (i * 8 + j) * NPH2
                    P.dma("sp", lambda e, off=off: e.dma_start(out=dbg_d[:, off:off + NPH2], in_=dtile[:]), "dbg", reads=[dtile])
        P.finish()
        P.emit()

    for r_ in PSR + [t_.r for t_ in [yrT, ymT, cst, xb] + wslot]:
        r_.lw = None
        r_.rd = {}

    with ExitStack() as es:
        def sbt(name, shape, dt=F32):
            return Tile(es.enter_context(nc.sbuf_tensor("sb_" + name, list(shape), dt)))

        P = Prog(nc, "b")
        epsL = sbt("epsL", [128, 1]); P.memset(epsL[:], LN_EPS, [epsL])
        bgate = sbt("bgate", [128, 32]); cload(P, bgate, bgate_d)
        lnp = sbt("lnp", [128, 64]); cload(P, lnp, lnp_d)
        WR = sbt("WR", [128, KC, 36]); cload(P, WR, WR_d.rearrange("p (k m) -> p k m", m=36))
        rb = sbt("rb", [128, 36]); cload(P, rb, rb_d)
        sele = sbt("sele", [32, 32, 128]); cload(P, sele, sele_d.rearrange("p (e m) -> p e m", m=128))
        ZB = sbt("ZB", [128, KC, TM])
        mrg = sbt("mrg", [128, KC, TM], BF16)
        x1T = sbt("x1T", [128, KC, TM], BF16)
        pTb = sbt("pTb", [128, 2, TM], BF16)
        hT = sbt("hT", [128, 4, TM], BF16)
        TS = [sbt("ts%d" % i, [128, TM]) for i in range(6)]
        COEFT = sbt("COEFT", [32, TM])
        cbt = sbt("cbt", [128, TM])
        LG = sbt("LG", [128, 36]); R8 = sbt("R8", [128, 40]); LE2 = sbt("LE2", [128, 32]); OH1 = sbt("OH1", [128, 32])
        OH2 = sbt("OH2", [128, 32]); COEF = sbt("COEF", [128, 32]); rs_ = sbt("rs_", [128, 16])

        def inproj2(cname):
            w = wload(P, Wp_d[CIDX[cname]])
            wv = w[:].rearrange("p (k m) -> p k m", m=128)
            _, pap, pr = psum()
            for kc in range(KC):
                P.mm(pap[:, 0:TM], wv[:, kc, :], xb[:, kc, :], kc == 0, kc == KC - 1, [w, xb], pr)
            return pap, pr

        def proj(src_d, nk, rhs_t, rhs_fn):
            w = wload(P, src_d) if nk == 16 else None
            return w

        def layernorm(wcol, bcol):
            _, psu, psur = psum()
            _, psq, psqr = psum()
            for j in range(KC):
                sq = TS[j % 2]
                P.act(sq[:], ZB[:, j, :], AF.Square, [ZB], [sq])
                P.mm(psu[:, 0:TM], ones, ZB[:, j, :], j == 0, j == KC - 1, [cst, ZB], psur)
                P.mm(psq[:, 0:TM], ones, sq[:], j == 0, j == KC - 1, [cst, sq], psqr)
            mean, rstd, msq = TS[2], TS[3], TS[4]
            P.act(mean[:], psu[:, 0:TM], AF.Copy, psur, [mean], scale=1.0 / D)
            P.tt(msq[:], mean[:], mean[:], ALU.mult, [mean], [msq])
            P.stt(rstd[:], psq[:, 0:TM], 1.0 / D, msq[:], ALU.mult, ALU.subtract, psqr + [msq], [rstd])
            P.act(rstd[:], rstd[:], AF.Sqrt, [rstd, epsL], [rstd], bias=epsL[:, 0:1])
            P.recip(rstd[:], rstd[:], [rstd], [rstd])
            for j in range(KC):
                d_ = TS[j % 2]
                P.tt(d_[:], ZB[:, j, :], mean[:], ALU.subtract, [ZB, mean], [d_])
                P.tt(d_[:], d_[:], rstd[:], ALU.mult, [d_, rstd], [d_])
                P.act(ZB[:, j, :], d_[:], AF.Identity, [d_, lnp], [ZB], scale=lnp[:, wcol + j:wcol + j + 1], bias=lnp[:, bcol + j:bcol + j + 1])

        for tt_ in (range(NPH2 // TM) if "ph2" in parts else []):
            q0 = tt_ * TM
            g0 = PH2_0 + q0
            P.dma("pool", lambda e, g0=g0: e.dma_start(out=xb[:], in_=xT_d[:, :, g0:g0 + TM]), "xb", writes=[xb])
            P.dma("sp", lambda e, g0=g0: e.dma_start(out=ZB[:], in_=xT_d[:, :, g0:g0 + TM]), "zb", writes=[ZB])
            P.dma("pool", lambda e, q0=q0: e.dma_start(out=pTb[:], in_=pT_d[:, :, q0:q0 + TM]), "ptb", writes=[pTb])
            for j in (range(KC) if lvl >= 1 else []):
                pgr, pgrr = inproj2("gr%d" % j)
                sgr = TS[0]
                P.act(sgr[:], pgr[:, 0:TM], AF.Sigmoid, pgrr + [bgate], [sgr], bias=bgate[:, j:j + 1])
                pgm, pgmr = inproj2("gm%d" % j)
                sgm = TS[1]
                P.act(sgm[:], pgm[:, 0:TM], AF.Sigmoid, pgmr + [bgate], [sgm], bias=bgate[:, 16 + j:17 + j])
                w = wload(P, WBR_d[j], 1024)
                wv = w[:, 0:1024].rearrange("p (k m) -> p k m", m=128)
                _, ppr, pprr = psum()
                for kc in range(8):
                    P.mm(ppr[:, 0:TM], wv[:, kc, :], yrT[:, kc, q0:q0 + TM], kc == 0, kc == 7, [w, yrT], pprr)
                P.tt(sgr[:], sgr[:], ppr[:, 0:TM], ALU.mult, pprr + [sgr], [sgr])
                w = wload(P, WBM_d[j], 1024)
                wv = w[:, 0:1024].rearrange("p (k m) -> p k m", m=128)
                _, ppm, ppmr = psum()
                for kc in range(8):
                    P.mm(ppm[:, 0:TM], wv[:, kc, :], ymT[:, kc, q0:q0 + TM], kc == 0, kc == 7, [w, ymT], ppmr)
                P.tt(sgm[:], sgm[:], ppm[:, 0:TM], ALU.mult, ppmr + [sgm], [sgm])
                P.tt(mrg[:, j, :], sgr[:], sgm[:], ALU.add, [sgr, sgm], [mrg])
            for j in (range(KC) if lvl >= 2 else []):
                w = wload(P, WOUT_d[j])
                wv = w[:].rearrange("p (k m) -> p k m", m=128)
                _, pm, pmr = psum()
                for kc in range(KC):
                    P.mm(pm[:, 0:TM], wv[:, kc, :], mrg[:, kc, :], kc == 0, kc == KC - 1, [w, mrg], pmr)
                P.stt(ZB[:, j, :], ZB[:, j, :], ALPHA, pm[:, 0:TM], ALU.mult, ALU.add, pmr + [ZB], [ZB])
            if lvl >= 3:
                layernorm(0, 16)
            for j in range(KC):
                P.copy(x1T[:, j, :], ZB[:, j, :], [ZB], [x1T])
            for ts_ in (range(TM // 128) if lvl >= 4 else []):
                tk = slice(ts_ * 128, (ts_ + 1) * 128)
                _, pl, plr = psum()
                for j in range(KC):
                    P.mm(pl[:, 0:36], ZB[:, j, tk], WR[:, j, :], j == 0, j == KC - 1, [ZB, WR], plr)
                P.tt(LG[:], pl[:, 0:36], rb[:], ALU.add, plr + [rb], [LG])
                P.rmax(R8[:, 0:1], LG[:, 0:4], [LG], [R8])
                P.ts(R8[:, 4:8], LG[:, 0:4], R8[:, 0:1], ALU.is_equal, [LG, R8], [R8])
                P.ts(R8[:, 1:2], R8[:, 0:1], -1.0, ALU.mult, [R8], [R8])
                P.act(R8[:, 8:12], LG[:, 0:4], AF.Exp, [LG, R8], [R8], bias=R8[:, 1:2])
                P.rsum(R8[:, 2:3], R8[:, 8:12], [R8], [R8])
                P.recip(R8[:, 3:4], R8[:, 2:3], [R8], [R8])
                P.ts(R8[:, 12:16], R8[:, 4:8], -1.0, ALU.add, [R8], [R8], s2=1e30, op1=ALU.mult)
                for g in range(4):
                    P.ts(LE2[:, g * 8:(g + 1) * 8], LG[:, 4 + g * 8:12 + g * 8], R8[:, 12 + g:13 + g], ALU.add, [LG, R8], [LE2])
                P.rmax(R8[:, 16:17], LE2[:], [LE2], [R8])
                P.ts(OH1[:], LE2[:], R8[:, 16:17], ALU.is_equal, [LE2, R8], [OH1])
                P.stt(LE2[:], OH1[:], -1e30, LE2[:], ALU.mult, ALU.add, [OH1, LE2], [LE2])
                P.rmax(R8[:, 17:18], LE2[:], [LE2], [R8])
                P.ts(OH2[:], LE2[:], R8[:, 17:18], ALU.is_equal, [LE2, R8], [OH2])
                P.tt(R8[:, 18:19], R8[:, 17:18], R8[:, 16:17], ALU.subtract, [R8], [R8])
                P.act(R8[:, 19:20], R8[:, 18:19], AF.Exp, [R8], [R8])
                P.ts(R8[:, 20:21], R8[:, 19:20], 1.0, ALU.add, [R8], [R8])
                P.recip(R8[:, 21:22], R8[:, 20:21], [R8], [R8])
                P.tt(R8[:, 22:23], R8[:, 21:22], R8[:, 3:4], ALU.mult, [R8], [R8])
                P.tt(R8[:, 23:24], R8[:, 22:23], R8[:, 19:20], ALU.mult, [R8], [R8])
                P.ts(COEF[:], OH1[:], R8[:, 22:23], ALU.mult, [OH1, R8], [COEF])
                P.stt(COEF[:], OH2[:], R8[:, 23:24], COEF[:], ALU.mult, ALU.add, [OH2, R8, COEF], [COEF])
                _, pct, pctr = psum()
                P.mm(pct[0:32, 0:128], COEF[:], ident, True, True, [COEF, cst], pctr)
                P.copy(COEFT[:, tk], pct[0:32, 0:128], pctr, [COEFT])
            for j in (range(KC) if lvl >= 5 else []):
                w = wload(P, WPG_d[j])
                wv = w[:].rearrange("p (k m) -> p k m", m=128)
                _, pp, ppr_ = psum()
                for kc in range(KC):
                    P.mm(pp[:, 0:TM], wv[:, kc, :], x1T[:, kc, :], kc == 0, kc == KC - 1, [w, x1T], ppr_)
                sg = TS[0]
                P.act(sg[:], pp[:, 0:TM], AF.Sigmoid, ppr_, [sg])
                w = wload(P, WPLE_d[j], 256)
                wv = w[:, 0:256].rearrange("p (k m) -> p k m", m=128)
                _, pq, pqr = psum()
                for kc in range(2):
                    P.mm(pq[:, 0:TM], wv[:, kc, :], pTb[:, kc, :], kc == 0, kc == 1, [w, pTb], pqr)
                P.tt(sg[:], sg[:], pq[:, 0:TM], ALU.mult, pqr + [sg], [sg])
                P.stt(ZB[:, j, :], ZB[:, j, :], ALPHA, sg[:], ALU.mult, ALU.add, [ZB, sg], [ZB])
            for e_ in (range(32) if lvl >= 6 else []):
                _, pcb, pcbr = psum()
                P.mm(pcb[:, 0:TM], sele[:, e_, :], COEFT[:], True, True, [sele, COEFT], pcbr)
                P.act(cbt[:], pcb[:, 0:TM], AF.Copy, pcbr, [cbt])
                for f in range(4):
                    w = wload(P, WG_d[e_, f])
                    wv = w[:].rearrange("p (k m) -> p k m", m=128)
                    _, pg, pgr_ = psum()
                    for kc in range(KC):
                        P.mm(pg[:, 0:TM], wv[:, kc, :], x1T[:, kc, :], kc == 0, kc == KC - 1, [w, x1T], pgr_)
                    w2 = wload(P, WU_d[e_, f])
                    wv2 = w2[:].rearrange("p (k m) -> p k m", m=128)
                    _, pu, pur = psum()
                    for kc in range(KC):
                        P.mm(pu[:, 0:TM], wv2[:, kc, :], x1T[:, kc, :], kc == 0, kc == KC - 1, [w2, x1T], pur)
                    sg = TS[f % 2]
                    P.act(sg[:], pg[:, 0:TM], AF.Silu, pgr_, [sg])
                    P.tt(sg[:], sg[:], pu[:, 0:TM], ALU.mult, pur + [sg], [sg])
                    P.tt(hT[:, f, :], sg[:], cbt[:], ALU.mult, [sg, cbt], [hT])
                for dg in range(4):
                    w = wload(P, WD_d[e_, dg])
                    wv = w[:].rearrange("p (c k m) -> p c k m", c=4, k=4)
                    for dcc in range(4):
                        j = dg * 4 + dcc
                        _, pd, pdr = psum()
                        for kc in range(4):
                            P.mm(pd[:, 0:TM], wv[:, dcc, kc, :], hT[:, kc, :], kc == 0, kc == 3, [w, hT], pdr)
                        P.tt(ZB[:, j, :], ZB[:, j, :], pd[:, 0:TM], ALU.add, pdr + [ZB], [ZB])
            if lvl >= 7:
                layernorm(32, 48)
            P.dma("sp", lambda e, q0=q0: e.dma_start(out=outT_d[:, :, q0:q0 + TM], in_=ZB[:]), "out", reads=[ZB])
        P.finish()
        P.emit()
    return nc


def _pack_lhsT(W):
    K, N = W.shape
    return np.ascontiguousarray(W.reshape(K // 128, 128, N // 128, 128).transpose(2, 1, 0, 3)).reshape(N // 128, 128, K)


def _consts():
    p = np.arange(128)
    ident = np.eye(128, dtype=np.float32)
    bones = (p[:, None] // 64 == p[None, :] // 64).astype(np.float32)
    tri = (p[:, None] <= p[None, :]).astype(np.float32)
    ones = np.ones((128, 128), np.float32)
    cst = np.concatenate([ident, bones, tri, ones, np.zeros((128, 128), np.float32)], axis=1)
    j = (p % 64)[:, None]
    t = np.arange(64)[None, :]
    lt = (j < t).astype(np.float32)
    le = (j <= t).astype(np.float32)
    one = np.ones((128, 64), np.float32)
    zero = np.zeros((128, 64), np.float32)
    mskA = np.concatenate([lt, le, one], axis=1)
    mskB = -mskA
    gt = (j > t).astype(np.float32)
    mskC = np.concatenate([gt, -gt], axis=1)
    i64 = (j == t).astype(np.float32)
    ifull = np.tile(i64, (1, 8))
    sele = np.zeros((32, 32, 128), np.float32)
    for e in range(32):
        sele[e, e, :] = 1.0
    return cst, mskA, mskB, mskC, ifull, sele.reshape(32, 32 * 128)


def _prep_shared(inp):
    g = lambda k: np.asarray(inp[k], dtype=np.float32)[0]
    w_in = g("w_in")
    Wp = np.zeros((NCH, 128, 2048), np.float32)
    for i, (_, c0, n) in enumerate(CHUNKS):
        blk = np.zeros((2048, 128), np.float32)
        blk[:, :n] = w_in[:, c0:c0 + n]
        Wp[i] = blk.reshape(16, 128, 128).transpose(1, 0, 2).reshape(128, 2048)
    sh = {"Wp": Wp}
    sh["W2A"] = np.ascontiguousarray(np.concatenate([g("w_w2"), g("w_a2")], axis=0))
    wg2 = g("w_g2")
    sh["G2a"] = np.ascontiguousarray(wg2[0:128])
    sh["G2b"] = np.ascontiguousarray(wg2[128:160])
    mu = g("mu_shift")
    vecR = np.zeros((128, 8, 10), np.float32)
    rk = g("r_k").reshape(1024)
    for hp in range(8):
        s = slice(hp * 128, (hp + 1) * 128)
        vecR[:, hp, 0] = mu[0:1024][s]
        vecR[:, hp, 1] = mu[1024:2048][s]
        vecR[:, hp, 2] = mu[2048:3072][s]
        vecR[:, hp, 3] = g("w0")[s]
        vecR[:, hp, 4] = g("a0")[s]
        vecR[:, hp, 5] = g("k_k")[s]
        vecR[:, hp, 6] = g("k_a")[s]
        vecR[:, hp, 7] = rk[s]
        vecR[:, hp, 8] = g("lnx_w")[s]
        vecR[:, hp, 9] = g("lnx_b")[s]
    sh["vecR"] = vecR.reshape(128, 80)
    vecL = np.zeros((128, 3), np.float32)
    vecL[:, 0] = mu[3072:3200]
    vecL[:, 1] = mu[3200:3328]
    vecL[0:32, 2] = mu[3328:3360]
    sh["vecL"] = vecL
    cw, cb = g("conv_w"), g("conv_b")
    vecM = np.zeros((128, 8, 5), np.float32)
    for hp2 in range(4):
        for wi, base in ((0, 0), (1, 512)):
            s = slice(base + hp2 * 128, base + (hp2 + 1) * 128)
            ci = hp2 * 2 + wi
            for j in range(4):
                vecM[:, ci, j] = cw[j, s]
            vecM[:, ci, 4] = cb[s]
    sh["vecM"] = vecM.reshape(128, 40)
    sh["mhw"] = np.ascontiguousarray(g("mh_w").reshape(8, 128).T)
    sh["gb"] = np.ascontiguousarray(np.broadcast_to(np.concatenate([g("i_bias"), g("f_bias")])[None, :], (128, 16)))
    sh["bgate"] = np.ascontiguousarray(g("b_gate").reshape(32, 128).T)
    sh["WBR"] = _pack_lhsT(g("w_br"))
    sh["WBM"] = _pack_lhsT(g("w_bm"))
    sh["WOUT"] = _pack_lhsT(g("w_out"))
    sh["WPG"] = _pack_lhsT(g("w_pg"))
    sh["WPLE"] = _pack_lhsT(g("w_ple"))
    wgt = g("w_gate")
    sh["WG"] = np.ascontiguousarray(wgt.reshape(32, 16, 128, 4, 128).transpose(0, 3, 2, 1, 4)).reshape(32, 4, 128, 2048)
    wup = g("w_up")
    sh["WU"] = np.ascontiguousarray(wup.reshape(32, 16, 128, 4, 128).transpose(0, 3, 2, 1, 4)).reshape(32, 4, 128, 2048)
    wdn = g("w_down")
    sh["WD"] = np.ascontiguousarray(wdn.reshape(32, 4, 128, 4, 4, 128).transpose(0, 3, 2, 4, 1, 5)).reshape(32, 4, 128, 2048)
    wr = np.concatenate([g("w_rg"), g("w_re")], axis=1)
    sh["WR"] = np.ascontiguousarray(wr.reshape(16, 128, 36).transpose(1, 0, 2)).reshape(128, 16 * 36)
    sh["rb"] = np.ascontiguousarray(np.broadcast_to(np.concatenate([g("b_rg"), g("b_re")])[None, :], (128, 36)))
    lnp = np.zeros((128, 64), np.float32)
    for i, k in enumerate(("ln1_w", "ln1_b", "ln2_w", "ln2_b")):
        lnp[:, i * 16:(i + 1) * 16] = g(k).reshape(16, 128).T
    sh["lnp"] = lnp
    cst, mskA, mskB, mskC, ifull, sele = _consts()
    sh.update({"cst": cst, "mskA": mskA, "mskB": mskB, "mskC": mskC, "ifull": ifull, "sele": sele})
    return sh


def _prep_core(x, p, b, half, T, NPH2):
    S = x.shape[1]
    end = (half + 1) * NPH2
    start = end - T
    win = np.zeros((T, D), np.float32)
    valid = np.zeros((T,), np.float32)
    s0 = max(start, 0)
    win[s0 - start:] = x[b, s0:end]
    valid[s0 - start:] = 1.0
    xT = np.ascontiguousarray(win.T.reshape(16, 128, T).transpose(1, 0, 2))
    pp = p[0, b, end - NPH2:end]
    pT = np.ascontiguousarray(pp.T.reshape(2, 128, NPH2).transpose(1, 0, 2))
    vch = np.ascontiguousarray(np.broadcast_to(valid.reshape(T // 128, 128)[:, 0][None, :], (128, T // 128)))
    return {"xT": xT, "pT": pT, "valid": vch}


def kernel(**inputs):
    x = np.asarray(inputs["x"], dtype=np.float32)
    p = np.asarray(inputs["p"], dtype=np.float32)
    B, S, _ = x.shape
    T, NPH2 = S, S // 2
    sh = _prep_shared(inputs)
    nc = build(T, NPH2)
    in_maps = []
    for c in range(8):
        m = dict(sh)
        m.update(_prep_core(x, p, c // 2, c % 2, T, NPH2))
        in_maps.append(m)
    res = run_bass_kernel_spmd(nc, in_maps, core_ids=list(range(8)))
    out = np.zeros((B, S, D), np.float32)
    for c in range(8):
        oT = res.results[c]["outT"]
        b, half = c // 2, c % 2
        out[b, half * NPH2:(half + 1) * NPH2, :] = oT.transpose(2, 1, 0).reshape(NPH2, D)
    return out
```

```python
import numpy as np
import concourse.bass as bass
import concourse.mybir as mybir
from concourse.bass_utils import run_bass_kernel_spmd
from contextlib import ExitStack

F32 = mybir.dt.float32
BF16 = mybir.dt.bfloat16
F32R = mybir.dt.float32r
ALU = mybir.AluOpType
AF = mybir.ActivationFunctionType
AX = mybir.AxisListType

D = 2048
KC = 16
ALPHA = 2.0 ** 0.25
LN_EPS = 1e-5
R_GN_EPS = 64e-5
M_NORM_EPS = 1e-6
TM = 512
USE_R = False


class Res:
    __slots__ = ("lw", "rd", "excl")

    def __init__(self, excl=False):
        self.lw = None
        self.rd = {}
        self.excl = excl


class Tile:
    def __init__(self, t):
        self.t = t
        self.r = Res()

    def __getitem__(self, k):
        return self.t[k]


def _base_part(ap):
    bp = ap.base_partition
    return bp() if callable(bp) else bp


def _res(x):
    return x.r if isinstance(x, Tile) else x


class Prog:
    ENG = ("pe", "act", "dve", "pool", "sp")
    SAME_WIN = 3

    def __init__(self, nc, tag):
        self.nc = nc
        self.tag = tag
        self.ops = {e: [] for e in self.ENG}
        self.cnt = {}
        self.clock = {e: {} for e in self.ENG}
        self.snap = {}
        self.sems = {}
        self._pe_free = False
        for e in self.ENG:
            self._mksem(e)

    def _mksem(self, key):
        self.sems[key] = self.nc.alloc_semaphore("s%s_%s" % (self.tag, str(key).replace(" ", "")))
        self.cnt[key] = 0

    def _need(self, eng, ev, waits):
        if ev is None:
            return
        key, val = ev
        if key == eng:
            if eng == "pe" and self._pe_free:
                return
            if self.cnt[eng] + 1 - val <= self.SAME_WIN:
                waits[key] = max(waits.get(key, 0), val)
            return
        if self.clock[eng].get(key, 0) >= val:
            return
        waits[key] = max(waits.get(key, 0), val)

    def _absorb(self, eng, waits):
        ck = self.clock[eng]
        for key, val in waits.items():
            if key == eng:
                continue
            if ck.get(key, 0) < val:
                ck[key] = val
            sn = self.snap.get((key, val))
            if sn:
                for k2, v2 in sn.items():
                    if k2 != eng and ck.get(k2, 0) < v2:
                        ck[k2] = v2

    def _deps(self, eng, reads, writes):
        waits = {}
        for r in reads:
            self._need(eng, _res(r).lw, waits)
        for w in writes:
            w = _res(w)
            self._need(eng, w.lw, waits)
            for k, v in w.rd.items():
                self._need(eng, (k, v), waits)
        return waits

    def op(self, eng, fn, reads=(), writes=()):
        ex = [r for r in reads if _res(r).excl]
        if ex:
            reads = [r for r in reads if not _res(r).excl]
            writes = list(writes) + ex
        waits = self._deps(eng, reads, writes)
        self._absorb(eng, waits)
        self.cnt[eng] += 1
        val = self.cnt[eng]
        self.ops[eng].append((tuple(waits.items()), fn, eng, 1))
        self.snap[(eng, val)] = dict(self.clock[eng])
        for r in reads:
            _res(r).rd[eng] = val
        for w in writes:
            w = _res(w)
            w.lw = (eng, val)
            w.rd = {}

    def dma(self, q, fn, semkey, reads=(), writes=()):
        if semkey not in self.sems:
            self._mksem(semkey)
        waits = self._deps(q, reads, writes)
        if self.cnt[semkey] > 0:
            self._need(q, (semkey, self.cnt[semkey]), waits)
        self._absorb(q, waits)
        self.cnt[semkey] += 16
        val = self.cnt[semkey]
        self.ops[q].append((tuple(waits.items()), fn, semkey, 16))
        self.snap[(semkey, val)] = dict(self.clock[q])
        for r in reads:
            _res(r).rd[semkey] = val
        for w in writes:
            w = _res(w)
            w.lw = (semkey, val)
            w.rd = {}

    def finish(self):
        for e in self.ENG:
            waits = {k: v for k, v in self.cnt.items() if v > 0 and k != e}
            self.ops[e].append((tuple(waits.items()), None, None, 0))

    def emit(self):
        nc = self.nc
        waited = {e: set() for e in self.ENG}
        for e in self.ENG:
            for waits, fn, semkey, inc in self.ops[e]:
                for k, v in waits:
                    if k in waited:
                        waited[k].add(v)
        rank = {e: {v: i + 1 for i, v in enumerate(sorted(waited[e]))} for e in self.ENG}
        with nc.Block() as block:
            def run(e, engobj):
                ci = 0
                for waits, fn, semkey, inc in self.ops[e]:
                    for k, v in waits:
                        engobj.wait_ge(self.sems[k], rank[k][v] if k in rank else v)
                    if fn is None:
                        continue
                    if semkey == e:
                        ci += 1
                        if ci in rank[e]:
                            fn(engobj).then_inc(self.sems[e], 1)
                        else:
                            fn(engobj)
                    else:
                        fn(engobj).then_inc(self.sems[semkey], inc)

            @block.tensor
            def _(eng):
                run("pe", eng)

            @block.scalar
            def _(eng):
                run("act", eng)

            @block.vector
            def _(eng):
                run("dve", eng)

            @block.gpsimd
            def _(eng):
                run("pool", eng)

            @block.sync
            def _(eng):
                run("sp", eng)

    def mm(self, out, lhsT, rhs, start, stop, reads, writes, r=False, free=True):
        self._pe_free = free
        try:
            self._mm(out, lhsT, rhs, start, stop, reads, writes, r)
        finally:
            self._pe_free = False

    def _mm(self, out, lhsT, rhs, start, stop, reads, writes, r=False):
        if r and USE_R and _base_part(out) == 0:
            lhsT = lhsT.bitcast(F32R)
            rhs = rhs.bitcast(F32R)
        elif r:
            lhsT = lhsT.bitcast(F32)
            rhs = rhs.bitcast(F32)
        self.op("pe", lambda e: e.matmul(out, lhsT=lhsT, rhs=rhs, start=start, stop=stop), reads, writes)

    def act(self, out, in_, func, reads, writes, bias=None, scale=None):
        kw = {}
        if bias is not None:
            kw["bias"] = bias
        if scale is not None:
            kw["scale"] = scale
        self.op("act", lambda e: e.activation(out=out, in_=in_, func=func, **kw), reads, writes)

    def tt(self, out, in0, in1, op, reads, writes, eng="dve"):
        self.op(eng, lambda e: e.tensor_tensor(out=out, in0=in0, in1=in1, op=op), reads, writes)

    def ts(self, out, in0, s1, op0, reads, writes, s2=None, op1=None, eng="dve"):
        if op1 is None:
            self.op(eng, lambda e: e.tensor_scalar(out=out, in0=in0, scalar1=s1, scalar2=None, op0=op0), reads, writes)
        else:
            self.op(eng, lambda e: e.tensor_scalar(out=out, in0=in0, scalar1=s1, scalar2=s2, op0=op0, op1=op1), reads, writes)

    def stt(self, out, in0, scalar, in1, op0, op1, reads, writes, eng="dve"):
        self.op(eng, lambda e: e.scalar_tensor_tensor(out=out, in0=in0, scalar=scalar, in1=in1, op0=op0, op1=op1), reads, writes)

    def copy(self, out, in_, reads, writes, eng="dve"):
        self.op(eng, lambda e: e.tensor_copy(out=out, in_=in_), reads, writes)

    def recip(self, out, in_, reads, writes):
        self.op("dve", lambda e: e.reciprocal(out=out, in_=in_), reads, writes)

    def memset(self, out, val, writes, eng="dve"):
        self.op(eng, lambda e: e.memset(out, val), (), writes)

    def rsum(self, out, in_, reads, writes):
        self.op("dve", lambda e: e.reduce_sum(out=out, in_=in_, axis=AX.X), reads, writes)

    def rmax(self, out, in_, reads, writes):
        self.op("dve", lambda e: e.reduce_max(out=out, in_=in_, axis=AX.X), reads, writes)


def _chunks():
    ch = []
    for hp in range(8):
        ch.append(("r%d" % hp, 0 + hp * 128, 128))
        ch.append(("k%d" % hp, 1024 + hp * 128, 128))
        ch.append(("v%d" % hp, 2048 + hp * 128, 128))
    ch.append(("L0", 3072, 128))
    ch.append(("L1", 3200, 128))
    ch.append(("L2", 3328, 32))
    for hp in range(4):
        ch.append(("mq%d" % hp, 3360 + hp * 128, 128))
        ch.append(("mk%d" % hp, 3872 + hp * 128, 128))
    for h in range(8):
        ch.append(("mv%d" % h, 4384 + h * 128, 128))
        ch.append(("mo%d" % h, 5424 + h * 128, 128))
    ch.append(("mg", 5408, 16))
    for j in range(16):
        ch.append(("gr%d" % j, 6448 + j * 128, 128))
        ch.append(("gm%d" % j, 8496 + j * 128, 128))
    return ch


CHUNKS = _chunks()
CIDX = {c[0]: i for i, c in enumerate(CHUNKS)}
NCH = len(CHUNKS)


def build(T, NPH2, debug=False, parts=("lora", "rwkv", "mlstm", "ph2"), lvl=9):
    NMT = T // TM
    PH2_0 = T - NPH2
    nc = bass.Bass("TRN2", target_bir_lowering=False)

    def din(name, shape):
        return nc.dram_tensor(name, list(shape), F32, kind="ExternalInput").ap()

    xT_d = din("xT", [128, KC, T])
    pT_d = din("pT", [128, 2, NPH2])
    valid_d = din("valid", [128, T // 128])
    Wp_d = din("Wp", [NCH, 128, 2048])
    W2A_d = din("W2A", [128, 1024])
    G2a_d = din("G2a", [128, 1024])
    G2b_d = din("G2b", [32, 1024])
    vecR_d = din("vecR", [128, 80])
    vecL_d = din("vecL", [128, 3])
    vecM_d = din("vecM", [128, 40])
    mhw_d = din("mhw", [128, 8])
    gb_d = din("gb", [128, 16])
    bgate_d = din("bgate", [128, 32])
    WBR_d = din("WBR", [16, 128, 1024])
    WBM_d = din("WBM", [16, 128, 1024])
    WOUT_d = din("WOUT", [16, 128, 2048])
    WPG_d = din("WPG", [16, 128, 2048])
    WPLE_d = din("WPLE", [16, 128, 256])
    WG_d = din("WG", [32, 4, 128, 2048])
    WU_d = din("WU", [32, 4, 128, 2048])
    WD_d = din("WD", [32, 4, 128, 2048])
    WR_d = din("WR", [128, KC * 36])
    rb_d = din("rb", [128, 36])
    lnp_d = din("lnp", [128, 64])
    cst_d = din("cst", [128, 128 * 5])
    mskA_d = din("mskA", [128, 192])
    mskB_d = din("mskB", [128, 192])
    mskC_d = din("mskC", [128, 128])
    ifull_d = din("ifull", [128, 512])
    sele_d = din("sele", [32, 32 * 128])
    outT_d = nc.dram_tensor("outT", [128, KC, NPH2], F32, kind="ExternalOutput").ap()
    if debug:
        dbg_d = nc.dram_tensor("dbg", [128, 2 * 8 * NPH2], F32, kind="ExternalOutput").ap()

    def sb(name, shape, dt=F32):
        return Tile(nc.alloc_sbuf_tensor("sb_" + name, list(shape), dt))

    PS = nc.alloc_psum_tensor("PS", [128, 4096], F32)
    PSR = [Res(excl=True) for _ in range(8)]
    yrT = sb("yrT", [128, 8, NPH2], BF16)
    ymT = sb("ymT", [128, 8, NPH2], BF16)
    cst = sb("cst", [128, 640])
    ident = cst[:, 0:128]
    bones = cst[:, 128:256]
    tri = cst[:, 256:384]
    ones = cst[:, 384:512]
    NW = 4
    wslot = [sb("wslot%d" % i, [128, 2048], BF16) for i in range(NW)]
    xb = sb("xb", [128, KC, TM], BF16)

    state = {"ps": 0, "w": 0}

    def psum(nb=1):
        i = state["ps"]
        if i + nb > 8:
            i = 0
        state["ps"] = (i + nb) % 8
        return i, PS[:, i * 512:(i + nb) * 512], PSR[i:i + nb]

    def wload(P, src, ncols=2048):
        i = state["w"]
        state["w"] = (i + 1) % NW
        t = wslot[i]
        P.dma("pool", lambda e: e.dma_start(out=t[:, 0:ncols], in_=src), ("w", i), writes=[t])
        return t

    def cload(P, tile, src, q="sp", key="c"):
        P.dma(q, lambda e: e.dma_start(out=tile[:], in_=src), key, writes=[tile])

    with ExitStack() as es:
        def sbt(name, shape, dt=F32):
            return Tile(es.enter_context(nc.sbuf_tensor("sb_" + name, list(shape), dt)))

        P = Prog(nc, "a")
        cload(P, cst, cst_d)
        W2A = sbt("W2A", [128, 1024], BF16)
        G2a = sbt("G2a", [128, 1024], BF16)
        G2b = sbt("G2b", [32, 1024], BF16)
        cload(P, W2A, W2A_d, "pool", "c2")
        cload(P, G2a, G2a_d, "pool", "c2")
        cload(P, G2b, G2b_d, "pool", "c2")
        vecR = sbt("vecR", [128, 80]); cload(P, vecR, vecR_d)
        vecL = sbt("vecL", [128, 3]); cload(P, vecL, vecL_d)
        vecM = sbt("vecM", [128, 40]); cload(P, vecM, vecM_d)
        mhw = sbt("mhw", [128, 8]); cload(P, mhw, mhw_d)
        gb = sbt("gb", [128, 16]); cload(P, gb, gb_d)
        valid = sbt("valid", [128, T // 128]); cload(P, valid, valid_d)
        mskA = sbt("mskA", [128, 192]); cload(P, mskA, mskA_d)
        mskB = sbt("mskB", [128, 192]); cload(P, mskB, mskB_d)
        mskC = sbt("mskC", [128, 128]); cload(P, mskC, mskC_d)
        ifull = sbt("ifull", [128, 512]); cload(P, ifull, ifull_d)
        cstR = sbt("cstR", [128, 512], F32R)
        P.copy(cstR[:], cst[:, 0:512], [cst], [cstR])
        identR = cstR[:, 0:128]
        bonesR = cstR[:, 128:256]
        triR = cstR[:, 256:384]
        onesR = cstR[:, 384:512]
        scanm = sbt("scanm", [128, 512])
        P.memset(scanm[:], 1.0, [scanm])
        P.memset(scanm[:].rearrange("p (c l) -> p c l", l=64)[:, :, 0:1], 0.0, [scanm])

        ST = sbt("ST", [128, 8, 64], F32R)
        P.memset(ST[:].bitcast(F32), 0.0, [ST])
        CS = sbt("CS", [128, 4, 129], F32R)
        P.memset(CS[:].bitcast(F32), 0.0, [CS])
        carry = sbt("carry", [128, 32])
        P.memset(carry[:], 0.0, [carry])
        ccarry = sbt("ccarry", [128, 8, 3])
        P.memset(ccarry[:], 0.0, [ccarry])

        NSC = 15
        SC = [sbt("sc%d" % i, [128, 516]) for i in range(NSC)]
        epsR = sbt("epsR", [128, 1]); P.memset(epsR[:], R_GN_EPS, [epsR])
        epsM = sbt("epsM", [128, 1]); P.memset(epsM[:], M_NORM_EPS, [epsM])
        oneC = sbt("oneC", [128, 1]); P.memset(oneC[:], 1.0, [oneC])
        TL = sbt("TL", [128, TM], BF16)
        SG0 = sbt("SG0", [128, TM], BF16)
        SG1 = sbt("SG1", [32, TM], BF16)
        R3 = sbt("R3", [128, 8, 192], F32R)
        K2 = sbt("K2", [128, 8, 128], F32R)
        EA = sbt("EA", [128, 4, 2, 192], F32R)
        EB = sbt("EB", [128, 4, 2, 192], F32R)
        EC = sbt("EC", [128, 4, 2, 128], F32R)
        XA = sbt("XA", [128, 4, 2, 64], F32R); XTA = sbt("XTA", [128, 4, 2, 64], F32R)
        XB = sbt("XB", [128, 4, 2, 64], F32R); XTB = sbt("XTB", [128, 4, 2, 64], F32R)
        PM = sbt("PM", [128, 4, 2, 64], F32R)
        VmT = sbt("VmT", [128, 4, 2, 64], F32R)
        RT = sbt("RT", [128, 2, 64], F32R); UT = sbt("UT", [128, 2, 64], F32R)
        P.memset(R3[:].bitcast(F32), 0.0, [R3])
        for c in range(8):
            P.copy(R3[0:64, c, 128:192], ident[0:64, 0:64], [cst], [R3])
            P.copy(R3[64:128, c, 128:192], ident[64:128, 64:128], [cst], [R3])
        GI = sbt("GI", [128, 16]); EL = sbt("EL", [128, 8]); LL = sbt("LL", [128, 8], F32R)
        EAc = sbt("EAc", [128, 4, 8]); EKc = sbt("EKc", [128, 4, 8]); EALc = sbt("EALc", [128, 4, 8])
        tm8 = sbt("tm8", [128, 8])
        VP = sbt("VP", [128, 4, 129], F32R)
        Gm = sbt("Gm", [128, 128], F32R); kTk = sbt("kTk", [128, 64], F32R); hh = sbt("hh", [128, 128]); hn = sbt("hn", [128, 128], F32R)
        hsq = sbt("hsq", [128, 128])
        sm = sbt("sm", [128, 16])

        def inproj(cname, ncols=128, n0=0, nn=TM):
            w = wload(P, Wp_d[CIDX[cname]])
            wv = w[:].rearrange("p (k m) -> p k m", m=128)
            pi, pap, pr = psum()
            for kc in range(KC):
                P.mm(pap[0:ncols, 0:nn], wv[:, kc, 0:ncols], xb[:, kc, n0:n0 + nn], kc == 0, kc == KC - 1, [w, xb], pr)
            return pap, pr

        def shifted(cname, ci, mu_ap, zt, out_t, ncols=128, rnd=False):
            pap, pr = inproj(cname, ncols)
            P.copy(zt[0:ncols, 0:1], carry[0:ncols, ci:ci + 1], [carry], [zt])
            P.act(zt[0:ncols, 1:TM + 1], pap[0:ncols, 0:TM], AF.Copy, pr, [zt])
            P.copy(carry[0:ncols, ci:ci + 1], zt[0:ncols, TM:TM + 1], [zt], [carry])
            P.tt(out_t[0:ncols, 0:TM], zt[0:ncols, 0:TM], zt[0:ncols, 1:TM + 1], ALU.subtract, [zt], [out_t])
            oo = out_t[0:ncols, 0:TM].bitcast(F32R) if rnd else out_t[0:ncols, 0:TM]
            P.stt(oo, out_t[0:ncols, 0:TM], mu_ap, zt[0:ncols, 1:TM + 1], ALU.mult, ALU.add, [out_t, zt, vecR, vecL], [out_t])

        def bmm(in_ap, in_t):
            pi, pap, pr = psum()
            P.mm(pap[:, 0:TM], bones, in_ap, True, True, [cst, in_t], pr)
            return pap, pr

        for mt in range(NMT):
            t0 = mt * TM
            P.dma("pool", lambda e, t0=t0: e.dma_start(out=xb[:], in_=xT_d[:, :, t0:t0 + TM]), "xb", writes=[xb])
            inph2 = t0 >= PH2_0
            need_carry = (t0 + TM >= PH2_0)
            q0 = t0 - PH2_0
            z, o = SC[0], SC[1]
            shifted("L0", 24, vecL[:, 0:1], z, o)
            P.act(TL[0:64, :], o[0:64, 0:TM], AF.Tanh, [o], [TL])
            P.copy(TL[64:128, :], o[64:128, 0:TM], [o], [TL])
            if need_carry:
                shifted("L1", 25, vecL[:, 1:2], z, o)
                P.act(SG0[:, :], o[:, 0:TM], AF.Sigmoid, [o], [SG0])
                shifted("L2", 26, vecL[0:32, 2:3], z, o, ncols=32)
                P.act(SG1[:, :], o[0:32, 0:TM], AF.Sigmoid, [o], [SG1])
            for hp in (range(8) if "rwkv" in parts else []):
                cs = slice(hp * 128, (hp + 1) * 128)
                vr = lambda i: vecR[:, hp * 10 + i:hp * 10 + i + 1]
                rs, ks, vs = SC[2], SC[3], SC[4]
                if need_carry:
                    shifted("r%d" % hp, hp * 3 + 0, vr(0), SC[0], rs)
                shifted("k%d" % hp, hp * 3 + 1, vr(1), SC[0], ks)
                shifted("v%d" % hp, hp * 3 + 2, vr(2), SC[0], vs)
                _, pw, pwr = psum()
                P.mm(pw[:, 0:TM], W2A[0:64, cs], TL[0:64, :], True, True, [W2A, TL], pwr)
                _, pa, par_ = psum()
                P.mm(pa[:, 0:TM], W2A[64:128, cs], TL[64:128, :], True, True, [W2A, TL], par_)
                lw, aa, gg = SC[5], SC[6], SC[7]
                if inph2:
                    _, pg, pgr = psum()
                    P.mm(pg[:, 0:TM], G2a[:, cs], SG0[:, :], True, False, [G2a, SG0], pgr)
                    P.mm(pg[:, 0:TM], G2b[:, cs], SG1[:, :], False, True, [G2b, SG1], pgr)
                    P.act(gg[:, 0:TM], pg[:, 0:TM], AF.Copy, pgr, [gg])
                P.act(lw[:, 0:TM], pw[:, 0:TM], AF.Sigmoid, pwr + [vecR], [lw], bias=vr(3))
                P.ts(lw[:, 0:TM], lw[:, 0:TM], -float(np.exp(-0.5)), ALU.mult, [lw], [lw])
                P.act(aa[:, 0:TM], pa[:, 0:TM], AF.Sigmoid, par_ + [vecR], [aa], bias=vr(4))
                kk, sq, kap = SC[8], SC[9], SC[10]
                P.ts(kk[:, 0:TM], ks[:, 0:TM], vr(5), ALU.mult, [ks, vecR], [kk])
                P.tt(sq[:, 0:TM], kk[:, 0:TM], kk[:, 0:TM], ALU.mult, [kk], [sq])
                pss, pssr = bmm(sq[:, 0:TM], sq)
                P.act(sq[:, 0:TM], pss[:, 0:TM], AF.Sqrt, pssr, [sq])
                P.ts(sq[:, 0:TM], sq[:, 0:TM], 1e-12, ALU.max, [sq], [sq])
                P.recip(sq[:, 0:TM], sq[:, 0:TM], [sq], [sq])
                P.tt(kap[:, 0:TM], kk[:, 0:TM], sq[:, 0:TM], ALU.mult, [kk, sq], [kap])
                km, beta = SC[11], SC[12]
                P.ts(km[:, 0:TM], aa[:, 0:TM], -1.0, ALU.add, [aa, vecR], [km], s2=vr(6), op1=ALU.mult)
                P.stt(km[:, 0:TM], km[:, 0:TM], 1.0, ks[:, 0:TM], ALU.add, ALU.mult, [km, ks], [km])
                P.tt(beta[:, 0:TM], aa[:, 0:TM], kap[:, 0:TM], ALU.mult, [aa, kap], [beta])
                bon = SC[13]
                if inph2:
                    P.stt(bon[:, 0:TM], rs[:, 0:TM], vr(7), km[:, 0:TM], ALU.mult, ALU.mult, [rs, km, vecR], [bon])
                    pb, pbr = bmm(bon[:, 0:TM], bon)
                    P.tt(bon[:, 0:TM], pb[:, 0:TM], vs[:, 0:TM], ALU.mult, pbr + [vs], [bon])
                cc, ep, en, epv = SC[8], SC[14], SC[6], SC[9]
                P.op("dve", lambda e, cc=cc, lw=lw: e.tensor_tensor_scan(out=cc[:, 0:TM], data0=scanm[:, 0:TM], data1=lw[:, 0:TM],
                                                                         initial=0.0, op0=ALU.mult, op1=ALU.add), [scanm, lw], [cc])
                P.act(ep[:, 0:TM], cc[:, 0:TM], AF.Exp, [cc], [ep])
                P.act(en[:, 0:TM], cc[:, 0:TM], AF.Exp, [cc], [en], scale=-1.0)
                P.tt(epv[:, 0:TM], cc[:, 0:TM], lw[:, 0:TM], ALU.subtract, [cc, lw], [epv])
                P.act(epv[:, 0:TM], epv[:, 0:TM], AF.Exp, [epv], [epv])
                c3 = lambda t_: t_[:, 0:TM].rearrange("p (c l) -> p c l", l=64)
                P.tt(R3[:, :, 0:64], c3(kap), c3(epv), ALU.mult, [kap, epv], [R3])
                if inph2:
                    P.tt(R3[:, :, 64:128], c3(rs), c3(ep), ALU.mult, [rs, ep], [R3])
                P.tt(K2[:, :, 0:64], c3(km), c3(en), ALU.mult, [km, en], [K2])
                P.tt(K2[:, :, 64:128], c3(beta), c3(en), ALU.mult, [beta, en], [K2])
                for dc in range(4):
                    _, pt, ptr = psum()
                    P.mm(pt[:, 0:128], vs[:, dc * 128:(dc + 1) * 128], ident, True, True, [vs, cst], ptr)
                    P.copy(VmT[:, dc, :, :], pt[:, 0:128].rearrange("p (h v) -> p h v", v=64), ptr, [VmT])
                hr = lambda h: slice(h * 64, (h + 1) * 64)
                pr_ = lambda c: slice((c % 2) * 64, (c % 2) * 64 + 64)
                _, pA, pAr = psum(4)
                pAv = pA.rearrange("p (d h w) -> p d h w", d=4, h=2)
                for h in range(2):
                    for c in range(8):
                        P.mm(pAv[pr_(c), c // 2, h, 0:192], K2[hr(h), c, 0:64], R3[hr(h), c, 0:192], True, True, [K2, R3], pAr, r=True, free=(c > 0))
                for d_ in range(4):
                    for h in range(2):
                        P.tt(EA[:, d_, h, :], pAv[:, d_, h, 0:192], mskA[:], ALU.mult, pAr + [mskA], [EA])
                _, pB, pBr = psum(4)
                pBv = pB.rearrange("p (d h w) -> p d h w", d=4, h=2)
                for h in range(2):
                    for c in range(8):
                        P.mm(pBv[pr_(c), c // 2, h, 0:192], K2[hr(h), c, 64:128], R3[hr(h), c, 0:192], True, True, [K2, R3], pBr, r=True, free=(c > 0))
                for d_ in range(4):
                    for h in range(2):
                        P.tt(EB[:, d_, h, :], pBv[:, d_, h, 0:192], mskB[:], ALU.mult, pBr + [mskB], [EB])
                _, pC, pCr = psum(2)
                pCv = pC.rearrange("p (d h w) -> p d h w", d=4, h=2)
                for h in range(2):
                    for c in range(8):
                        P.mm(pCv[pr_(c), c // 2, h, 0:128], R3[hr(h), c, 0:64], K2[hr(h), c, 0:128], True, True, [K2, R3], pCr, r=True, free=(c > 0))
                for d_ in range(4):
                    for h in range(2):
                        P.tt(EC[:, d_, h, :], pCv[:, d_, h, :], mskC[:], ALU.mult, pCr + [mskC], [EC])
                X = (EB, lambda c, h: EB[pr_(c), c // 2, h, 0:64])
                XT = (EC, lambda c, h: EC[pr_(c), c // 2, h, 64:128])
                P.tt(PM[:], EB[:, :, :, 0:64], ifull[:].rearrange("p (d h w) -> p d h w", d=4, h=2), ALU.add, [EB, ifull], [PM])
                bufs = [(XA, XTA), (XB, XTB)]
                for it in range(5):
                    nX, nXT = bufs[it % 2]
                    last = it == 4
                    _, p2, p2r = psum()
                    p2v = p2.rearrange("p (d h w) -> p d h w", d=4, h=2)
                    for c in range(8):
                        for h in range(2):
                            P.mm(p2v[pr_(c), c // 2, h, :], X[1](c, h), XT[1](c, h), True, True, [X[0], XT[0]], p2r, r=True)
                    P.act(nXT[:].rearrange("p d h w -> p (d h w)"), p2, AF.Copy, p2r, [nXT])
                    if not last:
                        _, p1, p1r = psum()
                        p1v = p1.rearrange("p (d h w) -> p d h w", d=4, h=2)
                        for c in range(8):
                            for h in range(2):
                                P.mm(p1v[pr_(c), c // 2, h, :], XT[1](c, h), X[1](c, h), True, True, [X[0], XT[0]], p1r, r=True)
                        P.act(nX[:].rearrange("p d h w -> p (d h w)"), p1, AF.Copy, p1r, [nX])
                    _, p3, p3r = psum()
                    p3v = p3.rearrange("p (d h w) -> p d h w", d=4, h=2)
                    for c in range(8):
                        for h in range(2):
                            P.mm(p3v[pr_(c), c // 2, h, :], nXT[pr_(c), c // 2, h, :], PM[pr_(c), c // 2, h, :], True, True, [nXT, PM], p3r, r=True)
                    P.tt(PM[:].rearrange("p d h w -> p (d h w)"), PM[:].rearrange("p d h w -> p (d h w)"), p3, ALU.add, p3r + [PM], [PM])
                    X = (nX, lambda c, h, nX=nX: nX[pr_(c), c // 2, h, :])
                    XT = (nXT, lambda c, h, nXT=nXT: nXT[pr_(c), c // 2, h, :])
                yb = SC[3]
                for c in range(8):
                    rows = pr_(c)
                    dc = c // 2
                    _, p1, p1r = psum()
                    for h in range(2):
                        P.mm(p1[rows, h * 64:(h + 1) * 64], R3[hr(h), c, 0:64], ST[hr(h), hp, :], True, False, [R3, ST], p1r, r=True, free=False)
                        P.mm(p1[rows, h * 64:(h + 1) * 64], EA[rows, dc, h, 0:64], VmT[rows, dc, h, :], False, True, [EA, VmT], p1r, r=True, free=(h == c % 2))
                    P.act(RT[rows, :, :], p1[rows, 0:128].rearrange("p (h v) -> p h v", v=64), AF.Copy, p1r, [RT])
                    _, p2, p2r = psum()
                    for h in range(2):
                        P.mm(p2[rows, h * 64:(h + 1) * 64], PM[rows, dc, h, :], RT[rows, h, :], True, True, [PM, RT], p2r, r=True, free=(h == 1))
                    P.copy(UT[rows, :, :], p2[rows, 0:128].rearrange("p (h v) -> p h v", v=64), p2r, [UT])
                    if inph2:
                        _, pY, pYr = psum()
                    _, pS, pSr = psum()
                    for h in range(2):
                        if inph2:
                            P.mm(pY[hr(h), 0:64], ST[hr(h), hp, :], R3[hr(h), c, 64:128], True, False, [ST, R3], pYr, r=True, free=False)
                            P.mm(pY[hr(h), 0:64], VmT[rows, dc, h, :], EA[rows, dc, h, 64:128], False, False, [VmT, EA], pYr, r=True, free=(h == c % 2))
                            P.mm(pY[hr(h), 0:64], UT[rows, h, :], EB[rows, dc, h, 64:128], False, True, [UT, EB], pYr, r=True, free=True)
                        idh = ident[hr(h), hr(h)]
                        P.mm(pS[hr(h), 0:64], idh, ST[hr(h), hp, :].bitcast(F32), True, False, [cst, ST], pSr, free=False)
                        P.mm(pS[hr(h), 0:64], EA[rows, dc, h, 128:192], VmT[rows, dc, h, :], False, False, [EA, VmT], pSr, r=True, free=(h == c % 2))
                        P.mm(pS[hr(h), 0:64], EB[rows, dc, h, 128:192], UT[rows, h, :], False, True, [EB, UT], pSr, r=True, free=True)
                    if inph2:
                        P.act(yb[:, c * 64:(c + 1) * 64], pY[:, 0:64], AF.Copy, pYr, [yb])
                    P.ts(ST[:, hp, :], pS[:, 0:64], ep[:, c * 64 + 63:c * 64 + 64], ALU.mult, pSr + [ep], [ST])
                if inph2:
                    pm, pmr = bmm(yb[:, 0:TM], yb)
                    mean, dd, var = SC[10], SC[11], SC[12]
                    P.act(mean[:, 0:TM], pm[:, 0:TM], AF.Copy, pmr, [mean], scale=1.0 / 64)
                    P.tt(dd[:, 0:TM], yb[:, 0:TM], mean[:, 0:TM], ALU.subtract, [yb, mean], [dd])
                    P.tt(var[:, 0:TM], dd[:, 0:TM], dd[:, 0:TM], ALU.mult, [dd], [var])
                    pq, pqr = bmm(var[:, 0:TM], var)
                    P.act(var[:, 0:TM], pq[:, 0:TM], AF.Sqrt, pqr + [epsR], [var], scale=1.0 / 64, bias=epsR[:, 0:1])
                    P.recip(var[:, 0:TM], var[:, 0:TM], [var], [var])
                    P.tt(dd[:, 0:TM], dd[:, 0:TM], var[:, 0:TM], ALU.mult, [dd, var], [dd])
                    P.act(dd[:, 0:TM], dd[:, 0:TM], AF.Identity, [dd, vecR], [dd], scale=vr(8), bias=vr(9))
                    P.tt(dd[:, 0:TM], dd[:, 0:TM], bon[:, 0:TM], ALU.add, [dd, bon], [dd])
                    P.tt(yrT[:, hp, q0:q0 + TM], dd[:, 0:TM], gg[:, 0:TM], ALU.mult, [dd, gg], [yrT])
            wmg = wload(P, Wp_d[CIDX["mg"]])
            wmgv = wmg[:].rearrange("p (k m) -> p k m", m=128)
            for ck in (range(4) if "mlstm" in parts else []):
                _, pgt, pgtr = psum()
                for kc in range(KC):
                    P.mm(pgt[:, 0:16], xb[:, kc, ck * 128:(ck + 1) * 128], wmgv[:, kc, 0:16], kc == 0, kc == KC - 1, [xb, wmg], pgtr)
                P.tt(GI[:], pgt[:, 0:16], gb[:], ALU.add, pgtr + [gb], [GI])
                P.act(EL[:], GI[:, 8:16], AF.Exp, [GI], [EL], scale=-1.0)
                P.act(LL[:], EL[:], AF.Ln, [EL, oneC], [LL], bias=oneC[:, 0:1])
                _, pc, pcr = psum()
                P.mm(pc[:, 0:8], triR, LL[:], True, True, [cstR, LL], pcr, r=True)
                P.mm(pc[:, 8:16], onesR, LL[:], True, True, [cstR, LL], pcr, r=True)
                P.act(EAc[:, ck, :], pc[:, 0:8], AF.Exp, pcr, [EAc], scale=-1.0)
                P.tt(tm8[:], GI[:, 0:8], pc[:, 0:8], ALU.add, pcr + [GI], [tm8])
                P.act(EKc[:, ck, :], tm8[:], AF.Exp, [tm8], [EKc])
                P.act(tm8[:], pc[:, 8:16], AF.Exp, pcr, [tm8], scale=-1.0)
                gck = mt * 4 + ck
                P.ts(EALc[:, ck, :], tm8[:], valid[:, gck:gck + 1], ALU.mult, [tm8, valid], [EALc])
            for hp2 in (range(4) if "mlstm" in parts else []):
                qf, kf = SC[2], SC[3]
                for which, dst in (("mq", qf), ("mk", kf)):
                    if which == "mq" and not need_carry:
                        continue
                    ci = hp2 * 2 + (0 if which == "mq" else 1)
                    vm = lambda i: vecM[:, ci * 5 + i:ci * 5 + i + 1]
                    pap, pr = inproj("%s%d" % (which, hp2))
                    zc = SC[0]
                    P.copy(zc[:, 0:3], ccarry[:, ci, :], [ccarry], [zc])
                    P.act(zc[:, 3:TM + 3], pap[:, 0:TM], AF.Copy, pr, [zc])
                    P.copy(ccarry[:, ci, :], zc[:, TM:TM + 3], [zc], [ccarry])
                    acc = SC[1]
                    P.ts(acc[:, 0:TM], zc[:, 0:TM], vm(0), ALU.mult, [zc, vecM], [acc], s2=vm(4), op1=ALU.add)
                    for j in range(1, 4):
                        P.stt(acc[:, 0:TM], zc[:, j:j + TM], vm(j), acc[:, 0:TM], ALU.mult, ALU.add, [zc, acc, vecM], [acc])
                    if which == "mk":
                        P.act(dst[:, 0:TM], acc[:, 0:TM], AF.Silu, [acc], [dst])
                        P.ts(dst[:, 0:TM], dst[:, 0:TM], 0.125, ALU.mult, [dst], [dst])
                    else:
                        P.act(dst[:, 0:TM], acc[:, 0:TM], AF.Silu, [acc], [dst])
                for hh_ in range(2):
                    h = hp2 * 2 + hh_
                    hrows = slice(hh_ * 64, hh_ * 64 + 64)
                    so = SC[4]
                    if inph2:
                        pap, pr = inproj("mo%d" % h)
                        P.act(so[:, 0:TM], pap[:, 0:TM], AF.Sigmoid, pr, [so])
                    wv_ = wload(P, Wp_d[CIDX["mv%d" % h]])
                    wvv = wv_[:].rearrange("p (k m) -> p k m", m=128)
                    for ck in range(4):
                        _, pv, pvr = psum()
                        for kc in range(KC):
                            P.mm(pv[:, 0:128], xb[:, kc, ck * 128:(ck + 1) * 128], wvv[:, kc, :], kc == 0, kc == KC - 1, [xb, wv_], pvr)
                        P.act(VP[:, ck, 0:128], pv[:, 0:128], AF.Copy, pvr, [VP])
                    P.memset(VP[:, :, 128:129].bitcast(F32), 1.0, [VP])
                    for ck in range(4):
                        tk = slice(ck * 128, (ck + 1) * 128)
                        if inph2:
                            _, pG, pGr = psum()
                            P.mm(pG[:, 0:128], kf[hrows, tk], qf[hrows, tk], True, True, [kf, qf], pGr)
                            P.stt(Gm[:], pG[:, 0:128], EKc[:, ck, h:h + 1], tri, ALU.mult, ALU.mult, pGr + [EKc, cst], [Gm])
                        _, pK, pKr = psum()
                        P.mm(pK[:, 0:64], kf[hrows, tk], ident[hrows, hrows], True, True, [kf, cst], pKr)
                        P.ts(kTk[:], pK[:, 0:64], EKc[:, ck, h:h + 1], ALU.mult, pKr + [EKc], [kTk])
                        if inph2:
                            _, pN, pNr = psum()
                            P.mm(pN[:, 0:129], Gm[:], VP[:, ck, :], True, False, [Gm, VP], pNr, r=True)
                            P.mm(pN[:, 0:129], qf[hrows, tk], CS[hrows, hp2, :].bitcast(F32), False, True, [qf, CS], pNr)
                        _, pS, pSr = psum()
                        P.mm(pS[hrows, 0:129], ident[hrows, hrows], CS[hrows, hp2, :].bitcast(F32), True, False, [cst, CS], pSr)
                        P.mm(pS[hrows, 0:129], kTk[:], VP[:, ck, :], False, True, [kTk, VP], pSr, r=True)
                        if inph2:
                            P.tt(sm[:, 0:1], pN[:, 128:129], EAc[:, ck, h:h + 1], ALU.mult, pNr + [EAc], [sm])
                            P.act(sm[:, 1:2], sm[:, 0:1], AF.Abs, [sm], [sm])
                            P.ts(sm[:, 1:2], sm[:, 1:2], 1.0, ALU.max, [sm], [sm])
                            P.recip(sm[:, 2:3], sm[:, 1:2], [sm], [sm])
                            P.tt(sm[:, 3:4], sm[:, 2:3], EAc[:, ck, h:h + 1], ALU.mult, [sm, EAc], [sm])
                            P.ts(hh[:], pN[:, 0:128], sm[:, 3:4], ALU.mult, pNr + [sm], [hh])
                            P.rsum(sm[:, 4:5], hh[:], [hh], [sm])
                            P.ts(sm[:, 5:6], sm[:, 4:5], 1.0 / 128, ALU.mult, [sm], [sm])
                            P.ts(hn[:], hh[:], sm[:, 5:6], ALU.subtract, [hh, sm], [hn])
                            P.tt(hsq[:], hn[:], hn[:], ALU.mult, [hn], [hsq])
                            P.rsum(sm[:, 6:7], hsq[:], [hsq], [sm])
                            P.act(sm[:, 7:8], sm[:, 6:7], AF.Sqrt, [sm, epsM], [sm], scale=1.0 / 128, bias=epsM[:, 0:1])
                            P.recip(sm[:, 8:9], sm[:, 7:8], [sm], [sm])
                            P.ts(hn[:], hn[:], sm[:, 8:9], ALU.mult, [hn, sm], [hn])
                            _, pT_, pTr = psum()
                            P.mm(pT_[:, 0:128], hn[:], identR, True, True, [hn, cstR], pTr, r=True)
                            P.stt(ymT[:, h, q0 + ck * 128:q0 + (ck + 1) * 128], pT_[:, 0:128], mhw[:, h:h + 1], so[:, tk],
                                  ALU.mult, ALU.mult, pTr + [mhw, so], [ymT])
                        P.ts(CS[hrows, hp2, :], pS[hrows, 0:129], EALc[hrows, ck, h:h + 1], ALU.mult, pSr + [EALc], [CS])
        if debug:
            dtile = sbt("dtile", [128, NPH2])
            for i, src in enumerate([yrT, ymT]):
                for j in range(8):
                    P.copy(dtile[:], src[:, j, :], [src], [dtile])
                    off = (i * 8 + j) * NPH2
                    P.dma("sp", lambda e, off=off: e.dma_start(out=dbg_d[:, off:off + NPH2], in_=dtile[:]), "dbg", reads=[dtile])
        P.finish()
        P.emit()

    for r_ in PSR + [t_.r for t_ in [yrT, ymT, cst, xb] + wslot]:
        r_.lw = None
        r_.rd = {}

    with ExitStack() as es:
        def sbt(name, shape, dt=F32):
            return Tile(es.enter_context(nc.sbuf_tensor("sb_" + name, list(shape), dt)))

        P = Prog(nc, "b")
        epsL = sbt("epsL", [128, 1]); P.memset(epsL[:], LN_EPS, [epsL])
        bgate = sbt("bgate", [128, 32]); cload(P, bgate, bgate_d)
        lnp = sbt("lnp", [128, 64]); cload(P, lnp, lnp_d)
        WR = sbt("WR", [128, KC, 36]); cload(P, WR, WR_d.rearrange("p (k m) -> p k m", m=36))
        rb = sbt("rb", [128, 36]); cload(P, rb, rb_d)
        sele = sbt("sele", [32, 32, 128]); cload(P, sele, sele_d.rearrange("p (e m) -> p e m", m=128))
        ZB = sbt("ZB", [128, KC, TM])
        mrg = sbt("mrg", [128, KC, TM], BF16)
        x1T = sbt("x1T", [128, KC, TM], BF16)
        pTb = sbt("pTb", [128, 2, TM], BF16)
        hT = sbt("hT", [128, 4, TM], BF16)
        TS = [sbt("ts%d" % i, [128, TM]) for i in range(6)]
        COEFT = sbt("COEFT", [32, TM])
        cbt = sbt("cbt", [128, TM])
        LG = sbt("LG", [128, 36]); R8 = sbt("R8", [128, 40]); LE2 = sbt("LE2", [128, 32]); OH1 = sbt("OH1", [128, 32])
        OH2 = sbt("OH2", [128, 32]); COEF = sbt("COEF", [128, 32]); rs_ = sbt("rs_", [128, 16])

        def inproj2(cname):
            w = wload(P, Wp_d[CIDX[cname]])
            wv = w[:].rearrange("p (k m) -> p k m", m=128)
            _, pap, pr = psum()
            for kc in range(KC):
                P.mm(pap[:, 0:TM], wv[:, kc, :], xb[:, kc, :], kc == 0, kc == KC - 1, [w, xb], pr)
            return pap, pr

        def proj(src_d, nk, rhs_t, rhs_fn):
            w = wload(P, src_d) if nk == 16 else None
            return w

        def layernorm(wcol, bcol):
            _, psu, psur = psum()
            _, psq, psqr = psum()
            for j in range(KC):
                sq = TS[j % 2]
                P.act(sq[:], ZB[:, j, :], AF.Square, [ZB], [sq])
                P.mm(psu[:, 0:TM], ones, ZB[:, j, :], j == 0, j == KC - 1, [cst, ZB], psur)
                P.mm(psq[:, 0:TM], ones, sq[:], j == 0, j == KC - 1, [cst, sq], psqr)
            mean, rstd, msq = TS[2], TS[3], TS[4]
            P.act(mean[:], psu[:, 0:TM], AF.Copy, psur, [mean], scale=1.0 / D)
            P.tt(msq[:], mean[:], mean[:], ALU.mult, [mean], [msq])
            P.stt(rstd[:], psq[:, 0:TM], 1.0 / D, msq[:], ALU.mult, ALU.subtract, psqr + [msq], [rstd])
            P.act(rstd[:], rstd[:], AF.Sqrt, [rstd, epsL], [rstd], bias=epsL[:, 0:1])
            P.recip(rstd[:], rstd[:], [rstd], [rstd])
            for j in range(KC):
                d_ = TS[j % 2]
                P.tt(d_[:], ZB[:, j, :], mean[:], ALU.subtract, [ZB, mean], [d_])
                P.tt(d_[:], d_[:], rstd[:], ALU.mult, [d_, rstd], [d_])
                P.act(ZB[:, j, :], d_[:], AF.Identity, [d_, lnp], [ZB], scale=lnp[:, wcol + j:wcol + j + 1], bias=lnp[:, bcol + j:bcol + j + 1])

        for tt_ in (range(NPH2 // TM) if "ph2" in parts else []):
            q0 = tt_ * TM
            g0 = PH2_0 + q0
            P.dma("pool", lambda e, g0=g0: e.dma_start(out=xb[:], in_=xT_d[:, :, g0:g0 + TM]), "xb", writes=[xb])
            P.dma("sp", lambda e, g0=g0: e.dma_start(out=ZB[:], in_=xT_d[:, :, g0:g0 + TM]), "zb", writes=[ZB])
            P.dma("pool", lambda e, q0=q0: e.dma_start(out=pTb[:], in_=pT_d[:, :, q0:q0 + TM]), "ptb", writes=[pTb])
            for j in (range(KC) if lvl >= 1 else []):
                pgr, pgrr = inproj2("gr%d" % j)
                sgr = TS[0]
                P.act(sgr[:], pgr[:, 0:TM], AF.Sigmoid, pgrr + [bgate], [sgr], bias=bgate[:, j:j + 1])
                pgm, pgmr = inproj2("gm%d" % j)
                sgm = TS[1]
                P.act(sgm[:], pgm[:, 0:TM], AF.Sigmoid, pgmr + [bgate], [sgm], bias=bgate[:, 16 + j:17 + j])
                w = wload(P, WBR_d[j], 1024)
                wv = w[:, 0:1024].rearrange("p (k m) -> p k m", m=128)
                _, ppr, pprr = psum()
                for kc in range(8):
                    P.mm(ppr[:, 0:TM], wv[:, kc, :], yrT[:, kc, q0:q0 + TM], kc == 0, kc == 7, [w, yrT], pprr)
                P.tt(sgr[:], sgr[:], ppr[:, 0:TM], ALU.mult, pprr + [sgr], [sgr])
                w = wload(P, WBM_d[j], 1024)
                wv = w[:, 0:1024].rearrange("p (k m) -> p k m", m=128)
                _, ppm, ppmr = psum()
                for kc in range(8):
                    P.mm(ppm[:, 0:TM], wv[:, kc, :], ymT[:, kc, q0:q0 + TM], kc == 0, kc == 7, [w, ymT], ppmr)
                P.tt(sgm[:], sgm[:], ppm[:, 0:TM], ALU.mult, ppmr + [sgm], [sgm])
                P.tt(mrg[:, j, :], sgr[:], sgm[:], ALU.add, [sgr, sgm], [mrg])
            for j in (range(KC) if lvl >= 2 else []):
                w = wload(P, WOUT_d[j])
                wv = w[:].rearrange("p (k m) -> p k m", m=128)
                _, pm, pmr = psum()
                for kc in range(KC):
                    P.mm(pm[:, 0:TM], wv[:, kc, :], mrg[:, kc, :], kc == 0, kc == KC - 1, [w, mrg], pmr)
                P.stt(ZB[:, j, :], ZB[:, j, :], ALPHA, pm[:, 0:TM], ALU.mult, ALU.add, pmr + [ZB], [ZB])
            if lvl >= 3:
                layernorm(0, 16)
            for j in range(KC):
                P.copy(x1T[:, j, :], ZB[:, j, :], [ZB], [x1T])
            for ts_ in (range(TM // 128) if lvl >= 4 else []):
                tk = slice(ts_ * 128, (ts_ + 1) * 128)
                _, pl, plr = psum()
                for j in range(KC):
                    P.mm(pl[:, 0:36], ZB[:, j, tk], WR[:, j, :], j == 0, j == KC - 1, [ZB, WR], plr)
                P.tt(LG[:], pl[:, 0:36], rb[:], ALU.add, plr + [rb], [LG])
                P.rmax(R8[:, 0:1], LG[:, 0:4], [LG], [R8])
                P.ts(R8[:, 4:8], LG[:, 0:4], R8[:, 0:1], ALU.is_equal, [LG, R8], [R8])
                P.ts(R8[:, 1:2], R8[:, 0:1], -1.0, ALU.mult, [R8], [R8])
                P.act(R8[:, 8:12], LG[:, 0:4], AF.Exp, [LG, R8], [R8], bias=R8[:, 1:2])
                P.rsum(R8[:, 2:3], R8[:, 8:12], [R8], [R8])
                P.recip(R8[:, 3:4], R8[:, 2:3], [R8], [R8])
                P.ts(R8[:, 12:16], R8[:, 4:8], -1.0, ALU.add, [R8], [R8], s2=1e30, op1=ALU.mult)
                for g in range(4):
                    P.ts(LE2[:, g * 8:(g + 1) * 8], LG[:, 4 + g * 8:12 + g * 8], R8[:, 12 + g:13 + g], ALU.add, [LG, R8], [LE2])
                P.rmax(R8[:, 16:17], LE2[:], [LE2], [R8])
                P.ts(OH1[:], LE2[:], R8[:, 16:17], ALU.is_equal, [LE2, R8], [OH1])
                P.stt(LE2[:], OH1[:], -1e30, LE2[:], ALU.mult, ALU.add, [OH1, LE2], [LE2])
                P.rmax(R8[:, 17:18], LE2[:], [LE2], [R8])
                P.ts(OH2[:], LE2[:], R8[:, 17:18], ALU.is_equal, [LE2, R8], [OH2])
                P.tt(R8[:, 18:19], R8[:, 17:18], R8[:, 16:17], ALU.subtract, [R8], [R8])
                P.act(R8[:, 19:20], R8[:, 18:19], AF.Exp, [R8], [R8])
                P.ts(R8[:, 20:21], R8[:, 19:20], 1.0, ALU.add, [R8], [R8])
                P.recip(R8[:, 21:22], R8[:, 20:21], [R8], [R8])
                P.tt(R8[:, 22:23], R8[:, 21:22], R8[:, 3:4], ALU.mult, [R8], [R8])
                P.tt(R8[:, 23:24], R8[:, 22:23], R8[:, 19:20], ALU.mult, [R8], [R8])
                P.ts(COEF[:], OH1[:], R8[:, 22:23], ALU.mult, [OH1, R8], [COEF])
                P.stt(COEF[:], OH2[:], R8[:, 23:24], COEF[:], ALU.mult, ALU.add, [OH2, R8, COEF], [COEF])
                _, pct, pctr = psum()
                P.mm(pct[0:32, 0:128], COEF[:], ident, True, True, [COEF, cst], pctr)
                P.copy(COEFT[:, tk], pct[0:32, 0:128], pctr, [COEFT])
            for j in (range(KC) if lvl >= 5 else []):
                w = wload(P, WPG_d[j])
                wv = w[:].rearrange("p (k m) -> p k m", m=128)
                _, pp, ppr_ = psum()
                for kc in range(KC):
                    P.mm(pp[:, 0:TM], wv[:, kc, :], x1T[:, kc, :], kc == 0, kc == KC - 1, [w, x1T], ppr_)
                sg = TS[0]
                P.act(sg[:], pp[:, 0:TM], AF.Sigmoid, ppr_, [sg])
                w = wload(P, WPLE_d[j], 256)
                wv = w[:, 0:256].rearrange("p (k m) -> p k m", m=128)
                _, pq, pqr = psum()
                for kc in range(2):
                    P.mm(pq[:, 0:TM], wv[:, kc, :], pTb[:, kc, :], kc == 0, kc == 1, [w, pTb], pqr)
                P.tt(sg[:], sg[:], pq[:, 0:TM], ALU.mult, pqr + [sg], [sg])
                P.stt(ZB[:, j, :], ZB[:, j, :], ALPHA, sg[:], ALU.mult, ALU.add, [ZB, sg], [ZB])
            for e_ in (range(32) if lvl >= 6 else []):
                _, pcb, pcbr = psum()
                P.mm(pcb[:, 0:TM], sele[:, e_, :], COEFT[:], True, True, [sele, COEFT], pcbr)
                P.act(cbt[:], pcb[:, 0:TM], AF.Copy, pcbr, [cbt])
                for f in range(4):
                    w = wload(P, WG_d[e_, f])
                    wv = w[:].rearrange("p (k m) -> p k m", m=128)
                    _, pg, pgr_ = psum()
                    for kc in range(KC):
                        P.mm(pg[:, 0:TM], wv[:, kc, :], x1T[:, kc, :], kc == 0, kc == KC - 1, [w, x1T], pgr_)
                    w2 = wload(P, WU_d[e_, f])
                    wv2 = w2[:].rearrange("p (k m) -> p k m", m=128)
                    _, pu, pur = psum()
                    for kc in range(KC):
                        P.mm(pu[:, 0:TM], wv2[:, kc, :], x1T[:, kc, :], kc == 0, kc == KC - 1, [w2, x1T], pur)
                    sg = TS[f % 2]
                    P.act(sg[:], pg[:, 0:TM], AF.Silu, pgr_, [sg])
                    P.tt(sg[:], sg[:], pu[:, 0:TM], ALU.mult, pur + [sg], [sg])
                    P.tt(hT[:, f, :], sg[:], cbt[:], ALU.mult, [sg, cbt], [hT])
                for dg in range(4):
                    w = wload(P, WD_d[e_, dg])
                    wv = w[:].rearrange("p (c k m) -> p c k m", c=4, k=4)
                    for dcc in range(4):
                        j = dg * 4 + dcc
                        _, pd, pdr = psum()
                        for kc in range(4):
                            P.mm(pd[:, 0:TM], wv[:, dcc, kc, :], hT[:, kc, :], kc == 0, kc == 3, [w, hT], pdr)
                        P.tt(ZB[:, j, :], ZB[:, j, :], pd[:, 0:TM], ALU.add, pdr + [ZB], [ZB])
            if lvl >= 7:
                layernorm(32, 48)
            P.dma("sp", lambda e, q0=q0: e.dma_start(out=outT_d[:, :, q0:q0 + TM], in_=ZB[:]), "out", reads=[ZB])
        P.finish()
        P.emit()
    return nc


def _pack_lhsT(W):
    K, N = W.shape
    return np.ascontiguousarray(W.reshape(K // 128, 128, N // 128, 128).transpose(2, 1, 0, 3)).reshape(N // 128, 128, K)


def _consts():
    p = np.arange(128)
    ident = np.eye(128, dtype=np.float32)
    bones = (p[:, None] // 64 == p[None, :] // 64).astype(np.float32)
    tri = (p[:, None] <= p[None, :]).astype(np.float32)
    ones = np.ones((128, 128), np.float32)
    cst = np.concatenate([ident, bones, tri, ones, np.zeros((128, 128), np.float32)], axis=1)
    j = (p % 64)[:, None]
    t = np.arange(64)[None, :]
    lt = (j < t).astype(np.float32)
    le = (j <= t).astype(np.float32)
    one = np.ones((128, 64), np.float32)
    zero = np.zeros((128, 64), np.float32)
    mskA = np.concatenate([lt, le, one], axis=1)
    mskB = -mskA
    gt = (j > t).astype(np.float32)
    mskC = np.concatenate([gt, -gt], axis=1)
    i64 = (j == t).astype(np.float32)
    ifull = np.tile(i64, (1, 8))
    sele = np.zeros((32, 32, 128), np.float32)
    for e in range(32):
        sele[e, e, :] = 1.0
    return cst, mskA, mskB, mskC, ifull, sele.reshape(32, 32 * 128)


def _prep_shared(inp):
    g = lambda k: np.asarray(inp[k], dtype=np.float32)[0]
    w_in = g("w_in")
    Wp = np.zeros((NCH, 128, 2048), np.float32)
    for i, (_, c0, n) in enumerate(CHUNKS):
        blk = np.zeros((2048, 128), np.float32)
        blk[:, :n] = w_in[:, c0:c0 + n]
        Wp[i] = blk.reshape(16, 128, 128).transpose(1, 0, 2).reshape(128, 2048)
    sh = {"Wp": Wp}
    sh["W2A"] = np.ascontiguousarray(np.concatenate([g("w_w2"), g("w_a2")], axis=0))
    wg2 = g("w_g2")
    sh["G2a"] = np.ascontiguousarray(wg2[0:128])
    sh["G2b"] = np.ascontiguousarray(wg2[128:160])
    mu = g("mu_shift")
    vecR = np.zeros((128, 8, 10), np.float32)
    rk = g("r_k").reshape(1024)
    for hp in range(8):
        s = slice(hp * 128, (hp + 1) * 128)
        vecR[:, hp, 0] = mu[0:1024][s]
        vecR[:, hp, 1] = mu[1024:2048][s]
        vecR[:, hp, 2] = mu[2048:3072][s]
        vecR[:, hp, 3] = g("w0")[s]
        vecR[:, hp, 4] = g("a0")[s]
        vecR[:, hp, 5] = g("k_k")[s]
        vecR[:, hp, 6] = g("k_a")[s]
        vecR[:, hp, 7] = rk[s]
        vecR[:, hp, 8] = g("lnx_w")[s]
        vecR[:, hp, 9] = g("lnx_b")[s]
    sh["vecR"] = vecR.reshape(128, 80)
    vecL = np.zeros((128, 3), np.float32)
    vecL[:, 0] = mu[3072:3200]
    vecL[:, 1] = mu[3200:3328]
    vecL[0:32, 2] = mu[3328:3360]
    sh["vecL"] = vecL
    cw, cb = g("conv_w"), g("conv_b")
    vecM = np.zeros((128, 8, 5), np.float32)
    for hp2 in range(4):
        for wi, base in ((0, 0), (1, 512)):
            s = slice(base + hp2 * 128, base + (hp2 + 1) * 128)
            ci = hp2 * 2 + wi
            for j in range(4):
                vecM[:, ci, j] = cw[j, s]
            vecM[:, ci, 4] = cb[s]
    sh["vecM"] = vecM.reshape(128, 40)
    sh["mhw"] = np.ascontiguousarray(g("mh_w").reshape(8, 128).T)
    sh["gb"] = np.ascontiguousarray(np.broadcast_to(np.concatenate([g("i_bias"), g("f_bias")])[None, :], (128, 16)))
    sh["bgate"] = np.ascontiguousarray(g("b_gate").reshape(32, 128).T)
    sh["WBR"] = _pack_lhsT(g("w_br"))
    sh["WBM"] = _pack_lhsT(g("w_bm"))
    sh["WOUT"] = _pack_lhsT(g("w_out"))
    sh["WPG"] = _pack_lhsT(g("w_pg"))
    sh["WPLE"] = _pack_lhsT(g("w_ple"))
    wgt = g("w_gate")
    sh["WG"] = np.ascontiguousarray(wgt.reshape(32, 16, 128, 4, 128).transpose(0, 3, 2, 1, 4)).reshape(32, 4, 128, 2048)
    wup = g("w_up")
    sh["WU"] = np.ascontiguousarray(wup.reshape(32, 16, 128, 4, 128).transpose(0, 3, 2, 1, 4)).reshape(32, 4, 128, 2048)
    wdn = g("w_down")
    sh["WD"] = np.ascontiguousarray(wdn.reshape(32, 4, 128, 4, 4, 128).transpose(0, 3, 2, 4, 1, 5)).reshape(32, 4, 128, 2048)
    wr = np.concatenate([g("w_rg"), g("w_re")], axis=1)
    sh["WR"] = np.ascontiguousarray(wr.reshape(16, 128, 36).transpose(1, 0, 2)).reshape(128, 16 * 36)
    sh["rb"] = np.ascontiguousarray(np.broadcast_to(np.concatenate([g("b_rg"), g("b_re")])[None, :], (128, 36)))
    lnp = np.zeros((128, 64), np.float32)
    for i, k in enumerate(("ln1_w", "ln1_b", "ln2_w", "ln2_b")):
        lnp[:, i * 16:(i + 1) * 16] = g(k).reshape(16, 128).T
    sh["lnp"] = lnp
    cst, mskA, mskB, mskC, ifull, sele = _consts()
    sh.update({"cst": cst, "mskA": mskA, "mskB": mskB, "mskC": mskC, "ifull": ifull, "sele": sele})
    return sh


def _prep_core(x, p, b, half, T, NPH2):
    S = x.shape[1]
    end = (half + 1) * NPH2
    start = end - T
    win = np.zeros((T, D), np.float32)
    valid = np.zeros((T,), np.float32)
    s0 = max(start, 0)
    win[s0 - start:] = x[b, s0:end]
    valid[s0 - start:] = 1.0
    xT = np.ascontiguousarray(win.T.reshape(16, 128, T).transpose(1, 0, 2))
    pp = p[0, b, end - NPH2:end]
    pT = np.ascontiguousarray(pp.T.reshape(2, 128, NPH2).transpose(1, 0, 2))
    vch = np.ascontiguousarray(np.broadcast_to(valid.reshape(T // 128, 128)[:, 0][None, :], (128, T // 128)))
    return {"xT": xT, "pT": pT, "valid": vch}


def kernel(**inputs):
    x = np.asarray(inputs["x"], dtype=np.float32)
    p = np.asarray(inputs["p"], dtype=np.float32)
    B, S, _ = x.shape
    T, NPH2 = S, S // 2
    sh = _prep_shared(inputs)
    nc = build(T, NPH2)
    in_maps = []
    for c in range(8):
        m = dict(sh)
        m.update(_prep_core(x, p, c // 2, c % 2, T, NPH2))
        in_maps.append(m)
    res = run_bass_kernel_spmd(nc, in_maps, core_ids=list(range(8)))
    out = np.zeros((B, S, D), np.float32)
    for c in range(8):
        oT = res.results[c]["outT"]
        b, half = c // 2, c % 2
        out[b, half * NPH2:(half + 1) * NPH2, :] = oT.transpose(2, 1, 0).reshape(NPH2, D)
    return out
```
